# Optimizing a Trainium2 kernel written in Bass

```python
import jax, jax.numpy as jnp
from jax import lax
import numpy as np

D_MODEL = 1024
BATCH = 8
SEQ = 2048
DEPTH = 2

D_MIX = D_MODEL
M_HEADS = 4
M_DH = 96
M_W = M_HEADS * M_DH
M_CHUNK = 64
M_CONV = 4
A_HEADS = 6
A_KV = 2
A_DH = 64
A_W = A_HEADS * A_DH
A_KVW = A_KV * A_DH
WINDOW = 128
S_HEADS = 4
S_DH = 64
S_W = S_HEADS * S_DH
S_BLOCK = 128
EPS = 1e-6

SPLITS = (M_W, M_W, M_HEADS, M_HEADS, M_W,
          A_W, A_KVW, A_KVW, A_W,
          S_W, S_W, S_W, S_W)
N_IN = int(sum(SPLITS))
OFFSETS = tuple(int(o) for o in np.cumsum(SPLITS)[:-1])

kernel_name = "hymba_style_mlstm_swa_stickbreak"


def rmsnorm(x, g):
    xf = x.astype(jnp.float32)
    y = xf * lax.rsqrt(jnp.mean(xf * xf, axis=-1, keepdims=True) + EPS)
    return (y * g.astype(jnp.float32)).astype(x.dtype)


def alibi_slopes():
    return jnp.asarray(2.0 ** (-8.0 * np.arange(1, A_HEADS + 1) / A_HEADS), dtype=jnp.float32)


def causal_conv(u, w, b):
    C = u.shape[-1]
    out = lax.conv_general_dilated(u, w.reshape(M_CONV, 1, C).astype(u.dtype), window_strides=(1,),
                                   padding=[(M_CONV - 1, 0)], dimension_numbers=('NWC', 'WIO', 'NWC'),
                                   feature_group_count=C)
    return out + b


def mlstm_branch(u, v, i_pre, f_pre, conv_w, conv_b, wq, wk, b_i, b_f, hn_g, skip):
    B, S, _ = u.shape
    f32 = jnp.float32
    cu = jax.nn.silu(causal_conv(u, conv_w, conv_b))
    ch = cu.reshape(B, S, M_HEADS, M_DH)
    q = jnp.einsum('bshd,hde->bhse', ch, wq).astype(f32) * (M_DH ** -0.5)
    k = jnp.einsum('bshd,hde->bhse', ch, wk).astype(f32)
    vh = v.reshape(B, S, M_HEADS, M_DH).transpose(0, 2, 1, 3).astype(f32)
    log_i = (i_pre + b_i).astype(f32).transpose(0, 2, 1)
    log_f = jax.nn.log_sigmoid((f_pre + b_f).astype(f32)).transpose(0, 2, 1)
    nc = S // M_CHUNK

    def chunks(t):
        return jnp.moveaxis(t.reshape(B, M_HEADS, nc, M_CHUNK, *t.shape[3:]), 2, 0)

    causal = jnp.tril(jnp.ones((M_CHUNK, M_CHUNK), dtype=bool))

    def step(carry, inp):
        C, n, m = carry
        qb, kb, vb, li, lf = inp
        b = jnp.cumsum(lf, axis=-1)
        g = b[..., -1]
        dmat = jnp.where(causal, b[..., :, None] - b[..., None, :] + li[..., None, :], -jnp.inf)
        inter = b + m[..., None]
        m_t = jnp.maximum(inter, jnp.max(dmat, axis=-1))
        w_inter = jnp.exp(inter - m_t)
        s = jnp.einsum('bhtd,bhsd->bhts', qb, kb) * jnp.exp(dmat - m_t[..., None])
        num = w_inter[..., None] * jnp.einsum('bhed,bhtd->bhte', C, qb) + jnp.einsum('bhts,bhse->bhte', s, vb)
        den = w_inter * jnp.einsum('bhd,bhtd->bht', n, qb) + jnp.sum(s, axis=-1)
        h = num / jnp.maximum(jnp.abs(den), jnp.exp(-m_t))[..., None]
        decay_s = g[..., None] - b + li
        m_next = jnp.maximum(g + m, jnp.max(decay_s, axis=-1))
        ws = jnp.exp(decay_s - m_next[..., None])
        carry_scale = jnp.exp(g + m - m_next)
        C_next = carry_scale[..., None, None] * C + jnp.einsum('bhs,bhse,bhsd->bhed', ws, vb, kb)
        n_next = carry_scale[..., None] * n + jnp.einsum('bhs,bhsd->bhd', ws, kb)
        return (C_next, n_next, m_next), h

    init = (jnp.zeros((B, M_HEADS, M_DH, M_DH), f32), jnp.zeros((B, M_HEADS, M_DH), f32),
            jnp.zeros((B, M_HEADS), f32))
    _, hs = lax.scan(step, init, (chunks(q), chunks(k), chunks(vh), chunks(log_i), chunks(log_f)))
    h = jnp.moveaxis(hs, 0, 2).reshape(B, M_HEADS, S, M_DH).transpose(0, 2, 1, 3)
    h = h * lax.rsqrt(jnp.mean(h * h, axis=-1, keepdims=True) + EPS)
    h = h.reshape(B, S, M_W) * hn_g.astype(f32)
    return (h + skip.astype(f32) * cu.astype(f32)).astype(u.dtype)


def swa_branch(q, k, v, sinks):
    B, S, _ = q.shape
    nb = S // WINDOW
    G = A_HEADS // A_KV
    qb = q.reshape(B, nb, WINDOW, A_KV, G, A_DH)
    kh = k.reshape(B, nb, WINDOW, A_KV, A_DH)
    vh = v.reshape(B, nb, WINDOW, A_KV, A_DH)
    pad = jnp.zeros_like(kh[:, :1])
    kk = jnp.concatenate([jnp.concatenate([pad, kh[:, :-1]], 1), kh], 2)
    vv = jnp.concatenate([jnp.concatenate([pad, vh[:, :-1]], 1), vh], 2)
    scores = jnp.einsum('bnqkgd,bnskd->bnkgqs', qb, kk).astype(jnp.float32) * (A_DH ** -0.5)
    qi = jnp.arange(WINDOW)[:, None]
    si = jnp.arange(2 * WINDOW)[None, :]
    rel = WINDOW + qi - si
    band = (rel >= 0) & (rel < WINDOW)
    valid = band[None] & ((jnp.arange(nb)[:, None, None] - 1) * WINDOW + si[None] >= 0)
    alibi = -alibi_slopes().reshape(A_KV, G)[:, :, None, None] * rel.astype(jnp.float32)
    scores = jnp.where(valid[None, :, None, None], scores + alibi[None, None], -jnp.inf)
    sink = jnp.broadcast_to(sinks.astype(jnp.float32).reshape(A_KV, G)[None, None, :, :, None, None],
                            scores.shape[:-1] + (1,))
    probs = jax.nn.softmax(jnp.concatenate([scores, sink], axis=-1), axis=-1)[..., :-1]
    out = jnp.einsum('bnkgqs,bnskd->bnqkgd', probs.astype(v.dtype), vv)
    return out.reshape(B, S, A_W)


def stickbreak_branch(q, k, v):
    B, S, _ = q.shape
    nb = S // S_BLOCK
    qh = q.reshape(B, nb, S_BLOCK, S_HEADS, S_DH).transpose(1, 0, 3, 2, 4)
    kh = k.reshape(B, S, S_HEADS, S_DH).transpose(0, 2, 1, 3)
    vh = v.reshape(B, S, S_HEADS, S_DH).transpose(0, 2, 1, 3)
    kpos = jnp.arange(S)

    def block(args):
        qb, n = args
        z = jnp.einsum('bhqd,bhsd->bhqs', qb, kh).astype(jnp.float32) * (S_DH ** -0.5)
        tpos = n * S_BLOCK + jnp.arange(S_BLOCK)
        strict = kpos[None, :] < tpos[:, None]
        log_keep = jnp.where(strict, jax.nn.log_sigmoid(-z), 0.0)
        suffix = lax.cumsum(log_keep, axis=log_keep.ndim - 1, reverse=True) - log_keep
        a = jnp.where(strict, jnp.exp(jax.nn.log_sigmoid(z) + suffix), 0.0)
        return jnp.einsum('bhqs,bhsd->bhqd', a.astype(vh.dtype), vh)

    out = lax.map(block, (qh, jnp.arange(nb)))
    return out.transpose(1, 0, 3, 2, 4).reshape(B, S, S_W)


def hybrid_layer(x, c_act, w_mod, b_mod, g_pre, g_post, w_in, m_conv_w, m_conv_b, m_wq, m_wk,
                 m_b_i, m_b_f, m_norm_g, m_skip, a_sinks, w_out):
    mod = c_act @ w_mod + b_mod
    shift, scale, gate = jnp.split(mod, 3, axis=-1)
    h = rmsnorm(x, g_pre) * (1.0 + scale[:, None]) + shift[:, None]
    proj = h @ w_in
    (mu, mv, mi, mf, mz, aq, ak, av, az, sq, sk, sv, sz) = jnp.split(proj, OFFSETS, axis=-1)
    y_m = mlstm_branch(mu, mv, mi, mf, m_conv_w, m_conv_b, m_wq, m_wk, m_b_i, m_b_f, m_norm_g, m_skip) * jax.nn.silu(mz)
    y_a = swa_branch(aq, ak, av, a_sinks) * jax.nn.silu(az)
    y_s = stickbreak_branch(sq, sk, sv) * jax.nn.silu(sz)
    y = jnp.concatenate([y_m, y_a, y_s], axis=-1) @ w_out
    return x + gate[:, None] * rmsnorm(y, g_post)


def setup_inputs(seed: int = 0) -> dict:
    key = jax.random.key(seed)
    ks = jax.random.split(key, 20)
    nrm = jax.random.normal
    f32 = jnp.float32
    d = D_MODEL
    return {
        "x": nrm(ks[0], (BATCH, SEQ, d), f32),
        "c": nrm(ks[1], (BATCH, d), f32),
        "w_mod": nrm(ks[2], (DEPTH, d, 3 * d), f32) * (0.5 * d ** -0.5),
        "b_mod": nrm(ks[3], (DEPTH, 3 * d), f32) * 0.02,
        "g_pre": 1.0 + 0.05 * nrm(ks[4], (DEPTH, d), f32),
        "g_post": 1.0 + 0.05 * nrm(ks[5], (DEPTH, d), f32),
        "w_in": nrm(ks[6], (DEPTH, d, N_IN), f32) * d ** -0.5,
        "m_conv_w": nrm(ks[7], (DEPTH, M_CONV, M_W), f32) * M_CONV ** -0.5,
        "m_conv_b": nrm(ks[8], (DEPTH, M_W), f32) * 0.02,
        "m_wq": nrm(ks[9], (DEPTH, M_HEADS, M_DH, M_DH), f32) * M_DH ** -0.5,
        "m_wk": nrm(ks[10], (DEPTH, M_HEADS, M_DH, M_DH), f32) * M_DH ** -0.5,
        "m_b_i": nrm(ks[11], (DEPTH, M_HEADS), f32) * 0.1,
        "m_b_f": 3.0 + 0.5 * nrm(ks[12], (DEPTH, M_HEADS), f32),
        "m_norm_g": 1.0 + 0.05 * nrm(ks[13], (DEPTH, M_W), f32),
        "m_skip": 1.0 + 0.05 * nrm(ks[14], (DEPTH, M_W), f32),
        "a_sinks": nrm(ks[15], (DEPTH, A_HEADS), f32) * 0.5,
        "w_out": nrm(ks[16], (DEPTH, D_MIX, d), f32) * D_MIX ** -0.5,
    }


def reference(x, c, w_mod, b_mod, g_pre, g_post, w_in, m_conv_w, m_conv_b, m_wq, m_wk,
              m_b_i, m_b_f, m_norm_g, m_skip, a_sinks, w_out):
    c_act = jax.nn.silu(c)
    for l in range(DEPTH):
        x = hybrid_layer(x, c_act, w_mod[l], b_mod[l], g_pre[l], g_post[l], w_in[l], m_conv_w[l], m_conv_b[l],
                         m_wq[l], m_wk[l], m_b_i[l], m_b_f[l], m_norm_g[l], m_skip[l], a_sinks[l], w_out[l])
    return x
```

```python
import math
from contextlib import ExitStack

import numpy as np
import concourse.bass as bass
import concourse.mybir as mybir
from concourse.bass_utils import run_bass_kernel_spmd

F32 = mybir.dt.float32
BF16 = mybir.dt.bfloat16
AF = mybir.ActivationFunctionType
ALU = mybir.AluOpType
AX = mybir.AxisListType

S = 2048
D = 1024
NT = 16
NL = 2
EPS = 1e-6
NIN = 3208
ENG_NAMES = ("pe", "act", "dve", "pool", "sp")


class _Op:
    __slots__ = ("eng", "sem", "val", "is_dma", "key")

    def __init__(self, eng):
        self.eng = eng
        self.sem = None
        self.val = None
        self.is_dma = False
        self.key = None


class Prog:
    EPOCH = 30000

    def __init__(self, nc, stack):
        self.nc = nc
        self.stack = stack
        self.engs = {"pe": nc.tensor, "act": nc.scalar, "dve": nc.vector, "pool": nc.gpsimd, "sp": nc.sync}
        self.cnt = {e: 0 for e in ENG_NAMES}
        self.sem = {e: None for e in ENG_NAMES}
        self.nsem = 0
        self.pending = {e: [] for e in ENG_NAMES}
        self.last_w = {}
        self.readers = {}
        self.waited = {}
        self.dma_sem = {}
        self.dma_cnt = {}
        self.n_ops = 0
        self.n_waits = 0

    def _new_sem(self, name):
        self.nsem += 1
        return self.stack.enter_context(self.nc.semaphore(f"{name}_{self.nsem}"))

    def _eng_sem(self, e):
        if self.sem[e] is None or (self.cnt[e] >= self.EPOCH and not self.pending[e]):
            self.sem[e] = self._new_sem("c" + e)
            self.cnt[e] = 0
        return self.sem[e]

    def _deps(self, reads, writes):
        deps = []
        for r in reads:
            p = self.last_w.get(r)
            if p is not None:
                deps.append((p, True))
        for w in writes:
            p = self.last_w.get(w)
            if p is not None:
                deps.append((p, False))
            deps.extend((q, False) for q in self.readers.get(w, ()))
        return deps

    def _wait(self, eng, sem, val):
        k = id(sem)
        if self.waited.get((eng, k), -1) >= val:
            return
        self.engs[eng].wait_ge(sem, val)
        self.waited[(eng, k)] = val
        self.n_waits += 1

    def _emit_waits(self, eng, deps, consumer_is_dma=False):
        need = {}
        for p, is_raw in deps:
            if p.is_dma:
                sem = self.dma_sem[p.key]
                val = 16 * self.dma_cnt[p.key]
            else:
                if p.eng == eng and not consumer_is_dma and (eng == "pe" or not is_raw):
                    continue
                assert p.val is not None, "dependency on an un-signalled op (close the PE group with sig=True)"
                sem, val = p.sem, p.val
            k = id(sem)
            if k not in need or need[k][1] < val:
                need[k] = (sem, val)
        for sem, val in need.values():
            self._wait(eng, sem, val)

    def _record(self, op, reads, writes):
        for w in writes:
            self.last_w[w] = op
            self.readers[w] = []
        for r in reads:
            if r in writes:
                continue
            self.readers.setdefault(r, []).append(op)

    def op(self, eng, fn, reads=(), writes=(), sig=True):
        psr = tuple(r for r in reads if isinstance(r, tuple) and r[0] == "ps")
        reads = tuple(r for r in reads if not (isinstance(r, tuple) and r[0] == "ps"))
        writes = tuple(writes) + psr
        o = _Op(eng)
        self._emit_waits(eng, self._deps(reads, writes))
        ins = fn(self.engs[eng])
        if sig:
            sem = self._eng_sem(eng)
            ins.then_inc(sem, 1)
            self.cnt[eng] += 1
            o.sem, o.val = sem, self.cnt[eng]
            for q in self.pending[eng]:
                q.sem, q.val = sem, self.cnt[eng]
            self.pending[eng] = []
        else:
            self.pending[eng].append(o)
        self._record(o, reads, writes)
        self.n_ops += 1
        return o

    def dma(self, eng, key, out, in_, reads=(), writes=(), **kw):
        reads = tuple(reads)
        writes = tuple(writes)
        key = (key, eng == "pool")
        o = _Op(eng)
        o.is_dma = True
        o.key = key
        if key not in self.dma_sem:
            self.dma_sem[key] = self._new_sem("d")
            self.dma_cnt[key] = 0
        self._emit_waits(eng, self._deps(reads, writes), consumer_is_dma=True)
        self.engs[eng].dma_start(out=out, in_=in_, **kw).then_inc(self.dma_sem[key], 16)
        self.dma_cnt[key] += 1
        self._record(o, reads, writes)
        self.n_ops += 1
        return o

    def barrier(self):
        for e in ENG_NAMES:
            assert not self.pending[e], f"open group on {e}"
        for e in ENG_NAMES:
            for f in ENG_NAMES:
                if f == e or self.sem[f] is None or self.cnt[f] == 0:
                    continue
                self._wait(e, self.sem[f], self.cnt[f])
            for key, sem in self.dma_sem.items():
                if self.dma_cnt[key]:
                    self._wait(e, sem, 16 * self.dma_cnt[key])
        self.last_w = {}
        self.readers = {}

    def wait_all_dma(self, eng="sp"):
        for key, sem in self.dma_sem.items():
            if self.dma_cnt[key]:
                self._wait(eng, sem, 16 * self.dma_cnt[key])


class Arena:
    def __init__(self, t, nelem):
        self.t = t
        self.n = nelem
        self.off = 0

    def reset(self):
        self.off = 0

    def alloc(self, shape, dt):
        n = int(np.prod(shape[1:]))
        nel = n * 2 if dt == F32 else n
        if self.off % 2:
            self.off += 1
        assert self.off + nel <= self.n, f"arena overflow {self.off + nel} > {self.n}"
        v = self.t[0:shape[0], self.off:self.off + nel]
        if dt == F32:
            v = v.bitcast(F32)
        self.off += nel + (nel % 2)
        self.hi = max(getattr(self, "hi", 0), self.off)
        if len(shape) == 3:
            v = v.rearrange("p (a b) -> p a b", b=shape[2])
        elif len(shape) == 4:
            v = v.rearrange("p (a b c) -> p a b c", b=shape[2], c=shape[3])
        return v


def build(debug=False, n_layers=NL, stop_after=None):
    nc = bass.Bass("TRN2", target_bir_lowering=False)
    dt_in = lambda name, shape: nc.dram_tensor(name, shape, F32, kind="ExternalInput").ap()
    x_in = dt_in("x", [S, D])
    c_cols = dt_in("c_cols", [128, 8])
    w_mod = dt_in("w_mod", [NL, D, 3 * D])
    b_mod = dt_in("b_mod", [NL, 3 * D])
    gpre_cols = dt_in("gpre_cols", [128, NL, 8])
    g_post = dt_in("g_post", [NL, D])
    w_in = dt_in("w_in", [NL, D, NIN])
    convw = dt_in("convw", [96, NL, 4, 4])
    mvec = dt_in("mvec", [96, NL, 3, 4])
    wq_d = dt_in("wq", [96, NL, 4, 96])
    wk_d = dt_in("wk", [96, NL, 4, 96])
    bi_d = dt_in("m_b_i", [NL, 4])
    bf_d = dt_in("m_b_f", [NL, 4])
    sinks_d = dt_in("a_sinks", [NL, 6])
    w_out = dt_in("w_out", [NL, D, D])
    out = nc.dram_tensor("out", [S, D], F32, kind="ExternalOutput").ap()
    x1 = nc.dram_tensor("x1_scratch", [S, D], F32).ap()
    dbg = {}
    if debug:
        dbg["hT"] = nc.dram_tensor("dbg_hT", [128, 8, S], BF16, kind="ExternalOutput").ap()
        dbg["yT"] = nc.dram_tensor("dbg_yT", [128, 9, S], BF16, kind="ExternalOutput").ap()
        dbg["mod"] = nc.dram_tensor("dbg_mod", [128, 16 + 1024], F32, kind="ExternalOutput").ap()

    with ExitStack() as st:
        P = Prog(nc, st)
        sb = lambda name, shape, dt: st.enter_context(nc.sbuf_tensor(name, shape, dt))
        PS = [st.enter_context(nc.psum_tensor(f"ps{i}", [128, 512], F32)) for i in range(8)]

        def psv(i, shape, dt=F32):
            n = int(np.prod(shape[1:]))
            if dt == F32:
                v = PS[i][0:shape[0], 0:n]
            else:
                v = PS[i][0:shape[0], 0:(n + 1) // 2].bitcast(BF16)
            if len(shape) == 3:
                v = v.rearrange("p (a b) -> p a b", b=shape[2])
            elif len(shape) == 4:
                v = v.rearrange("p (a b c) -> p a b c", b=shape[2], c=shape[3])
            return v

        hT = sb("hT", [128, 8, S], BF16)
        yT = sb("yT", [128, 9, S], BF16)
        WA = sb("WA", [128, 8, 1160], BF16)
        WB = sb("WB", [128, 9, 1024], BF16)
        ident = sb("ident", [128, 128], BF16)
        triu = sb("triu", [128, 128], F32)
        sellast = sb("sellast", [128, 96], F32)
        masksb = sb("masksb", [128, 128], F32)
        notmask = triu
        negm = sb("negm", [128, 128], BF16)
        F32R = mybir.dt.float32r
        Qr_t = [sb(f"Qr{i}", [128, 514], F32R) for i in range(4)]
        idp_r = sb("idp_r", [128, 128], F32R)
        idn_r = sb("idn_r", [128, 128], F32R)
        ones = sb("ones", [128, 512], F32)
        rel = sb("rel", [128, 256], F32)
        bias6 = sb("bias6", [128, 6, 256], F32)
        GG = sb("GG", [128, NL, D], F32)
        gcols = sb("gcols", [128, NL, 16], F32)
        gpre_sb = sb("gpre_sb", [128, NL, 8], F32)
        cact = sb("cact", [128, 8], F32)
        convw_sb = sb("convw_sb", [96, NL, 4, 4], F32)
        mvec_sb = sb("mvec_sb", [96, NL, 3, 4], F32)
        WQ = sb("WQ", [96, NL, 4, 96], BF16)
        WK = sb("WK", [96, NL, 4, 96], BF16)
        bi_b = sb("bi_b", [128, NL, 4], F32)
        bf_b = sb("bf_b", [128, NL, 4], F32)
        sinks_b = sb("sinks_b", [128, NL, 6], F32)
        small = sb("small", [128, 256], F32)
        ARENA_N = 35600
        arena_t = sb("arena", [128, ARENA_N], BF16)
        AR = Arena(arena_t, ARENA_N)

        P.dma("sp", "c0", cact[:], c_cols, writes=["cact"])
        P.dma("sp", "c0", gpre_sb[:], gpre_cols, writes=["gpre"])
        P.dma("sp", "c0", convw_sb[:], convw, writes=["convw"])
        P.dma("sp", "c0", mvec_sb[:], mvec, writes=["mvec"])
        for l in range(NL):
            P.dma("sp", "c0", bi_b[:, l, :], bi_d[l].partition_broadcast(128), writes=["bib"])
            P.dma("sp", "c0", bf_b[:, l, :], bf_d[l].partition_broadcast(128), writes=["bfb"])
            P.dma("sp", "c0", sinks_b[:, l, :], sinks_d[l].partition_broadcast(128), writes=["sinksb"])
        P.dma("pool", "c1", WQ[:], wq_d, writes=["WQ"])
        P.dma("pool", "c1", WK[:], wk_d, writes=["WK"])

        def aff(out_ap, pattern, base, cm, cmp, fill=0.0, in_ap=None):
            P.op("pool", lambda e: e.affine_select(out=out_ap, in_=out_ap if in_ap is None else in_ap, pattern=pattern,
                                                   compare_op=cmp, fill=fill, base=base, channel_multiplier=cm),
                 reads=["cst"], writes=["cst"])

        for tile_, in ((ident,), (triu,), (sellast,), (masksb,), (ones,)):
            P.op("pool", lambda e: e.memset(tile_[:], 1.0), writes=["cst"])
        aff(ident[:], [[-1, 128]], 0, 1, ALU.is_equal)
        aff(triu[:], [[1, 128]], 0, -1, ALU.is_ge)
        aff(sellast[:], [[0, 96]], -127, 1, ALU.is_equal)
        aff(masksb[:], [[-1, 128]], 0, 1, ALU.is_gt)
        P.op("dve", lambda e: e.tensor_scalar(out=negm[:], in0=triu[:], scalar1=-30000.0, scalar2=None, op0=ALU.mult), reads=["cst"],
             writes=["cst"])
        P.op("dve", lambda e: e.tensor_copy(idp_r[:], ident[:]), reads=["cst"], writes=["cst"])
        P.op("dve", lambda e: e.tensor_scalar(out=idn_r[:], in0=ident[:], scalar1=-1.0, scalar2=None, op0=ALU.mult), reads=["cst"],
             writes=["cst"])
        P.op("pool", lambda e: e.iota(rel[:], pattern=[[-1, 256]], base=128, channel_multiplier=1,
                                      allow_small_or_imprecise_dtypes=True), reads=["cst"], writes=["cst"])
        for h in range(6):
            slope = 2.0 ** (-8.0 * (h + 1) / 6.0)
            P.op("pool", lambda e: e.tensor_scalar(out=bias6[:, h, :], in0=rel[:], scalar1=-slope, scalar2=None, op0=ALU.mult),
                 reads=["cst"], writes=["cst"])
        aff(bias6[:], [[0, 6], [-1, 256]], 128, 1, ALU.is_ge, fill=-1e30)
        aff(bias6[:], [[0, 6], [1, 256]], -1, -1, ALU.is_ge, fill=-1e30)
        P.op("act", lambda e: e.activation(cact[:], cact[:], AF.Silu), reads=["cact"], writes=["cact"])

        def load_w_in(buf, name, l, c0, ncols):
            for k in range(8):
                P.dma("pool", name, buf[:, k, 0:ncols], w_in[l, k * 128:(k + 1) * 128, c0:c0 + ncols], writes=[(name, k)])

        def load_w_out(l):
            for c in range(4):
                P.dma("pool", "WB", WB[0:96, c, :], w_out[l, c * 96:(c + 1) * 96, :], writes=[("WB", c)])
            for c in range(4, 9):
                r0 = 384 + (c - 4) * 128
                P.dma("pool", "WB", WB[:, c, :], w_out[l, r0:r0 + 128, :], writes=[("WB", c)])

        def mod_parts(l, stg, acc, mod_row, gg_row, queues):
            NSTG = len(stg)

            def chunk(k):
                si = k % NSTG
                s_ = stg[si]
                if k == 0:
                    P.dma("sp", "rows", mod_row[:], b_mod[l:l + 1, :], writes=["mod_row"])
                    P.dma("sp", "rows", gg_row[:], g_post[l:l + 1, :], writes=["gg_row"])
                P.dma(queues[k % len(queues)], ("stg", si), s_[:], w_mod[l, k * 128:(k + 1) * 128, :], writes=[("stg", si)])
                if k == 0:
                    P.op("dve", lambda e: e.tensor_scalar(out=acc[:], in0=s_[:], scalar1=cact[:, k:k + 1], scalar2=None,
                                                          op0=ALU.mult), reads=[("stg", si), "cact"], writes=["acc"])
                else:
                    P.op("dve", lambda e: e.scalar_tensor_tensor(out=acc[:], in0=s_[:], scalar=cact[:, k:k + 1], op0=ALU.mult,
                                                                 in1=acc[:], op1=ALU.add), reads=[("stg", si), "cact", "acc"],
                         writes=["acc"])

            def finish():
                for j in range(6):
                    P.op("pe", lambda e: e.matmul(PS[j][0:1, :], ones[:, 0:1], acc[:, j * 512:(j + 1) * 512], start=True, stop=True),
                         reads=["acc", "cst"], writes=[("ps", j)])
                for j in range(6):
                    P.op("dve", lambda e: e.tensor_tensor(out=mod_row[:, j * 512:(j + 1) * 512], in0=PS[j][0:1, :],
                                                          in1=mod_row[:, j * 512:(j + 1) * 512], op=ALU.add),
                         reads=[("ps", j), "mod_row"], writes=["mod_row"])
                for k in range(16):
                    P.op("pe", lambda e: e.matmul(PS[6][:, k:k + 1], mod_row[:, k * 128:(k + 1) * 128], ones[0:1, 0:1],
                                                  start=True, stop=True),
                         reads=["mod_row", "cst"], writes=[("ps", 6)], sig=(k == 15))
                P.op("dve", lambda e: e.tensor_copy(gcols[:, l, 0:8], PS[6][:, 0:8]), reads=[("ps", 6)], writes=[("gcols", l)])
                P.op("dve", lambda e: e.scalar_tensor_tensor(out=gcols[:, l, 8:16], in0=PS[6][:, 8:16], scalar=1.0, op0=ALU.add,
                                                             in1=gpre_sb[:, l, :], op1=ALU.mult),
                     reads=[("ps", 6), "gpre"], writes=[("gcols", l)])
                P.op("dve", lambda e: e.tensor_tensor(out=gg_row[:], in0=mod_row[:, 2048:3072], in1=gg_row[:], op=ALU.mult),
                     reads=["mod_row", "gg_row"], writes=["gg_row"])
                for j in range(2):
                    P.op("pe", lambda e: e.matmul(PS[7][:, :], ones[0:1, 0:128], gg_row[:, j * 512:(j + 1) * 512],
                                                  start=True, stop=True), reads=["gg_row", "cst"], writes=[("ps", 7)])
                    P.op("act", lambda e: e.copy(GG[:, l, j * 512:(j + 1) * 512], PS[7][:, :]), reads=[("ps", 7)],
                         writes=[("GG", l)])

            return [lambda k=k: chunk(k) for k in range(8)], finish

        def phase_mod():
            AR.reset()
            stg = [AR.alloc([128, 3072], F32) for _ in range(3)]
            acc = AR.alloc([128, 3072], F32)
            mod_row = AR.alloc([1, 3072], F32)
            gg_row = AR.alloc([1, 1024], F32)
            chunks, fin = mod_parts(0, stg, acc, mod_row, gg_row, ("sp", "act"))
            for c_ in chunks:
                c_()
            fin()
            if debug:
                P.dma("sp", "dbg", dbg["mod"][:, 0:16], gcols[:, 0, :], reads=[("gcols", 0)])
                P.dma("sp", "dbg", dbg["mod"][:, 16:16 + 1024], GG[:, 0, :], reads=[("GG", 0)])

        def phase_h(l, xsrc, mod_layer=None):
            AR.reset()
            xt = [AR.alloc([128, D], F32) for _ in range(4)]
            xs = [AR.alloc([128, D], BF16) for _ in range(4)]
            junk = AR.alloc([128, D], BF16)
            mchunks, mfin = [], None
            if mod_layer is not None:
                AY = Arena(yT[:].rearrange("p a b -> p (a b)"), 9 * S)
                stg = [AY.alloc([128, 3072], F32) for _ in range(3)]
                acc = AR.alloc([128, 3072], F32)
                mod_row = AR.alloc([1, 3072], F32)
                gg_row = AR.alloc([1, 1024], F32)
                mchunks, mfin = mod_parts(mod_layer, stg, acc, mod_row, gg_row, ("pool",))
            ss = small[:, 0:16]
            sq = small[:, 16:32]
            rstd = small[:, 32:48]
            ev = 0
            for g in range(4):
                for c_ in mchunks[2 * g:2 * g + 2]:
                    c_()
                for j in range(4):
                    t = 4 * g + j
                    P.dma("sp", ("xt", j), xt[j][:], xsrc[t * 128:(t + 1) * 128, :], writes=[("xt", j)])
                    P.op("act", lambda e: e.activation(junk[:], xt[j][:], AF.Square, accum_out=ss[:, t:t + 1]),
                         reads=[("xt", j)], writes=["junk", ("ss", t)])
                    P.op("act", lambda e: e.activation(sq[:, t:t + 1], ss[:, t:t + 1], AF.Sqrt, bias=EPS, scale=1.0 / D),
                         reads=[("ss", t)], writes=[("sq", t)])
                    P.op("dve", lambda e: e.reciprocal(rstd[:, t:t + 1], sq[:, t:t + 1]), reads=[("sq", t)], writes=[("rstd", t)])
                    P.op("dve", lambda e: e.tensor_scalar(out=xs[j][:], in0=xt[j][:], scalar1=rstd[:, t:t + 1], scalar2=None,
                                                          op0=ALU.mult), reads=[("xt", j), ("rstd", t)], writes=[("xs", j)])
                for k in range(8):
                    b = k % 2
                    pt = psv(b, [128, 4, 128], BF16)
                    for j in range(4):
                        P.op("pe", lambda e: e.transpose(pt[:, j, :], xs[j][:, k * 128:(k + 1) * 128], ident[:]),
                             reads=[("xs", j), "cst"], writes=[("ps", b)], sig=(j == 3))
                    o_ap = hT[:, k, g * 512:(g + 1) * 512]
                    i_ap = psv(b, [128, 512], BF16)
                    if ev % 2 == 0:
                        P.op("dve", lambda e: e.tensor_scalar(out=o_ap, in0=i_ap, scalar1=gcols[:, l, 8 + k:9 + k],
                                                              scalar2=gcols[:, l, k:k + 1], op0=ALU.mult, op1=ALU.add),
                             reads=[("ps", b), ("gcols", l)], writes=[("hT", k, g)])
                    else:
                        P.op("act", lambda e: e.activation(o_ap, i_ap, AF.Identity, bias=gcols[:, l, k:k + 1],
                                                           scale=gcols[:, l, 8 + k:9 + k]),
                             reads=[("ps", b), ("gcols", l)], writes=[("hT", k, g)])
                    ev += 1
            if mfin is not None:
                mfin()

        HT_ALL = [("hT", k, g) for k in range(8) for g in range(4)]
        H_MOD_FIN = []

        def phase_m(l):
            AR.reset()
            VA = AR.alloc([128, NT, 4, 97], BF16)
            Us = [AR.alloc([96, S + 4], BF16) for _ in range(2)]
            DG = AR.alloc([96, 4, 4, 96], BF16)
            LI = AR.alloc([128, NT, 4], F32)
            XF = AR.alloc([128, NT, 4], F32)
            SP_ = AR.alloc([128, 64], F32)
            BP = AR.alloc([128, 64], F32)
            A_ = AR.alloc([128, 64], F32)
            C_ = AR.alloc([128, 64], F32)
            EG = AR.alloc([96, 64], F32)
            jk = AR.alloc([128, 96], BF16)
            HB = []
            for _ in range(2):
                HB.append(dict(
                    CU=AR.alloc([96, S], BF16),
                    SZ=AR.alloc([96, S], BF16), QT=AR.alloc([96, S], BF16), KT=AR.alloc([96, S], BF16),
                    KP=AR.alloc([128, NT, 96], BF16),
                    ST=[AR.alloc([128, 128], BF16) for _ in range(2)],
                    N2=[AR.alloc([128, 97], F32) for _ in range(2)],
                    HN=[AR.alloc([128, 96], BF16) for _ in range(2)],
                    TMP=[AR.alloc([96, 128], F32) for _ in range(2)],
                    Df=AR.alloc([96, 97], F32), Cb=AR.alloc([96, 97], BF16)))
            sm = small
            P.op("pool", lambda e: e.memset(VA[:, :, :, 96:97], 1.0), writes=["VA1"])
            for U in Us:
                P.op("pool", lambda e: e.memset(U[:, 0:4], 0.0), writes=["Upad"])
            for h in range(4):
                for j in range(4):
                    P.op("pool", lambda e: e.tensor_scalar(out=DG[:, h, j, :], in0=ident[0:96, 0:96], scalar1=convw_sb[:, l, h, j:j + 1],
                                                           scalar2=None, op0=ALU.mult), reads=["cst", "convw"], writes=["DG"])
            pg = psv(0, [128, NT, 8])
            for t in range(NT):
                for k in range(8):
                    P.op("pe", lambda e: e.matmul(pg[:, t, :], hT[:, k, t * 128:(t + 1) * 128], WA[:, k, 768:776],
                                                  start=(k == 0), stop=(k == 7)),
                         reads=[("WA", k), ("hT", k, t // 4)], writes=[("ps", 0)], sig=(t == NT - 1 and k == 7))
            P.op("dve", lambda e: e.tensor_tensor(out=LI[:], in0=pg[:, :, 0:4],
                                                  in1=bi_b[:, l, :].unsqueeze(1).to_broadcast([128, NT, 4]), op=ALU.add),
                 reads=[("ps", 0), "bib"], writes=["LI"])
            P.op("dve", lambda e: e.tensor_tensor(out=XF[:], in0=pg[:, :, 4:8],
                                                  in1=bf_b[:, l, :].unsqueeze(1).to_broadcast([128, NT, 4]), op=ALU.add),
                 reads=[("ps", 0), "bfb"], writes=["XF"])
            XF2 = XF.rearrange("p a b -> p (a b)")
            LI2 = LI.rearrange("p a b -> p (a b)")
            P.op("act", lambda e: e.activation(XF2, XF2, AF.Exp, scale=-1.0), reads=["XF"], writes=["XF"])
            P.op("act", lambda e: e.activation(SP_[:], XF2, AF.Ln, bias=1.0), reads=["XF"], writes=["SP"])
            P.op("pe", lambda e: e.matmul(PS[1][:, 0:64], triu[:], SP_[:], start=True, stop=True), reads=["SP", "cst"],
                 writes=[("ps", 1)])
            P.op("dve", lambda e: e.tensor_copy(BP[:], PS[1][:, 0:64]), reads=[("ps", 1)], writes=["BP"])
            P.op("act", lambda e: e.activation(A_[:], BP[:], AF.Exp, scale=-1.0), reads=["BP"], writes=["A"])
            P.op("dve", lambda e: e.tensor_tensor(out=C_[:], in0=LI2, in1=BP[:], op=ALU.add), reads=["LI", "BP"], writes=["C"])
            P.op("act", lambda e: e.activation(C_[:], C_[:], AF.Exp), reads=["C"], writes=["C"])
            P.op("pe", lambda e: e.matmul(PS[2][0:96, 0:64], sellast[:], BP[:], start=True, stop=True), reads=["BP", "cst"],
                 writes=[("ps", 2)])
            P.op("act", lambda e: e.activation(EG[:], PS[2][0:96, 0:64], AF.Exp, scale=-1.0), reads=[("ps", 2)], writes=["EG"])
            for t in range(NT):
                b = 3 + t % 2
                pv = psv(b, [128, 4, 96])
                for k in range(8):
                    P.op("pe", lambda e: e.matmul(PS[b][:, 0:384], hT[:, k, t * 128:(t + 1) * 128], WA[:, k, 384:768],
                                                  start=(k == 0), stop=(k == 7)),
                         reads=[("WA", k), ("hT", k, t // 4)], writes=[("ps", b)], sig=(k == 7))
                if t % 2 == 0:
                    P.op("dve", lambda e: e.tensor_copy(VA[:, t, :, 0:96], pv), reads=[("ps", b)], writes=[("VA", t)])
                else:
                    P.op("act", lambda e: e.copy(VA[:, t, :, 0:96], pv), reads=[("ps", b)], writes=[("VA", t)])
            bank = [5, 6]
            bi = [0]

            def nb_():
                b = bank[bi[0] % 2]
                bi[0] += 1
                return b

            def hp_u(h, s_):
                U = Us[s_]
                for tc in range(4):
                    b = nb_()
                    for k in range(8):
                        P.op("pe", lambda e: e.matmul(PS[b][0:96, :], WA[:, k, h * 96:(h + 1) * 96], hT[:, k, tc * 512:(tc + 1) * 512],
                                                      start=(k == 0), stop=(k == 7)),
                             reads=[("WA", k), ("hT", k, tc)], writes=[("ps", b)], sig=(k == 7))
                    P.op("act", lambda e: e.copy(U[:, 4 + tc * 512:4 + (tc + 1) * 512], PS[b][0:96, :]), reads=[("ps", b)],
                         writes=[("U", s_, tc)])

            def hp_z(h, s_):
                SZ = HB[s_]["SZ"]
                for tc in range(4):
                    b = nb_()
                    for k in range(8):
                        P.op("pe", lambda e: e.matmul(PS[b][0:96, :], WA[:, k, 776 + h * 96:776 + (h + 1) * 96],
                                                      hT[:, k, tc * 512:(tc + 1) * 512], start=(k == 0), stop=(k == 7)),
                             reads=[("WA", k), ("hT", k, tc)], writes=[("ps", b)], sig=(k == 7))
                    P.op("act", lambda e: e.activation(SZ[:, tc * 512:(tc + 1) * 512], PS[b][0:96, :], AF.Silu), reads=[("ps", b)],
                         writes=[("SZ", s_, tc)])

            def hp_conv(h, s_):
                U = Us[s_]
                CU = HB[s_]["CU"]
                cb_col = mvec_sb[:, l, 0, h:h + 1]
                UALL = [("U", s_, tc) for tc in range(4)] + ["Upad"]
                for tc in range(4):
                    b = nb_()
                    for j in range(4):
                        P.op("pe", lambda e: e.matmul(PS[b][0:96, :], DG[:, h, j, :], U[:, 1 + j + tc * 512:1 + j + (tc + 1) * 512],
                                                      start=(j == 0), stop=(j == 3)), reads=UALL + ["DG"], writes=[("ps", b)],
                             sig=(j == 3))
                    P.op("act", lambda e: e.activation(CU[:, tc * 512:(tc + 1) * 512], PS[b][0:96, :], AF.Silu, bias=cb_col),
                         reads=[("ps", b), "mvec"], writes=[("CU", s_)])

            def hp_qk(h, s_):
                B = HB[s_]
                CU, QT, KT, KP = B["CU"], B["QT"], B["KT"], B["KP"]
                skip_col = mvec_sb[:, l, 2, h:h + 1]
                for which, dst, wt in ((0, QT, WQ), (1, KT, WK)):
                    for tc in range(4):
                        b = nb_()
                        P.op("pe", lambda e: e.matmul(PS[b][0:96, :], wt[:, l, h, :], CU[:, tc * 512:(tc + 1) * 512], start=True,
                                                      stop=True), reads=[("CU", s_), "WQ", "WK"], writes=[("ps", b)])
                        if which == 0:
                            P.op("act", lambda e: e.activation(dst[:, tc * 512:(tc + 1) * 512], PS[b][0:96, :], AF.Copy,
                                                               scale=96.0 ** -0.5), reads=[("ps", b)], writes=[("QT", s_, tc)])
                        else:
                            P.op("act", lambda e: e.copy(dst[:, tc * 512:(tc + 1) * 512], PS[b][0:96, :]),
                                 reads=[("ps", b)], writes=[("KT", s_, tc)])
                for t4 in range(4):
                    b = nb_()
                    pk = psv(b, [128, 4, 96])
                    for j in range(4):
                        t = 4 * t4 + j
                        P.op("pe", lambda e: e.matmul(pk[:, j, :], CU[:, t * 128:(t + 1) * 128], WK[:, l, h, :], start=True,
                                                      stop=True), reads=[("CU", s_), "WK"], writes=[("ps", b)], sig=(j == 3))
                    cview = C_.rearrange("p (a b) -> p a b", b=4)[:, 4 * t4:4 * t4 + 4, h:h + 1].to_broadcast([128, 4, 96])
                    P.op("dve", lambda e: e.tensor_tensor(out=KP[:, 4 * t4:4 * t4 + 4, :], in0=pk, in1=cview, op=ALU.mult),
                         reads=[("ps", b), "C"], writes=[("KP", s_, t4)])
                P.op("act", lambda e: e.activation(CU[:], CU[:], AF.Copy, scale=skip_col), reads=[("CU", s_), "mvec"],
                     writes=[("CU", s_)])

            for pair in range(2):
                heads = (2 * pair, 2 * pair + 1)
                for stage in (hp_u, hp_z, hp_conv, hp_qk):
                    for s_, h in enumerate(heads):
                        stage(h, s_)
                PB1 = (0, 1)
                PBN = (2, 3)
                PBC = (5, 6)
                PBT = (7, 4)
                def o_(s_, c):
                    return 64 + 16 * s_ + 8 * (c % 2)

                def t_abs(c):
                    i2 = c % 2
                    for s_, h in enumerate(heads):
                        B = HB[s_]
                        o = o_(s_, c)
                        P.op("act", lambda e: e.activation(sm[:, o:o + 1], B["N2"][i2][:, 96:97], AF.Abs), reads=[("N2", s_, i2)],
                             writes=[("sm", s_, i2, 0)])

                def t_1(c):
                    i2 = c % 2
                    for s_, h in enumerate(heads):
                        o = o_(s_, c)
                        P.op("dve", lambda e: e.tensor_scalar(out=sm[:, o + 1:o + 2], in0=sm[:, o:o + 1], scalar1=1.0, scalar2=None,
                                                              op0=ALU.max), reads=[("sm", s_, i2, 0)], writes=[("sm", s_, i2, 1)])
                    for s_, h in enumerate(heads):
                        o = o_(s_, c)
                        P.op("dve", lambda e: e.reciprocal(sm[:, o + 2:o + 3], sm[:, o + 1:o + 2]), reads=[("sm", s_, i2, 1)],
                             writes=[("sm", s_, i2, 2)])

                def t_act2(c):
                    i2 = c % 2
                    for s_, h in enumerate(heads):
                        B = HB[s_]
                        o = o_(s_, c)
                        P.op("act", lambda e: e.activation(jk[:], B["N2"][i2][:, 0:96], AF.Square, scale=sm[:, o + 2:o + 3],
                                                           accum_out=sm[:, o + 3:o + 4]), reads=[("N2", s_, i2), ("sm", s_, i2, 2)],
                             writes=["jk", ("sm", s_, i2, 3)])
                    for s_, h in enumerate(heads):
                        o = o_(s_, c)
                        P.op("act", lambda e: e.activation(sm[:, o + 4:o + 5], sm[:, o + 3:o + 4], AF.Sqrt, bias=EPS, scale=1.0 / 96),
                             reads=[("sm", s_, i2, 3)], writes=[("sm", s_, i2, 4)])

                def t_2(c):
                    i2 = c % 2
                    for s_, h in enumerate(heads):
                        o = o_(s_, c)
                        P.op("dve", lambda e: e.reciprocal(sm[:, o + 5:o + 6], sm[:, o + 4:o + 5]), reads=[("sm", s_, i2, 4)],
                             writes=[("sm", s_, i2, 5)])
                    for s_, h in enumerate(heads):
                        o = o_(s_, c)
                        P.op("dve", lambda e: e.tensor_tensor(out=sm[:, o + 6:o + 7], in0=sm[:, o + 5:o + 6], in1=sm[:, o + 2:o + 3],
                                                              op=ALU.mult), reads=[("sm", s_, i2, 5), ("sm", s_, i2, 2)],
                             writes=[("sm", s_, i2, 6)])
                    for s_, h in enumerate(heads):
                        B = HB[s_]
                        o = o_(s_, c)
                        P.op("dve", lambda e: e.tensor_scalar(out=B["HN"][i2][:], in0=B["N2"][i2][:, 0:96], scalar1=sm[:, o + 6:o + 7],
                                                              scalar2=None, op0=ALU.mult), reads=[("N2", s_, i2), ("sm", s_, i2, 6)],
                             writes=[("HN", s_, i2)])

                def t_tr(c):
                    i2 = c % 2
                    for s_, h in enumerate(heads):
                        B = HB[s_]
                        ptr = psv(PBT[s_], [96, 128], BF16)
                        P.op("pe", lambda e: e.transpose(ptr, B["HN"][i2][:], ident[:]), reads=[("HN", s_, i2), "cst"],
                             writes=[("ps", PBT[s_])])

                def t_3(c):
                    i2 = c % 2
                    cs = slice(c * 128, (c + 1) * 128)
                    for s_, h in enumerate(heads):
                        B = HB[s_]
                        ptr = psv(PBT[s_], [96, 128], BF16)
                        g_col = mvec_sb[:, l, 1, h:h + 1]
                        P.op("dve", lambda e: e.scalar_tensor_tensor(out=B["TMP"][i2][:], in0=ptr, scalar=g_col, op0=ALU.mult,
                                                                     in1=B["CU"][:, cs], op1=ALU.add),
                             reads=[("ps", PBT[s_]), "mvec", ("CU", s_)], writes=[("TMP", s_, i2)])
                        P.op("pool", lambda e: e.tensor_tensor(out=yT[0:96, h, cs], in0=B["TMP"][i2][:], in1=B["SZ"][:, cs], op=ALU.mult),
                             reads=[("TMP", s_, i2), ("SZ", s_, c // 4)], writes=[("yT", h, c)])

                def mm1(c):
                    cs = slice(c * 128, (c + 1) * 128)
                    tq = c // 4
                    for s_, h in enumerate(heads):
                        B = HB[s_]
                        P.op("pe", lambda e: e.matmul(PS[PB1[s_]][:, 0:128], B["KT"][:, cs], B["QT"][:, cs], start=True, stop=True),
                             reads=[("KT", s_, tq), ("QT", s_, tq)], writes=[("ps", PB1[s_])])

                mm1(0)
                for c in range(NT):
                    cs = slice(c * 128, (c + 1) * 128)
                    tq = c // 4
                    i2 = c % 2
                    if c > 0:
                        t_abs(c - 1)
                    for s_, h in enumerate(heads):
                        B = HB[s_]
                        col = c * 4 + h
                        P.op("dve", lambda e: e.scalar_tensor_tensor(out=B["ST"][i2][:], in0=PS[PB1[s_]][:, 0:128],
                                                                     scalar=C_[:, col:col + 1], op0=ALU.mult, in1=triu[:],
                                                                     op1=ALU.mult),
                             reads=[("ps", PB1[s_]), "C", "cst"], writes=[("ST", s_, i2)])
                    if c + 1 < NT:
                        mm1(c + 1)
                    for s_, h in enumerate(heads):
                        B = HB[s_]
                        P.op("pe", lambda e: e.matmul(PS[PBN[s_]][:, 0:97], B["ST"][i2][:], VA[:, c, h, :], start=True, stop=(c == 0)),
                             reads=[("ST", s_, i2), ("VA", c), "VA1"], writes=[("ps", PBN[s_])], sig=(c == 0))
                        if c > 0:
                            P.op("pe", lambda e: e.matmul(PS[PBN[s_]][:, 0:97], B["QT"][:, cs], B["Cb"][:], start=False, stop=True),
                                 reads=[("QT", s_, tq), ("Cb", s_)], writes=[("ps", PBN[s_])])
                        if c < NT - 1:
                            P.op("pe", lambda e: e.matmul(PS[PBC[s_]][0:96, 0:97], B["KP"][:, c, :], VA[:, c, h, :], start=True,
                                                          stop=True), reads=[("KP", s_, c // 4), ("VA", c), "VA1"],
                                 writes=[("ps", PBC[s_])])
                    if c > 0:
                        t_1(c - 1)
                        t_act2(c - 1)
                    if c < NT - 1:
                        for s_, h in enumerate(heads):
                            B = HB[s_]
                            col = c * 4 + h
                            if c == 0:
                                P.op("dve", lambda e: e.tensor_copy(B["Df"][:], PS[PBC[s_]][0:96, 0:97]), reads=[("ps", PBC[s_])],
                                     writes=[("Df", s_)])
                            else:
                                pcol = (c - 1) * 4 + h
                                P.op("dve", lambda e: e.scalar_tensor_tensor(out=B["Df"][:], in0=B["Df"][:],
                                                                             scalar=EG[:, pcol:pcol + 1], op0=ALU.mult,
                                                                             in1=PS[PBC[s_]][0:96, 0:97], op1=ALU.add),
                                     reads=[("ps", PBC[s_]), ("Df", s_), "EG"], writes=[("Df", s_)])
                            P.op("act", lambda e: e.activation(B["Cb"][:], B["Df"][:], AF.Copy, scale=EG[:, col:col + 1]),
                                 reads=[("Df", s_), "EG"], writes=[("Cb", s_)])
                    if c > 0:
                        t_2(c - 1)
                        t_tr(c - 1)
                    for s_, h in enumerate(heads):
                        B = HB[s_]
                        col = c * 4 + h
                        P.op("dve", lambda e: e.tensor_scalar(out=B["N2"][i2][:], in0=PS[PBN[s_]][:, 0:97], scalar1=A_[:, col:col + 1],
                                                              scalar2=None, op0=ALU.mult), reads=[("ps", PBN[s_]), "A"],
                             writes=[("N2", s_, i2)])
                    if c > 0:
                        t_3(c - 1)
                c = NT - 1
                t_abs(c)
                t_1(c)
                t_act2(c)
                t_2(c)
                t_tr(c)
                t_3(c)

        def phase_a(l):
            AR.reset()
            QA = AR.alloc([128, 3, S], BF16)
            KA = AR.alloc([128, S], BF16)
            VSa = AR.alloc([128, NT, 128], BF16)
            ZSa = AR.alloc([128, NT, 384], BF16)
            SC6s = [AR.alloc([128, 6, 256], F32) for _ in range(2)]
            E6s = [AR.alloc([128, 6, 256], BF16) for _ in range(2)]
            ETs = AR.alloc([128, 6, 2, 128], BF16)
            YA0 = AR.alloc([128, 384], F32)
            YA = AR.alloc([128, 384], BF16)
            sm = small
            for t in range(NT):
                b = t % 2
                for k in range(8):
                    P.op("pe", lambda e: e.matmul(PS[b][:, :], hT[:, k, t * 128:(t + 1) * 128], WB[:, k, 512:1024],
                                                  start=(k == 0), stop=(k == 7)),
                         reads=[("WBi", k), ("hT", k, t // 4)], writes=[("ps", b)], sig=(k == 7))
                P.op("dve", lambda e: e.tensor_copy(VSa[:, t, :], PS[b][:, 0:128]), reads=[("ps", b)], writes=[("VSa", t)])
                P.op("act", lambda e: e.activation(ZSa[:, t, :], PS[b][:, 128:512], AF.Silu), reads=[("ps", b)], writes=[("ZSa", t)])
            bi = 0
            for gg in range(4):
                for tc in range(4):
                    b = 2 + bi % 2
                    bi += 1
                    if gg < 3:
                        for half, hq in enumerate((gg, gg + 3)):
                            for k in range(8):
                                P.op("pe", lambda e: e.matmul(PS[b][64 * half:64 * half + 64, :], WB[:, k, hq * 64:(hq + 1) * 64],
                                                              hT[:, k, tc * 512:(tc + 1) * 512], start=(k == 0), stop=(k == 7)),
                                     reads=[("WBi", k), ("hT", k, tc)], writes=[("ps", b)], sig=(half == 1 and k == 7))
                        P.op("act", lambda e: e.activation(QA[:, gg, tc * 512:(tc + 1) * 512], PS[b][:, :], AF.Copy, scale=0.125),
                             reads=[("ps", b)], writes=[("QA", tc)])
                    else:
                        for k in range(8):
                            P.op("pe", lambda e: e.matmul(PS[b][:, :], WB[:, k, 384:512], hT[:, k, tc * 512:(tc + 1) * 512],
                                                          start=(k == 0), stop=(k == 7)),
                                 reads=[("WBi", k), ("hT", k, tc)], writes=[("ps", b)], sig=(k == 7))
                        P.op("dve", lambda e: e.tensor_copy(KA[:, tc * 512:(tc + 1) * 512], PS[b][:, :]),
                             reads=[("ps", b)], writes=[("KA", tc)])
            def cols(n):
                o = 96 + 48 * (n % 2)
                return (sm[:, o:o + 6], sm[:, o + 8:o + 14], sm[:, o + 16:o + 22], sm[:, o + 24:o + 30], sm[:, o + 32:o + 38])

            def geom(n):
                k0 = max(0, n - 1) * 128
                W = (n + 1) * 128 - k0
                return k0, W, 256 - W, W // 128

            def f_pair(n, hp):
                pr = n % 2
                k0, W, boff, nb = geom(n)
                M6, NM6, ES6, RS6, RD6 = cols(n)
                SC6 = SC6s[pr]
                for h in (2 * hp, 2 * hp + 1):
                    kv = h // 3
                    b = 4 + h % 2
                    P.op("pe", lambda e: e.matmul(PS[b][:, 0:W], QA[64 * kv:64 * kv + 64, h % 3, n * 128:(n + 1) * 128],
                                                  KA[64 * kv:64 * kv + 64, k0:k0 + W], start=True, stop=True), reads=[], writes=[("ps", b)])
                    P.op("dve", lambda e: e.tensor_tensor(out=SC6[:, h, 0:W], in0=PS[b][:, 0:W], in1=bias6[:, h, boff:256], op=ALU.add),
                         reads=[("ps", b), "cst"], writes=[("SC6", pr, h)])
                    P.op("dve", lambda e: e.reduce_max(out=M6[:, h:h + 1], in_=SC6[:, h, 0:W], axis=AX.X), reads=[("SC6", pr, h)],
                         writes=[("M6", pr)])

            def f_tail(n, hp):
                pr = n % 2
                k0, W, boff, nb = geom(n)
                M6, NM6, ES6, RS6, RD6 = cols(n)
                SC6, E6 = SC6s[pr], E6s[pr]
                hs = slice(2 * hp, 2 * hp + 2)
                P.op("dve", lambda e: e.tensor_tensor(out=M6[:, hs], in0=M6[:, hs], in1=sinks_b[:, l, hs], op=ALU.max),
                     reads=[("M6", pr), "sinksb"], writes=[("M6", pr)])
                P.op("dve", lambda e: e.tensor_scalar(out=NM6[:, hs], in0=M6[:, hs], scalar1=-1.0, scalar2=None, op0=ALU.mult),
                     reads=[("M6", pr)], writes=[("NM6", pr, hp)])
                P.op("dve", lambda e: e.tensor_tensor(out=ES6[:, hs], in0=NM6[:, hs], in1=sinks_b[:, l, hs], op=ALU.add),
                     reads=[("NM6", pr, hp), "sinksb"], writes=[("ES6", pr, hp)])
                P.op("act", lambda e: e.activation(ES6[:, hs], ES6[:, hs], AF.Exp), reads=[("ES6", pr, hp)], writes=[("ES6", pr, hp)])
                for h in (2 * hp, 2 * hp + 1):
                    P.op("act", lambda e: e.activation(E6[:, h, 0:W], SC6[:, h, 0:W], AF.Exp, bias=NM6[:, h:h + 1],
                                                       accum_out=RS6[:, h:h + 1]), reads=[("SC6", pr, h), ("NM6", pr, hp)],
                         writes=[("E6", pr, h), ("RS6", pr, h)])

            def b_tr(n):
                pr = n % 2
                k0, W, boff, nb = geom(n)
                E6 = E6s[pr]
                for hg in range(2):
                    b = 6 + hg
                    pt = psv(b, [128, 3, 2, 128], BF16)
                    for hh in range(3):
                        h = hg * 3 + hh
                        for j in range(nb):
                            P.op("pe", lambda e: e.transpose(pt[:, hh, j, :], E6[:, h, j * 128:(j + 1) * 128], ident[:]),
                                 reads=[("E6", pr, h), "cst"], writes=[("ps", b)], sig=(hh == 2 and j == nb - 1))

            def b_ev(n):
                pr = n % 2
                k0, W, boff, nb = geom(n)
                M6, NM6, ES6, RS6, RD6 = cols(n)
                P.op("dve", lambda e: e.tensor_tensor(out=RD6, in0=RS6, in1=ES6, op=ALU.add),
                     reads=[("RS6", pr, h) for h in range(6)] + [("ES6", pr, hp) for hp in range(3)], writes=[("RD6", pr)])
                P.op("dve", lambda e: e.reciprocal(RD6, RD6), reads=[("RD6", pr)], writes=[("RD6", pr)])
                pt0 = psv(6, [128, 3, 2, 128], BF16)
                pt1 = psv(7, [128, 3, 2, 128], BF16)
                P.op("dve", lambda e: e.tensor_copy(ETs[:, 0:3, 0:nb, :], pt0[:, :, 0:nb, :]), reads=[("ps", 6)], writes=[("ETs", 0)])
                P.op("act", lambda e: e.copy(ETs[:, 3:6, 0:nb, :], pt1[:, :, 0:nb, :]), reads=[("ps", 7)], writes=[("ETs", 1)])

            def b_pv(n):
                k0, W, boff, nb = geom(n)
                po = psv(n % 2, [128, 6, 64])
                pb = n % 2
                for h in range(6):
                    kv = h // 3
                    for j in range(nb):
                        kb = k0 // 128 + j
                        P.op("pe", lambda e: e.matmul(po[:, h, :], ETs[:, h, j, :], VSa[:, kb, kv * 64:(kv + 1) * 64], start=(j == 0),
                                                      stop=(j == nb - 1)), reads=[("ETs", h // 3), ("VSa", kb)], writes=[("ps", pb)],
                             sig=(h == 5 and j == nb - 1))

            def b_ya(n):
                pr = n % 2
                M6, NM6, ES6, RS6, RD6 = cols(n)
                po = psv(n % 2, [128, 6, 64])
                pb = n % 2
                P.op("dve", lambda e: e.tensor_tensor(out=YA0.rearrange("p (a b) -> p a b", b=64), in0=po,
                                                      in1=RD6.unsqueeze(2).to_broadcast([128, 6, 64]), op=ALU.mult),
                     reads=[("ps", pb), ("RD6", pr)], writes=["YA0"])
                P.op("pool", lambda e: e.tensor_tensor(out=YA[:], in0=YA0[:], in1=ZSa[:, n, :], op=ALU.mult), reads=["YA0", ("ZSa", n)],
                     writes=["YA"])

            def b_out(n):
                b = 2 + n % 2
                pt = psv(b, [128, 3, 128], BF16)
                for j in range(3):
                    P.op("pe", lambda e: e.transpose(pt[:, j, :], YA[:, j * 128:(j + 1) * 128], ident[:]), reads=["YA", "cst"],
                         writes=[("ps", b)], sig=(j == 2))
                P.op("act", lambda e: e.copy(yT[:, 4:7, n * 128:(n + 1) * 128], pt), reads=[("ps", b)], writes=[("yTa", n)])

            for hp in range(3):
                f_pair(0, hp)
                f_tail(0, hp)
            for n in range(NT):
                nxt = n + 1 < NT
                if nxt:
                    f_pair(n + 1, 0)
                    f_tail(n + 1, 0)
                b_tr(n)
                b_ev(n)
                b_pv(n)
                if nxt:
                    f_pair(n + 1, 1)
                    f_tail(n + 1, 1)
                b_ya(n)
                if nxt:
                    f_pair(n + 1, 2)
                    f_tail(n + 1, 2)
                b_out(n)

        def phase_s(l):
            AR.reset()
            QS = AR.alloc([128, 2, S], BF16)
            KS = AR.alloc([128, 2, S], BF16)
            VS = AR.alloc([128, NT, 256], BF16)
            ZS = AR.alloc([128, NT, 256], BF16)
            YS = AR.alloc([128, 256], BF16)
            for t in range(NT):
                b = t % 2
                for k in range(8):
                    P.op("pe", lambda e: e.matmul(PS[b][:, :], hT[:, k, t * 128:(t + 1) * 128], WA[:, k, 512:1024],
                                                  start=(k == 0), stop=(k == 7)),
                         reads=[("WA", k), ("hT", k, t // 4)], writes=[("ps", b)], sig=(k == 7))
                P.op("dve", lambda e: e.tensor_copy(VS[:, t, :], PS[b][:, 0:256]), reads=[("ps", b)], writes=[("VS", t)])
                P.op("act", lambda e: e.activation(ZS[:, t, :], PS[b][:, 256:512], AF.Silu), reads=[("ps", b)], writes=[("ZS", t)])
            bi = 0
            for gg in range(4):
                for tc in range(4):
                    b = 2 + bi % 2
                    bi += 1
                    for k in range(8):
                        P.op("pe", lambda e: e.matmul(PS[b][:, :], WA[:, k, gg * 128:(gg + 1) * 128], hT[:, k, tc * 512:(tc + 1) * 512],
                                                      start=(k == 0), stop=(k == 7)),
                             reads=[("WA", k), ("hT", k, tc)], writes=[("ps", b)], sig=(k == 7))
                    if gg < 2:
                        P.op("act", lambda e: e.activation(QS[:, gg, tc * 512:(tc + 1) * 512], PS[b][:, :], AF.Copy, scale=0.125),
                             reads=[("ps", b)], writes=[("QS", tc)])
                    else:
                        P.op("dve", lambda e: e.tensor_copy(KS[:, gg - 2, tc * 512:(tc + 1) * 512], PS[b][:, :]),
                             reads=[("ps", b)], writes=[("KS", tc)])
            P.barrier()
            AH = Arena(hT[:].rearrange("p a b -> p (a b)"), 8 * S)
            Kh = [[AH.alloc([128, 512], F32) for _ in range(4)] for _ in range(2)]
            Qr = [q_[:] for q_ in Qr_t]
            Qh = Qr
            PEH = (3,)
            Ah = [AH.alloc([128, 512], BF16) for _ in range(4)]
            ATh = [AH.alloc([128, 4, 128], BF16) for _ in range(4)]
            cnt = {"z": 0, "t": 0}
            its = []
            for qb in range(NT):
                Wq = (qb + 1) * 128
                hi = Wq
                si = 0
                while hi > 0:
                    lo = max(0, hi - 512)
                    its.append((qb, si, lo, hi, lo == 0))
                    hi = lo
                    si += 1

            def stage1(i):
                qb, si, lo, hi, last_seg = its[i]
                pr = i % 2
                w = hi - lo
                for h in range(4):
                    zb = cnt["z"] % 3
                    cnt["z"] += 1
                    hp0 = 64 * (h % 2)
                    P.op("pe", lambda e: e.matmul(PS[zb][:, 0:w], QS[hp0:hp0 + 64, h // 2, qb * 128:(qb + 1) * 128],
                                                  KS[hp0:hp0 + 64, h // 2, lo:hi], start=True, stop=(si != 0)),
                         reads=[], writes=[("ps", zb)], sig=(si != 0))
                    if si == 0:
                        P.op("pe", lambda e: e.matmul(PS[zb][:, w - 128:w], ident[:], negm[:], start=False, stop=True),
                             reads=["cst"], writes=[("ps", zb)])
                    P.op("act", lambda e: e.activation(Kh[pr][h][:, 0:w], PS[zb][:, 0:w], AF.Sigmoid, scale=-1.0), reads=[("ps", zb)],
                         writes=[("K", pr, h)])

            def stage234(i):
                qb, si, lo, hi, last_seg = its[i]
                pr = i % 2
                w = hi - lo
                nbk = w // 128
                pob = 5 + qb % 2
                po = psv(pob, [128, 4, 64])
                for h in range(4):
                    Q = Qr[h]
                    if si == 0:
                        P.op("pool", lambda e: e.tensor_copy(Q[:, w:w + 1], ones[:, 0:1]), reads=[("Q", h), "cst"], writes=[("Qc", h)])
                    else:
                        P.op("pool", lambda e: e.tensor_copy(Q[:, w:w + 1], Qh[h][:, 0:1]), reads=[("Q", h)], writes=[("Qc", h)])
                for h in range(4):
                    Q = Qr[h]
                    P.op("dve", lambda e: e.tensor_tensor_scan(out=Q[:, 0:w][:, ::-1], data0=Kh[pr][h][:, 0:w][:, ::-1],
                                                               data1=ones[:, 0:w], initial=Qh[h][:, w:w + 1], op0=ALU.mult,
                                                               op1=ALU.mult), reads=[("K", pr, h), "cst", ("Qc", h)],
                         writes=[("Q", h)])
                for h in range(4):
                    if h not in PEH:
                        Q = Qr[h]
                        P.op("dve", lambda e: e.tensor_tensor(out=Ah[h][:, 0:w], in0=Q[:, 1:w + 1], in1=Q[:, 0:w], op=ALU.subtract),
                             reads=[("Q", h), ("Qc", h)], writes=[("A", h)])

                def tr(h):
                    tb = 3 + h % 2
                    if h in PEH:
                        pt = psv(tb, [128, 4, 128])
                        Q = Qr[h]
                        for j in range(nbk):
                            P.op("pe", lambda e: e.matmul(pt[:, j, :], Q[:, 1 + j * 128:1 + (j + 1) * 128], idp_r[:], start=True,
                                                          stop=False), reads=[("Q", h), ("Qc", h), "cst"], writes=[("ps", tb)], sig=False)
                            P.op("pe", lambda e: e.matmul(pt[:, j, :], Q[:, j * 128:(j + 1) * 128], idn_r[:], start=False, stop=True),
                                 reads=[("Q", h), "cst"], writes=[("ps", tb)], sig=(j == nbk - 1))
                    else:
                        pt = psv(tb, [128, 4, 128], BF16)
                        for j in range(nbk):
                            P.op("pe", lambda e: e.transpose(pt[:, j, :], Ah[h][:, j * 128:(j + 1) * 128], ident[:]),
                                 reads=[("A", h), "cst"], writes=[("ps", tb)], sig=(j == nbk - 1))

                def ev(h):
                    tb = 3 + h % 2
                    pt = psv(tb, [128, 4, 128]) if h in PEH else psv(tb, [128, 4, 128], BF16)
                    P.op("act", lambda e: e.copy(ATh[h][:, 0:nbk, :], pt[:, 0:nbk, :]), reads=[("ps", tb)], writes=[("AT", h)])

                def pv(h):
                    for j in range(nbk):
                        kb = lo // 128 + j
                        first_pv = (si == 0 and h == 3 and j == 0)
                        P.op("pe", lambda e: e.matmul(po[:, h, :], ATh[h][:, j, :], VS[:, kb, h * 64:(h + 1) * 64], start=first_pv,
                                                      stop=(last_seg and j == nbk - 1), skip_group_check=True),
                             reads=[("AT", h), ("VS", kb)], writes=[("ps", pob)], sig=(j == nbk - 1))

                tr(3); tr(0); ev(3); ev(0); tr(1); tr(2); pv(3); pv(0); ev(1); ev(2); pv(1); pv(2)
                return qb if last_seg else None

            def finalize(qb):
                pob = 5 + qb % 2
                P.op("dve", lambda e: e.tensor_tensor(out=YS[:], in0=psv(pob, [128, 256]), in1=ZS[:, qb, :], op=ALU.mult),
                     reads=[("ps", pob), ("ZS", qb)], writes=["YS"])
                pt = psv(7, [128, 2, 128], BF16)
                for j in range(2):
                    P.op("pe", lambda e: e.transpose(pt[:, j, :], YS[:, j * 128:(j + 1) * 128], ident[:]), reads=["YS", "cst"],
                         writes=[("ps", 7)], sig=(j == 1))
                P.op("act", lambda e: e.copy(yT[:, 7:9, qb * 128:(qb + 1) * 128], pt), reads=[("ps", 7)], writes=[("yTs", qb)])

            stage1(0)
            pend = None
            for i in range(len(its)):
                if i + 1 < len(its):
                    stage1(i + 1)
                done = stage234(i)
                if pend is not None:
                    finalize(pend)
                pend = done
            if pend is not None:
                finalize(pend)

        def phase_o(l, xsrc, xdst, fuse_next_h):
            AR.reset()
            XR = [AR.alloc([128, D], F32) for _ in range(2)]
            XO = [AR.alloc([128, D], F32) for _ in range(2)]
            jk = AR.alloc([128, D], BF16)
            xs = [AR.alloc([128, D], BF16) for _ in range(8)]
            sm = small
            evc = [0]

            def ldx(t):
                i2 = t % 2
                P.dma("sp", ("XR", i2), XR[i2][:], xsrc[t * 128:(t + 1) * 128, :], writes=[("XR", i2)])

            def mm(t):
                i2 = t % 2
                ts_ = slice(t * 128, (t + 1) * 128)
                o = 160 + 8 * i2
                for half in range(2):
                    b = 2 * i2 + half
                    for c in range(9):
                        kc = 96 if c < 4 else 128
                        P.op("pe", lambda e: e.matmul(PS[b][:, :], yT[0:kc, c, ts_], WB[0:kc, c, half * 512:(half + 1) * 512],
                                                      start=(c == 0), stop=(c == 8)), reads=[("WB", c)], writes=[("ps", b)],
                             sig=(c == 8))
                    P.op("act", lambda e: e.activation(jk[:, 0:512], PS[b][:, :], AF.Square, accum_out=sm[:, o + half:o + half + 1]),
                         reads=[("ps", b)], writes=["jko", ("smo", i2, half)])

            def chain(t):
                i2 = t % 2
                ts_ = slice(t * 128, (t + 1) * 128)
                o = 160 + 8 * i2
                P.op("dve", lambda e: e.tensor_tensor(out=sm[:, o + 2:o + 3], in0=sm[:, o:o + 1], in1=sm[:, o + 1:o + 2], op=ALU.add),
                     reads=[("smo", i2, 0), ("smo", i2, 1)], writes=[("smo", i2, 2)])
                P.op("act", lambda e: e.activation(sm[:, o + 3:o + 4], sm[:, o + 2:o + 3], AF.Sqrt, bias=EPS, scale=1.0 / D),
                     reads=[("smo", i2, 2)], writes=[("smo", i2, 3)])
                P.op("dve", lambda e: e.reciprocal(sm[:, o + 4:o + 5], sm[:, o + 3:o + 4]), reads=[("smo", i2, 3)],
                     writes=[("smo", i2, 4)])
                for half in range(2):
                    b = 2 * i2 + half
                    hs = slice(half * 512, (half + 1) * 512)
                    P.op("dve", lambda e: e.scalar_tensor_tensor(out=XO[i2][:, hs], in0=PS[b][:, :], scalar=sm[:, o + 4:o + 5],
                                                                 op0=ALU.mult, in1=GG[:, l, hs], op1=ALU.mult),
                         reads=[("ps", b), ("smo", i2, 4), ("GG", l)], writes=[("XO", i2, half)])
                    P.op("dve", lambda e: e.tensor_tensor(out=XO[i2][:, hs], in0=XO[i2][:, hs], in1=XR[i2][:, hs], op=ALU.add),
                         reads=[("XO", i2, half), ("XR", i2)], writes=[("XO", i2, half)])
                P.dma("pool", ("XO", i2), xdst[ts_, :], XO[i2][:], reads=[("XO", i2, 0), ("XO", i2, 1)], writes=[("xdst", t)])

            def hnorm(t):
                i2 = t % 2
                j = t % 8
                o2 = 208 + 4 * j
                P.op("act", lambda e: e.activation(jk[:], XO[i2][:], AF.Square, accum_out=sm[:, o2:o2 + 1]),
                     reads=[("XO", i2, 0), ("XO", i2, 1)], writes=["jko", ("hss", j)])
                P.op("act", lambda e: e.activation(sm[:, o2 + 1:o2 + 2], sm[:, o2:o2 + 1], AF.Sqrt, bias=EPS, scale=1.0 / D),
                     reads=[("hss", j)], writes=[("hsq", j)])
                P.op("dve", lambda e: e.reciprocal(sm[:, o2 + 2:o2 + 3], sm[:, o2 + 1:o2 + 2]), reads=[("hsq", j)],
                     writes=[("hrs", j)])
                P.op("act", lambda e: e.activation(xs[j][:], XO[i2][:], AF.Copy, scale=sm[:, o2 + 2:o2 + 3]),
                     reads=[("XO", i2, 0), ("XO", i2, 1), ("hrs", j)], writes=[("xs", j)])

            def htrans(g):
                ln = l + 1
                for k in range(8):
                    b = 4 + k % 2
                    pt = psv(b, [128, 4, 128], BF16)
                    for jj in range(4):
                        xj = (4 * g + jj) % 8
                        P.op("pe", lambda e: e.transpose(pt[:, jj, :], xs[xj][:, k * 128:(k + 1) * 128], ident[:]),
                             reads=[("xs", xj), "cst"], writes=[("ps", b)], sig=(jj == 3))
                    o_ap = hT[:, k, g * 512:(g + 1) * 512]
                    i_ap = psv(b, [128, 512], BF16)
                    if evc[0] % 2 == 0:
                        P.op("dve", lambda e: e.tensor_scalar(out=o_ap, in0=i_ap, scalar1=gcols[:, ln, 8 + k:9 + k],
                                                              scalar2=gcols[:, ln, k:k + 1], op0=ALU.mult, op1=ALU.add),
                             reads=[("ps", b), ("gcols", ln)], writes=[("hT", k, g)])
                    else:
                        P.op("act", lambda e: e.activation(o_ap, i_ap, AF.Identity, bias=gcols[:, ln, k:k + 1],
                                                           scale=gcols[:, ln, 8 + k:9 + k]),
                             reads=[("ps", b), ("gcols", ln)], writes=[("hT", k, g)])
                    evc[0] += 1

            ldx(0)
            for t in range(NT):
                if t + 1 < NT:
                    ldx(t + 1)
                mm(t)
                if fuse_next_h:
                    if t > 0:
                        hnorm(t - 1)
                    if t >= 5 and (t - 5) % 4 == 0:
                        htrans((t - 5) // 4)
                chain(t)
            if fuse_next_h:
                hnorm(NT - 1)
                htrans(3)

        def schedule():
            load_w_in(WA, "WA", 0, 0, 1160)
            phase_mod()
            P.barrier()
            if stop_after == "mod":
                return
            for l in range(n_layers):
                xsrc = x_in if l == 0 else x1
                xdst = out if l == n_layers - 1 else x1
                if l == 0:
                    phase_h(l, xsrc, 1 if n_layers > 1 else None)
                load_w_in(WB, "WBi", l, 1160, 1024)
                P.barrier()
                if debug and l == 0:
                    P.dma("sp", "dbg", dbg["hT"], hT[:], reads=HT_ALL)
                if stop_after == "h":
                    return
                phase_m(l)
                P.barrier()
                if stop_after == "m":
                    break
                load_w_in(WA, "WA", l, 2184, 1024)
                phase_a(l)
                P.barrier()
                if stop_after == "a":
                    break
                load_w_out(l)
                phase_s(l)
                P.barrier()
                if stop_after == "s":
                    break
                if l + 1 < n_layers:
                    load_w_in(WA, "WA", l + 1, 0, 1160)
                if debug and l == 0:
                    P.dma("sp", "dbg", dbg["yT"], yT[:], reads=[])
                phase_o(l, xsrc, xdst, l + 1 < n_layers)
                P.barrier()
            if stop_after in ("m", "a", "s") and debug:
                P.dma("sp", "dbg", dbg["yT"], yT[:], reads=[])

        schedule()
        P.wait_all_dma("sp")
        build.stats = (P.n_ops, P.n_waits, P.nsem, AR.hi, nc.sbuf_bytes_remaining)
    return nc


def make_in_maps(x, c, w_mod, b_mod, g_pre, g_post, w_in, m_conv_w, m_conv_b, m_wq, m_wk, m_b_i, m_b_f, m_norm_g, m_skip,
                 a_sinks, w_out):
    f = lambda a: np.ascontiguousarray(np.asarray(a, dtype=np.float32))
    x, c = f(x), f(c)
    shared = {
        "w_mod": f(w_mod), "b_mod": f(b_mod), "g_post": f(g_post), "w_in": f(w_in), "w_out": f(w_out),
        "gpre_cols": f(np.asarray(g_pre).reshape(NL, 8, 128).transpose(2, 0, 1)),
        "convw": f(np.asarray(m_conv_w).reshape(NL, 4, 4, 96).transpose(3, 0, 2, 1)),
        "mvec": f(np.stack([np.asarray(m_conv_b), np.asarray(m_norm_g), np.asarray(m_skip)], 1).reshape(NL, 3, 4, 96)
                  .transpose(3, 0, 1, 2)),
        "wq": f(np.asarray(m_wq).transpose(2, 0, 1, 3)),
        "wk": f(np.asarray(m_wk).transpose(2, 0, 1, 3)),
        "m_b_i": f(m_b_i), "m_b_f": f(m_b_f), "a_sinks": f(a_sinks),
    }
    maps = []
    for b in range(8):
        m = dict(shared)
        m["x"] = x[b]
        m["c_cols"] = f(c[b].reshape(8, 128).T)
        maps.append(m)
    return maps


def kernel(**inputs):
    nc = build()
    maps = make_in_maps(**inputs)
    res = run_bass_kernel_spmd(nc, maps, core_ids=list(range(8)))
    return np.stack([np.asarray(r["out"], dtype=np.float32) for r in res.results], axis=0)
```

```python
import math
from contextlib import ExitStack

import numpy as np
import concourse.bass as bass
import concourse.mybir as mybir
from concourse.bass_utils import run_bass_kernel_spmd

F32 = mybir.dt.float32
BF16 = mybir.dt.bfloat16
AF = mybir.ActivationFunctionType
ALU = mybir.AluOpType
AX = mybir.AxisListType

S = 2048
D = 1024
NT = 16
NL = 2
EPS = 1e-6
NIN = 3208
ENG_NAMES = ("pe", "act", "dve", "pool", "sp")


class _Op:
    __slots__ = ("eng", "sem", "val", "is_dma", "key")

    def __init__(self, eng):
        self.eng = eng
        self.sem = None
        self.val = None
        self.is_dma = False
        self.key = None


class Prog:
    EPOCH = 30000

    def __init__(self, nc, stack):
        self.nc = nc
        self.stack = stack
        self.engs = {"pe": nc.tensor, "act": nc.scalar, "dve": nc.vector, "pool": nc.gpsimd, "sp": nc.sync}
        self.cnt = {e: 0 for e in ENG_NAMES}
        self.sem = {e: None for e in ENG_NAMES}
        self.nsem = 0
        self.pending = {e: [] for e in ENG_NAMES}
        self.last_w = {}
        self.readers = {}
        self.waited = {}
        self.dma_sem = {}
        self.dma_cnt = {}
        self.n_ops = 0
        self.n_waits = 0

    def _new_sem(self, name):
        self.nsem += 1
        return self.stack.enter_context(self.nc.semaphore(f"{name}_{self.nsem}"))

    def _eng_sem(self, e):
        if self.sem[e] is None or (self.cnt[e] >= self.EPOCH and not self.pending[e]):
            self.sem[e] = self._new_sem("c" + e)
            self.cnt[e] = 0
        return self.sem[e]

    def _deps(self, reads, writes):
        deps = []
        for r in reads:
            p = self.last_w.get(r)
            if p is not None:
                deps.append((p, True))
        for w in writes:
            p = self.last_w.get(w)
            if p is not None:
                deps.append((p, False))
            deps.extend((q, False) for q in self.readers.get(w, ()))
        return deps

    def _wait(self, eng, sem, val):
        k = id(sem)
        if self.waited.get((eng, k), -1) >= val:
            return
        self.engs[eng].wait_ge(sem, val)
        self.waited[(eng, k)] = val
        self.n_waits += 1

    def _emit_waits(self, eng, deps, consumer_is_dma=False):
        need = {}
        for p, is_raw in deps:
            if p.is_dma:
                sem = self.dma_sem[p.key]
                val = 16 * self.dma_cnt[p.key]
            else:
                if p.eng == eng and not consumer_is_dma and (eng == "pe" or not is_raw):
                    continue
                assert p.val is not None, "dependency on an un-signalled op (close the PE group with sig=True)"
                sem, val = p.sem, p.val
            k = id(sem)
            if k not in need or need[k][1] < val:
                need[k] = (sem, val)
        for sem, val in need.values():
            self._wait(eng, sem, val)

    def _record(self, op, reads, writes):
        for w in writes:
            self.last_w[w] = op
            self.readers[w] = []
        for r in reads:
            if r in writes:
                continue
            self.readers.setdefault(r, []).append(op)

    def op(self, eng, fn, reads=(), writes=(), sig=True):
        psr = tuple(r for r in reads if isinstance(r, tuple) and r[0] == "ps")
        reads = tuple(r for r in reads if not (isinstance(r, tuple) and r[0] == "ps"))
        writes = tuple(writes) + psr
        o = _Op(eng)
        self._emit_waits(eng, self._deps(reads, writes))
        ins = fn(self.engs[eng])
        if sig:
            sem = self._eng_sem(eng)
            ins.then_inc(sem, 1)
            self.cnt[eng] += 1
            o.sem, o.val = sem, self.cnt[eng]
            for q in self.pending[eng]:
                q.sem, q.val = sem, self.cnt[eng]
            self.pending[eng] = []
        else:
            self.pending[eng].append(o)
        self._record(o, reads, writes)
        self.n_ops += 1
        return o

    def dma(self, eng, key, out, in_, reads=(), writes=(), **kw):
        reads = tuple(reads)
        writes = tuple(writes)
        key = (key, eng == "pool")
        o = _Op(eng)
        o.is_dma = True
        o.key = key
        if key not in self.dma_sem:
            self.dma_sem[key] = self._new_sem("d")
            self.dma_cnt[key] = 0
        self._emit_waits(eng, self._deps(reads, writes), consumer_is_dma=True)
        self.engs[eng].dma_start(out=out, in_=in_, **kw).then_inc(self.dma_sem[key], 16)
        self.dma_cnt[key] += 1
        self._record(o, reads, writes)
        self.n_ops += 1
        return o

    def barrier(self):
        for e in ENG_NAMES:
            assert not self.pending[e], f"open group on {e}"
        for e in ENG_NAMES:
            for f in ENG_NAMES:
                if f == e or self.sem[f] is None or self.cnt[f] == 0:
                    continue
                self._wait(e, self.sem[f], self.cnt[f])
            for key, sem in self.dma_sem.items():
                if self.dma_cnt[key]:
                    self._wait(e, sem, 16 * self.dma_cnt[key])
        self.last_w = {}
        self.readers = {}

    def wait_all_dma(self, eng="sp"):
        for key, sem in self.dma_sem.items():
            if self.dma_cnt[key]:
                self._wait(eng, sem, 16 * self.dma_cnt[key])


class Arena:
    def __init__(self, t, nelem):
        self.t = t
        self.n = nelem
        self.off = 0

    def reset(self):
        self.off = 0

    def alloc(self, shape, dt):
        n = int(np.prod(shape[1:]))
        nel = n * 2 if dt == F32 else n
        if self.off % 2:
            self.off += 1
        assert self.off + nel <= self.n, f"arena overflow {self.off + nel} > {self.n}"
        v = self.t[0:shape[0], self.off:self.off + nel]
        if dt == F32:
            v = v.bitcast(F32)
        self.off += nel + (nel % 2)
        self.hi = max(getattr(self, "hi", 0), self.off)
        if len(shape) == 3:
            v = v.rearrange("p (a b) -> p a b", b=shape[2])
        elif len(shape) == 4:
            v = v.rearrange("p (a b c) -> p a b c", b=shape[2], c=shape[3])
        return v


def build(debug=False, n_layers=NL, stop_after=None):
    nc = bass.Bass("TRN2", target_bir_lowering=False)
    dt_in = lambda name, shape: nc.dram_tensor(name, shape, F32, kind="ExternalInput").ap()
    x_in = dt_in("x", [S, D])
    c_cols = dt_in("c_cols", [128, 8])
    w_mod = dt_in("w_mod", [NL, D, 3 * D])
    b_mod = dt_in("b_mod", [NL, 3 * D])
    gpre_cols = dt_in("gpre_cols", [128, NL, 8])
    g_post = dt_in("g_post", [NL, D])
    w_in = dt_in("w_in", [NL, D, NIN])
    convw = dt_in("convw", [96, NL, 4, 4])
    mvec = dt_in("mvec", [96, NL, 3, 4])
    wq_d = dt_in("wq", [96, NL, 4, 96])
    wk_d = dt_in("wk", [96, NL, 4, 96])
    bi_d = dt_in("m_b_i", [NL, 4])
    bf_d = dt_in("m_b_f", [NL, 4])
    sinks_d = dt_in("a_sinks", [NL, 6])
    w_out = dt_in("w_out", [NL, D, D])
    out = nc.dram_tensor("out", [S, D], F32, kind="ExternalOutput").ap()
    x1 = nc.dram_tensor("x1_scratch", [S, D], F32).ap()
    dbg = {}
    if debug:
        dbg["hT"] = nc.dram_tensor("dbg_hT", [128, 8, S], BF16, kind="ExternalOutput").ap()
        dbg["yT"] = nc.dram_tensor("dbg_yT", [128, 9, S], BF16, kind="ExternalOutput").ap()
        dbg["mod"] = nc.dram_tensor("dbg_mod", [128, 16 + 1024], F32, kind="ExternalOutput").ap()

    with ExitStack() as st:
        P = Prog(nc, st)
        sb = lambda name, shape, dt: st.enter_context(nc.sbuf_tensor(name, shape, dt))
        PS = [st.enter_context(nc.psum_tensor(f"ps{i}", [128, 512], F32)) for i in range(8)]

        def psv(i, shape, dt=F32):
            n = int(np.prod(shape[1:]))
            if dt == F32:
                v = PS[i][0:shape[0], 0:n]
            else:
                v = PS[i][0:shape[0], 0:(n + 1) // 2].bitcast(BF16)
            if len(shape) == 3:
                v = v.rearrange("p (a b) -> p a b", b=shape[2])
            elif len(shape) == 4:
                v = v.rearrange("p (a b c) -> p a b c", b=shape[2], c=shape[3])
            return v

        hT = sb("hT", [128, 8, S], BF16)
        yT = sb("yT", [128, 9, S], BF16)
        WA = sb("WA", [128, 8, 1160], BF16)
        WB = sb("WB", [128, 9, 1024], BF16)
        ident = sb("ident", [128, 128], BF16)
        triu = sb("triu", [128, 128], F32)
        sellast = sb("sellast", [128, 96], F32)
        masksb = sb("masksb", [128, 128], F32)
        notmask = triu
        negm = sb("negm", [128, 128], BF16)
        F32R = mybir.dt.float32r
        Qr_t = [sb(f"Qr{i}", [128, 514], F32R) for i in range(4)]
        idp_r = sb("idp_r", [128, 128], F32R)
        idn_r = sb("idn_r", [128, 128], F32R)
        ones = sb("ones", [128, 512], F32)
        rel = sb("rel", [128, 256], F32)
        bias6 = sb("bias6", [128, 6, 256], F32)
        GG = sb("GG", [128, NL, D], F32)
        gcols = sb("gcols", [128, NL, 16], F32)
        gpre_sb = sb("gpre_sb", [128, NL, 8], F32)
        cact = sb("cact", [128, 8], F32)
        convw_sb = sb("convw_sb", [96, NL, 4, 4], F32)
        mvec_sb = sb("mvec_sb", [96, NL, 3, 4], F32)
        WQ = sb("WQ", [96, NL, 4, 96], BF16)
        WK = sb("WK", [96, NL, 4, 96], BF16)
        bi_b = sb("bi_b", [128, NL, 4], F32)
        bf_b = sb("bf_b", [128, NL, 4], F32)
        sinks_b = sb("sinks_b", [128, NL, 6], F32)
        small = sb("small", [128, 256], F32)
        ARENA_N = 35600
        arena_t = sb("arena", [128, ARENA_N], BF16)
        AR = Arena(arena_t, ARENA_N)

        P.dma("sp", "c0", cact[:], c_cols, writes=["cact"])
        P.dma("sp", "c0", gpre_sb[:], gpre_cols, writes=["gpre"])
        P.dma("sp", "c0", convw_sb[:], convw, writes=["convw"])
        P.dma("sp", "c0", mvec_sb[:], mvec, writes=["mvec"])
        for l in range(NL):
            P.dma("sp", "c0", bi_b[:, l, :], bi_d[l].partition_broadcast(128), writes=["bib"])
            P.dma("sp", "c0", bf_b[:, l, :], bf_d[l].partition_broadcast(128), writes=["bfb"])
            P.dma("sp", "c0", sinks_b[:, l, :], sinks_d[l].partition_broadcast(128), writes=["sinksb"])
        P.dma("pool", "c1", WQ[:], wq_d, writes=["WQ"])
        P.dma("pool", "c1", WK[:], wk_d, writes=["WK"])

        def aff(out_ap, pattern, base, cm, cmp, fill=0.0, in_ap=None):
            P.op("pool", lambda e: e.affine_select(out=out_ap, in_=out_ap if in_ap is None else in_ap, pattern=pattern,
                                                   compare_op=cmp, fill=fill, base=base, channel_multiplier=cm),
                 reads=["cst"], writes=["cst"])

        for tile_, in ((ident,), (triu,), (sellast,), (masksb,), (ones,)):
            P.op("pool", lambda e: e.memset(tile_[:], 1.0), writes=["cst"])
        aff(ident[:], [[-1, 128]], 0, 1, ALU.is_equal)
        aff(triu[:], [[1, 128]], 0, -1, ALU.is_ge)
        aff(sellast[:], [[0, 96]], -127, 1, ALU.is_equal)
        aff(masksb[:], [[-1, 128]], 0, 1, ALU.is_gt)
        P.op("dve", lambda e: e.tensor_scalar(out=negm[:], in0=triu[:], scalar1=-30000.0, scalar2=None, op0=ALU.mult), reads=["cst"],
             writes=["cst"])
        P.op("dve", lambda e: e.tensor_copy(idp_r[:], ident[:]), reads=["cst"], writes=["cst"])
        P.op("dve", lambda e: e.tensor_scalar(out=idn_r[:], in0=ident[:], scalar1=-1.0, scalar2=None, op0=ALU.mult), reads=["cst"],
             writes=["cst"])
        P.op("pool", lambda e: e.iota(rel[:], pattern=[[-1, 256]], base=128, channel_multiplier=1,
                                      allow_small_or_imprecise_dtypes=True), reads=["cst"], writes=["cst"])
        for h in range(6):
            slope = 2.0 ** (-8.0 * (h + 1) / 6.0)
            P.op("pool", lambda e: e.tensor_scalar(out=bias6[:, h, :], in0=rel[:], scalar1=-slope, scalar2=None, op0=ALU.mult),
                 reads=["cst"], writes=["cst"])
        aff(bias6[:], [[0, 6], [-1, 256]], 128, 1, ALU.is_ge, fill=-1e30)
        aff(bias6[:], [[0, 6], [1, 256]], -1, -1, ALU.is_ge, fill=-1e30)
        P.op("act", lambda e: e.activation(cact[:], cact[:], AF.Silu), reads=["cact"], writes=["cact"])

        def load_w_in(buf, name, l, c0, ncols):
            for k in range(8):
                P.dma("pool", name, buf[:, k, 0:ncols], w_in[l, k * 128:(k + 1) * 128, c0:c0 + ncols], writes=[(name, k)])

        def load_w_out(l):
            for c in range(4):
                P.dma("pool", "WB", WB[0:96, c, :], w_out[l, c * 96:(c + 1) * 96, :], writes=[("WB", c)])
            for c in range(4, 9):
                r0 = 384 + (c - 4) * 128
                P.dma("pool", "WB", WB[:, c, :], w_out[l, r0:r0 + 128, :], writes=[("WB", c)])

        def mod_parts(l, stg, acc, mod_row, gg_row, queues):
            NSTG = len(stg)

            def chunk(k):
                si = k % NSTG
                s_ = stg[si]
                if k == 0:
                    P.dma("sp", "rows", mod_row[:], b_mod[l:l + 1, :], writes=["mod_row"])
                    P.dma("sp", "rows", gg_row[:], g_post[l:l + 1, :], writes=["gg_row"])
                P.dma(queues[k % len(queues)], ("stg", si), s_[:], w_mod[l, k * 128:(k + 1) * 128, :], writes=[("stg", si)])
                if k == 0:
                    P.op("dve", lambda e: e.tensor_scalar(out=acc[:], in0=s_[:], scalar1=cact[:, k:k + 1], scalar2=None,
                                                          op0=ALU.mult), reads=[("stg", si), "cact"], writes=["acc"])
                else:
                    P.op("dve", lambda e: e.scalar_tensor_tensor(out=acc[:], in0=s_[:], scalar=cact[:, k:k + 1], op0=ALU.mult,
                                                                 in1=acc[:], op1=ALU.add), reads=[("stg", si), "cact", "acc"],
                         writes=["acc"])

            def finish():
                for j in range(6):
                    P.op("pe", lambda e: e.matmul(PS[j][0:1, :], ones[:, 0:1], acc[:, j * 512:(j + 1) * 512], start=True, stop=True),
                         reads=["acc", "cst"], writes=[("ps", j)])
                for j in range(6):
                    P.op("dve", lambda e: e.tensor_tensor(out=mod_row[:, j * 512:(j + 1) * 512], in0=PS[j][0:1, :],
                                                          in1=mod_row[:, j * 512:(j + 1) * 512], op=ALU.add),
                         reads=[("ps", j), "mod_row"], writes=["mod_row"])
                for k in range(16):
                    P.op("pe", lambda e: e.matmul(PS[6][:, k:k + 1], mod_row[:, k * 128:(k + 1) * 128], ones[0:1, 0:1],
                                                  start=True, stop=True),
                         reads=["mod_row", "cst"], writes=[("ps", 6)], sig=(k == 15))
                P.op("dve", lambda e: e.tensor_copy(gcols[:, l, 0:8], PS[6][:, 0:8]), reads=[("ps", 6)], writes=[("gcols", l)])
                P.op("dve", lambda e: e.scalar_tensor_tensor(out=gcols[:, l, 8:16], in0=PS[6][:, 8:16], scalar=1.0, op0=ALU.add,
                                                             in1=gpre_sb[:, l, :], op1=ALU.mult),
                     reads=[("ps", 6), "gpre"], writes=[("gcols", l)])
                P.op("dve", lambda e: e.tensor_tensor(out=gg_row[:], in0=mod_row[:, 2048:3072], in1=gg_row[:], op=ALU.mult),
                     reads=["mod_row", "gg_row"], writes=["gg_row"])
                for j in range(2):
                    P.op("pe", lambda e: e.matmul(PS[7][:, :], ones[0:1, 0:128], gg_row[:, j * 512:(j + 1) * 512],
                                                  start=True, stop=True), reads=["gg_row", "cst"], writes=[("ps", 7)])
                    P.op("act", lambda e: e.copy(GG[:, l, j * 512:(j + 1) * 512], PS[7][:, :]), reads=[("ps", 7)],
                         writes=[("GG", l)])

            return [lambda k=k: chunk(k) for k in range(8)], finish

        def phase_mod():
            AR.reset()
            stg = [AR.alloc([128, 3072], F32) for _ in range(3)]
            acc = AR.alloc([128, 3072], F32)
            mod_row = AR.alloc([1, 3072], F32)
            gg_row = AR.alloc([1, 1024], F32)
            chunks, fin = mod_parts(0, stg, acc, mod_row, gg_row, ("sp", "act"))
            for c_ in chunks:
                c_()
            fin()
            if debug:
                P.dma("sp", "dbg", dbg["mod"][:, 0:16], gcols[:, 0, :], reads=[("gcols", 0)])
                P.dma("sp", "dbg", dbg["mod"][:, 16:16 + 1024], GG[:, 0, :], reads=[("GG", 0)])

        def phase_h(l, xsrc, mod_layer=None):
            AR.reset()
            xt = [AR.alloc([128, D], F32) for _ in range(4)]
            xs = [AR.alloc([128, D], BF16) for _ in range(4)]
            junk = AR.alloc([128, D], BF16)
            mchunks, mfin = [], None
            if mod_layer is not None:
                AY = Arena(yT[:].rearrange("p a b -> p (a b)"), 9 * S)
                stg = [AY.alloc([128, 3072], F32) for _ in range(3)]
                acc = AR.alloc([128, 3072], F32)
                mod_row = AR.alloc([1, 3072], F32)
                gg_row = AR.alloc([1, 1024], F32)
                mchunks, mfin = mod_parts(mod_layer, stg, acc, mod_row, gg_row, ("pool",))
            ss = small[:, 0:16]
            sq = small[:, 16:32]
            rstd = small[:, 32:48]
            ev = 0
            for g in range(4):
                for c_ in mchunks[2 * g:2 * g + 2]:
                    c_()
                for j in range(4):
                    t = 4 * g + j
                    P.dma("sp", ("xt", j), xt[j][:], xsrc[t * 128:(t + 1) * 128, :], writes=[("xt", j)])
                    P.op("act", lambda e: e.activation(junk[:], xt[j][:], AF.Square, accum_out=ss[:, t:t + 1]),
                         reads=[("xt", j)], writes=["junk", ("ss", t)])
                    P.op("act", lambda e: e.activation(sq[:, t:t + 1], ss[:, t:t + 1], AF.Sqrt, bias=EPS, scale=1.0 / D),
                         reads=[("ss", t)], writes=[("sq", t)])
                    P.op("dve", lambda e: e.reciprocal(rstd[:, t:t + 1], sq[:, t:t + 1]), reads=[("sq", t)], writes=[("rstd", t)])
                    P.op("dve", lambda e: e.tensor_scalar(out=xs[j][:], in0=xt[j][:], scalar1=rstd[:, t:t + 1], scalar2=None,
                                                          op0=ALU.mult), reads=[("xt", j), ("rstd", t)], writes=[("xs", j)])
                for k in range(8):
                    b = k % 2
                    pt = psv(b, [128, 4, 128], BF16)
                    for j in range(4):
                        P.op("pe", lambda e: e.transpose(pt[:, j, :], xs[j][:, k * 128:(k + 1) * 128], ident[:]),
                             reads=[("xs", j), "cst"], writes=[("ps", b)], sig=(j == 3))
                    o_ap = hT[:, k, g * 512:(g + 1) * 512]
                    i_ap = psv(b, [128, 512], BF16)
                    if ev % 2 == 0:
                        P.op("dve", lambda e: e.tensor_scalar(out=o_ap, in0=i_ap, scalar1=gcols[:, l, 8 + k:9 + k],
                                                              scalar2=gcols[:, l, k:k + 1], op0=ALU.mult, op1=ALU.add),
                             reads=[("ps", b), ("gcols", l)], writes=[("hT", k, g)])
                    else:
                        P.op("act", lambda e: e.activation(o_ap, i_ap, AF.Identity, bias=gcols[:, l, k:k + 1],
                                                           scale=gcols[:, l, 8 + k:9 + k]),
                             reads=[("ps", b), ("gcols", l)], writes=[("hT", k, g)])
                    ev += 1
            if mfin is not None:
                mfin()

        HT_ALL = [("hT", k, g) for k in range(8) for g in range(4)]
        H_MOD_FIN = []

        def phase_m(l):
            AR.reset()
            VA = AR.alloc([128, NT, 4, 97], BF16)
            Us = [AR.alloc([96, S + 4], BF16) for _ in range(2)]
            DG = AR.alloc([96, 4, 4, 96], BF16)
            LI = AR.alloc([128, NT, 4], F32)
            XF = AR.alloc([128, NT, 4], F32)
            SP_ = AR.alloc([128, 64], F32)
            BP = AR.alloc([128, 64], F32)
            A_ = AR.alloc([128, 64], F32)
            C_ = AR.alloc([128, 64], F32)
            EG = AR.alloc([96, 64], F32)
            jk = AR.alloc([128, 96], BF16)
            HB = []
            for _ in range(2):
                HB.append(dict(
                    CU=AR.alloc([96, S], BF16),
                    SZ=AR.alloc([96, S], BF16), QT=AR.alloc([96, S], BF16), KT=AR.alloc([96, S], BF16),
                    KP=AR.alloc([128, NT, 96], BF16),
                    ST=[AR.alloc([128, 128], BF16) for _ in range(2)],
                    N2=[AR.alloc([128, 97], F32) for _ in range(2)],
                    HN=[AR.alloc([128, 96], BF16) for _ in range(2)],
                    TMP=[AR.alloc([96, 128], F32) for _ in range(2)],
                    Df=AR.alloc([96, 97], F32), Cb=AR.alloc([96, 97], BF16)))
            sm = small
            P.op("pool", lambda e: e.memset(VA[:, :, :, 96:97], 1.0), writes=["VA1"])
            for U in Us:
                P.op("pool", lambda e: e.memset(U[:, 0:4], 0.0), writes=["Upad"])
            for h in range(4):
                for j in range(4):
                    P.op("pool", lambda e: e.tensor_scalar(out=DG[:, h, j, :], in0=ident[0:96, 0:96], scalar1=convw_sb[:, l, h, j:j + 1],
                                                           scalar2=None, op0=ALU.mult), reads=["cst", "convw"], writes=["DG"])
            pg = psv(0, [128, NT, 8])
            for t in range(NT):
                for k in range(8):
                    P.op("pe", lambda e: e.matmul(pg[:, t, :], hT[:, k, t * 128:(t + 1) * 128], WA[:, k, 768:776],
                                                  start=(k == 0), stop=(k == 7)),
                         reads=[("WA", k), ("hT", k, t // 4)], writes=[("ps", 0)], sig=(t == NT - 1 and k == 7))
            P.op("dve", lambda e: e.tensor_tensor(out=LI[:], in0=pg[:, :, 0:4],
                                                  in1=bi_b[:, l, :].unsqueeze(1).to_broadcast([128, NT, 4]), op=ALU.add),
                 reads=[("ps", 0), "bib"], writes=["LI"])
            P.op("dve", lambda e: e.tensor_tensor(out=XF[:], in0=pg[:, :, 4:8],
                                                  in1=bf_b[:, l, :].unsqueeze(1).to_broadcast([128, NT, 4]), op=ALU.add),
                 reads=[("ps", 0), "bfb"], writes=["XF"])
            XF2 = XF.rearrange("p a b -> p (a b)")
            LI2 = LI.rearrange("p a b -> p (a b)")
            P.op("act", lambda e: e.activation(XF2, XF2, AF.Exp, scale=-1.0), reads=["XF"], writes=["XF"])
            P.op("act", lambda e: e.activation(SP_[:], XF2, AF.Ln, bias=1.0), reads=["XF"], writes=["SP"])
            P.op("pe", lambda e: e.matmul(PS[1][:, 0:64], triu[:], SP_[:], start=True, stop=True), reads=["SP", "cst"],
                 writes=[("ps", 1)])
            P.op("dve", lambda e: e.tensor_copy(BP[:], PS[1][:, 0:64]), reads=[("ps", 1)], writes=["BP"])
            P.op("act", lambda e: e.activation(A_[:], BP[:], AF.Exp, scale=-1.0), reads=["BP"], writes=["A"])
            P.op("dve", lambda e: e.tensor_tensor(out=C_[:], in0=LI2, in1=BP[:], op=ALU.add), reads=["LI", "BP"], writes=["C"])
            P.op("act", lambda e: e.activation(C_[:], C_[:], AF.Exp), reads=["C"], writes=["C"])
            P.op("pe", lambda e: e.matmul(PS[2][0:96, 0:64], sellast[:], BP[:], start=True, stop=True), reads=["BP", "cst"],
                 writes=[("ps", 2)])
            P.op("act", lambda e: e.activation(EG[:], PS[2][0:96, 0:64], AF.Exp, scale=-1.0), reads=[("ps", 2)], writes=["EG"])
            for t in range(NT):
                b = 3 + t % 2
                pv = psv(b, [128, 4, 96])
                for k in range(8):
                    P.op("pe", lambda e: e.matmul(PS[b][:, 0:384], hT[:, k, t * 128:(t + 1) * 128], WA[:, k, 384:768],
                                                  start=(k == 0), stop=(k == 7)),
                         reads=[("WA", k), ("hT", k, t // 4)], writes=[("ps", b)], sig=(k == 7))
                if t % 2 == 0:
                    P.op("dve", lambda e: e.tensor_copy(VA[:, t, :, 0:96], pv), reads=[("ps", b)], writes=[("VA", t)])
                else:
                    P.op("act", lambda e: e.copy(VA[:, t, :, 0:96], pv), reads=[("ps", b)], writes=[("VA", t)])
            bank = [5, 6]
            bi = [0]

            def nb_():
                b = bank[bi[0] % 2]
                bi[0] += 1
                return b

            def hp_u(h, s_):
                U = Us[s_]
                for tc in range(4):
                    b = nb_()
                    for k in range(8):
                        P.op("pe", lambda e: e.matmul(PS[b][0:96, :], WA[:, k, h * 96:(h + 1) * 96], hT[:, k, tc * 512:(tc + 1) * 512],
                                                      start=(k == 0), stop=(k == 7)),
                             reads=[("WA", k), ("hT", k, tc)], writes=[("ps", b)], sig=(k == 7))
                    P.op("act", lambda e: e.copy(U[:, 4 + tc * 512:4 + (tc + 1) * 512], PS[b][0:96, :]), reads=[("ps", b)],
                         writes=[("U", s_, tc)])

            def hp_z(h, s_):
                SZ = HB[s_]["SZ"]
                for tc in range(4):
                    b = nb_()
                    for k in range(8):
                        P.op("pe", lambda e: e.matmul(PS[b][0:96, :], WA[:, k, 776 + h * 96:776 + (h + 1) * 96],
                                                      hT[:, k, tc * 512:(tc + 1) * 512], start=(k == 0), stop=(k == 7)),
                             reads=[("WA", k), ("hT", k, tc)], writes=[("ps", b)], sig=(k == 7))
                    P.op("act", lambda e: e.activation(SZ[:, tc * 512:(tc + 1) * 512], PS[b][0:96, :], AF.Silu), reads=[("ps", b)],
                         writes=[("SZ", s_, tc)])

            def hp_conv(h, s_):
                U = Us[s_]
                CU = HB[s_]["CU"]
                cb_col = mvec_sb[:, l, 0, h:h + 1]
                UALL = [("U", s_, tc) for tc in range(4)] + ["Upad"]
                for tc in range(4):
                    b = nb_()
                    for j in range(4):
                        P.op("pe", lambda e: e.matmul(PS[b][0:96, :], DG[:, h, j, :], U[:, 1 + j + tc * 512:1 + j + (tc + 1) * 512],
                                                      start=(j == 0), stop=(j == 3)), reads=UALL + ["DG"], writes=[("ps", b)],
                             sig=(j == 3))
                    P.op("act", lambda e: e.activation(CU[:, tc * 512:(tc + 1) * 512], PS[b][0:96, :], AF.Silu, bias=cb_col),
                         reads=[("ps", b), "mvec"], writes=[("CU", s_)])

            def hp_qk(h, s_):
                B = HB[s_]
                CU, QT, KT, KP = B["CU"], B["QT"], B["KT"], B["KP"]
                skip_col = mvec_sb[:, l, 2, h:h + 1]
                for which, dst, wt in ((0, QT, WQ), (1, KT, WK)):
                    for tc in range(4):
                        b = nb_()
                        P.op("pe", lambda e: e.matmul(PS[b][0:96, :], wt[:, l, h, :], CU[:, tc * 512:(tc + 1) * 512], start=True,
                                                      stop=True), reads=[("CU", s_), "WQ", "WK"], writes=[("ps", b)])
                        if which == 0:
                            P.op("act", lambda e: e.activation(dst[:, tc * 512:(tc + 1) * 512], PS[b][0:96, :], AF.Copy,
                                                               scale=96.0 ** -0.5), reads=[("ps", b)], writes=[("QT", s_, tc)])
                        else:
                            P.op("act", lambda e: e.copy(dst[:, tc * 512:(tc + 1) * 512], PS[b][0:96, :]),
                                 reads=[("ps", b)], writes=[("KT", s_, tc)])
                for t4 in range(4):
                    b = nb_()
                    pk = psv(b, [128, 4, 96])
                    for j in range(4):
                        t = 4 * t4 + j
                        P.op("pe", lambda e: e.matmul(pk[:, j, :], CU[:, t * 128:(t + 1) * 128], WK[:, l, h, :], start=True,
                                                      stop=True), reads=[("CU", s_), "WK"], writes=[("ps", b)], sig=(j == 3))
                    cview = C_.rearrange("p (a b) -> p a b", b=4)[:, 4 * t4:4 * t4 + 4, h:h + 1].to_broadcast([128, 4, 96])
                    P.op("dve", lambda e: e.tensor_tensor(out=KP[:, 4 * t4:4 * t4 + 4, :], in0=pk, in1=cview, op=ALU.mult),
                         reads=[("ps", b), "C"], writes=[("KP", s_, t4)])
                P.op("act", lambda e: e.activation(CU[:], CU[:], AF.Copy, scale=skip_col), reads=[("CU", s_), "mvec"],
                     writes=[("CU", s_)])

            for pair in range(2):
                heads = (2 * pair, 2 * pair + 1)
                for stage in (hp_u, hp_z, hp_conv, hp_qk):
                    for s_, h in enumerate(heads):
                        stage(h, s_)
                PB1 = (0, 1)
                PBN = (2, 3)
                PBC = (5, 6)
                PBT = (7, 4)
                def o_(s_, c):
                    return 64 + 16 * s_ + 8 * (c % 2)

                def t_abs(c):
                    i2 = c % 2
                    for s_, h in enumerate(heads):
                        B = HB[s_]
                        o = o_(s_, c)
                        P.op("act", lambda e: e.activation(sm[:, o:o + 1], B["N2"][i2][:, 96:97], AF.Abs), reads=[("N2", s_, i2)],
                             writes=[("sm", s_, i2, 0)])

                def t_1(c):
                    i2 = c % 2
                    for s_, h in enumerate(heads):
                        o = o_(s_, c)
                        P.op("dve", lambda e: e.tensor_scalar(out=sm[:, o + 1:o + 2], in0=sm[:, o:o + 1], scalar1=1.0, scalar2=None,
                                                              op0=ALU.max), reads=[("sm", s_, i2, 0)], writes=[("sm", s_, i2, 1)])
                    for s_, h in enumerate(heads):
                        o = o_(s_, c)
                        P.op("dve", lambda e: e.reciprocal(sm[:, o + 2:o + 3], sm[:, o + 1:o + 2]), reads=[("sm", s_, i2, 1)],
                             writes=[("sm", s_, i2, 2)])

                def t_act2(c):
                    i2 = c % 2
                    for s_, h in enumerate(heads):
                        B = HB[s_]
                        o = o_(s_, c)
                        P.op("act", lambda e: e.activation(jk[:], B["N2"][i2][:, 0:96], AF.Square, scale=sm[:, o + 2:o + 3],
                                                           accum_out=sm[:, o + 3:o + 4]), reads=[("N2", s_, i2), ("sm", s_, i2, 2)],
                             writes=["jk", ("sm", s_, i2, 3)])
                    for s_, h in enumerate(heads):
                        o = o_(s_, c)
                        P.op("act", lambda e: e.activation(sm[:, o + 4:o + 5], sm[:, o + 3:o + 4], AF.Sqrt, bias=EPS, scale=1.0 / 96),
                             reads=[("sm", s_, i2, 3)], writes=[("sm", s_, i2, 4)])

                def t_2(c):
                    i2 = c % 2
                    for s_, h in enumerate(heads):
                        o = o_(s_, c)
                        P.op("dve", lambda e: e.reciprocal(sm[:, o + 5:o + 6], sm[:, o + 4:o + 5]), reads=[("sm", s_, i2, 4)],
                             writes=[("sm", s_, i2, 5)])
                    for s_, h in enumerate(heads):
                        o = o_(s_, c)
                        P.op("dve", lambda e: e.tensor_tensor(out=sm[:, o + 6:o + 7], in0=sm[:, o + 5:o + 6], in1=sm[:, o + 2:o + 3],
                                                              op=ALU.mult), reads=[("sm", s_, i2, 5), ("sm", s_, i2, 2)],
                             writes=[("sm", s_, i2, 6)])
                    for s_, h in enumerate(heads):
                        B = HB[s_]
                        o = o_(s_, c)
                        P.op("dve", lambda e: e.tensor_scalar(out=B["HN"][i2][:], in0=B["N2"][i2][:, 0:96], scalar1=sm[:, o + 6:o + 7],
                                                              scalar2=None, op0=ALU.mult), reads=[("N2", s_, i2), ("sm", s_, i2, 6)],
                             writes=[("HN", s_, i2)])

                def t_tr(c):
                    i2 = c % 2
                    for s_, h in enumerate(heads):
                        B = HB[s_]
                        ptr = psv(PBT[s_], [96, 128], BF16)
                        P.op("pe", lambda e: e.transpose(ptr, B["HN"][i2][:], ident[:]), reads=[("HN", s_, i2), "cst"],
                             writes=[("ps", PBT[s_])])

                def t_3(c):
                    i2 = c % 2
                    cs = slice(c * 128, (c + 1) * 128)
                    for s_, h in enumerate(heads):
                        B = HB[s_]
                        ptr = psv(PBT[s_], [96, 128], BF16)
                        g_col = mvec_sb[:, l, 1, h:h + 1]
                        P.op("dve", lambda e: e.scalar_tensor_tensor(out=B["TMP"][i2][:], in0=ptr, scalar=g_col, op0=ALU.mult,
                                                                     in1=B["CU"][:, cs], op1=ALU.add),
                             reads=[("ps", PBT[s_]), "mvec", ("CU", s_)], writes=[("TMP", s_, i2)])
                        P.op("pool", lambda e: e.tensor_tensor(out=yT[0:96, h, cs], in0=B["TMP"][i2][:], in1=B["SZ"][:, cs], op=ALU.mult),
                             reads=[("TMP", s_, i2), ("SZ", s_, c // 4)], writes=[("yT", h, c)])

                def mm1(c):
                    cs = slice(c * 128, (c + 1) * 128)
                    tq = c // 4
                    for s_, h in enumerate(heads):
                        B = HB[s_]
                        P.op("pe", lambda e: e.matmul(PS[PB1[s_]][:, 0:128], B["KT"][:, cs], B["QT"][:, cs], start=True, stop=True),
                             reads=[("KT", s_, tq), ("QT", s_, tq)], writes=[("ps", PB1[s_])])

                mm1(0)
                for c in range(NT):
                    cs = slice(c * 128, (c + 1) * 128)
                    tq = c // 4
                    i2 = c % 2
                    if c > 0:
                        t_abs(c - 1)
                    for s_, h in enumerate(heads):
                        B = HB[s_]
                        col = c * 4 + h
                        P.op("dve", lambda e: e.scalar_tensor_tensor(out=B["ST"][i2][:], in0=PS[PB1[s_]][:, 0:128],
                                                                     scalar=C_[:, col:col + 1], op0=ALU.mult, in1=triu[:],
                                                                     op1=ALU.mult),
                             reads=[("ps", PB1[s_]), "C", "cst"], writes=[("ST", s_, i2)])
                    if c + 1 < NT:
                        mm1(c + 1)
                    for s_, h in enumerate(heads):
                        B = HB[s_]
                        P.op("pe", lambda e: e.matmul(PS[PBN[s_]][:, 0:97], B["ST"][i2][:], VA[:, c, h, :], start=True, stop=(c == 0)),
                             reads=[("ST", s_, i2), ("VA", c), "VA1"], writes=[("ps", PBN[s_])], sig=(c == 0))
                        if c > 0:
                            P.op("pe", lambda e: e.matmul(PS[PBN[s_]][:, 0:97], B["QT"][:, cs], B["Cb"][:], start=False, stop=True),
                                 reads=[("QT", s_, tq), ("Cb", s_)], writes=[("ps", PBN[s_])])
                        if c < NT - 1:
                            P.op("pe", lambda e: e.matmul(PS[PBC[s_]][0:96, 0:97], B["KP"][:, c, :], VA[:, c, h, :], start=True,
                                                          stop=True), reads=[("KP", s_, c // 4), ("VA", c), "VA1"],
                                 writes=[("ps", PBC[s_])])
                    if c > 0:
                        t_1(c - 1)
                        t_act2(c - 1)
                    if c < NT - 1:
                        for s_, h in enumerate(heads):
                            B = HB[s_]
                            col = c * 4 + h
                            if c == 0:
                                P.op("dve", lambda e: e.tensor_copy(B["Df"][:], PS[PBC[s_]][0:96, 0:97]), reads=[("ps", PBC[s_])],
                                     writes=[("Df", s_)])
                            else:
                                pcol = (c - 1) * 4 + h
                                P.op("dve", lambda e: e.scalar_tensor_tensor(out=B["Df"][:], in0=B["Df"][:],
                                                                             scalar=EG[:, pcol:pcol + 1], op0=ALU.mult,
                                                                             in1=PS[PBC[s_]][0:96, 0:97], op1=ALU.add),
                                     reads=[("ps", PBC[s_]), ("Df", s_), "EG"], writes=[("Df", s_)])
                            P.op("act", lambda e: e.activation(B["Cb"][:], B["Df"][:], AF.Copy, scale=EG[:, col:col + 1]),
                                 reads=[("Df", s_), "EG"], writes=[("Cb", s_)])
                    if c > 0:
                        t_2(c - 1)
                        t_tr(c - 1)
                    for s_, h in enumerate(heads):
                        B = HB[s_]
                        col = c * 4 + h
                        P.op("dve", lambda e: e.tensor_scalar(out=B["N2"][i2][:], in0=PS[PBN[s_]][:, 0:97], scalar1=A_[:, col:col + 1],
                                                              scalar2=None, op0=ALU.mult), reads=[("ps", PBN[s_]), "A"],
                             writes=[("N2", s_, i2)])
                    if c > 0:
                        t_3(c - 1)
                c = NT - 1
                t_abs(c)
                t_1(c)
                t_act2(c)
                t_2(c)
                t_tr(c)
                t_3(c)

        def phase_a(l):
            AR.reset()
            QA = AR.alloc([128, 3, S], BF16)
            KA = AR.alloc([128, S], BF16)
            VSa = AR.alloc([128, NT, 128], BF16)
            ZSa = AR.alloc([128, NT, 384], BF16)
            SC6s = [AR.alloc([128, 6, 256], F32) for _ in range(2)]
            E6s = [AR.alloc([128, 6, 256], BF16) for _ in range(2)]
            ETs = AR.alloc([128, 6, 2, 128], BF16)
            YA0 = AR.alloc([128, 384], F32)
            YA = AR.alloc([128, 384], BF16)
            sm = small
            for t in range(NT):
                b = t % 2
                for k in range(8):
                    P.op("pe", lambda e: e.matmul(PS[b][:, :], hT[:, k, t * 128:(t + 1) * 128], WB[:, k, 512:1024],
                                                  start=(k == 0), stop=(k == 7)),
                         reads=[("WBi", k), ("hT", k, t // 4)], writes=[("ps", b)], sig=(k == 7))
                P.op("dve", lambda e: e.tensor_copy(VSa[:, t, :], PS[b][:, 0:128]), reads=[("ps", b)], writes=[("VSa", t)])
                P.op("act", lambda e: e.activation(ZSa[:, t, :], PS[b][:, 128:512], AF.Silu), reads=[("ps", b)], writes=[("ZSa", t)])
            bi = 0
            for gg in range(4):
                for tc in range(4):
                    b = 2 + bi % 2
                    bi += 1
                    if gg < 3:
                        for half, hq in enumerate((gg, gg + 3)):
                            for k in range(8):
                                P.op("pe", lambda e: e.matmul(PS[b][64 * half:64 * half + 64, :], WB[:, k, hq * 64:(hq + 1) * 64],
                                                              hT[:, k, tc * 512:(tc + 1) * 512], start=(k == 0), stop=(k == 7)),
                                     reads=[("WBi", k), ("hT", k, tc)], writes=[("ps", b)], sig=(half == 1 and k == 7))
                        P.op("act", lambda e: e.activation(QA[:, gg, tc * 512:(tc + 1) * 512], PS[b][:, :], AF.Copy, scale=0.125),
                             reads=[("ps", b)], writes=[("QA", tc)])
                    else:
                        for k in range(8):
                            P.op("pe", lambda e: e.matmul(PS[b][:, :], WB[:, k, 384:512], hT[:, k, tc * 512:(tc + 1) * 512],
                                                          start=(k == 0), stop=(k == 7)),
                                 reads=[("WBi", k), ("hT", k, tc)], writes=[("ps", b)], sig=(k == 7))
                        P.op("dve", lambda e: e.tensor_copy(KA[:, tc * 512:(tc + 1) * 512], PS[b][:, :]),
                             reads=[("ps", b)], writes=[("KA", tc)])
            def cols(n):
                o = 96 + 48 * (n % 2)
                return (sm[:, o:o + 6], sm[:, o + 8:o + 14], sm[:, o + 16:o + 22], sm[:, o + 24:o + 30], sm[:, o + 32:o + 38])

            def geom(n):
                k0 = max(0, n - 1) * 128
                W = (n + 1) * 128 - k0
                return k0, W, 256 - W, W // 128

            def f_pair(n, hp):
                pr = n % 2
                k0, W, boff, nb = geom(n)
                M6, NM6, ES6, RS6, RD6 = cols(n)
                SC6 = SC6s[pr]
                for h in (2 * hp, 2 * hp + 1):
                    kv = h // 3
                    b = 4 + h % 2
                    P.op("pe", lambda e: e.matmul(PS[b][:, 0:W], QA[64 * kv:64 * kv + 64, h % 3, n * 128:(n + 1) * 128],
                                                  KA[64 * kv:64 * kv + 64, k0:k0 + W], start=True, stop=True), reads=[], writes=[("ps", b)])
                    P.op("dve", lambda e: e.tensor_tensor(out=SC6[:, h, 0:W], in0=PS[b][:, 0:W], in1=bias6[:, h, boff:256], op=ALU.add),
                         reads=[("ps", b), "cst"], writes=[("SC6", pr, h)])
                    P.op("dve", lambda e: e.reduce_max(out=M6[:, h:h + 1], in_=SC6[:, h, 0:W], axis=AX.X), reads=[("SC6", pr, h)],
                         writes=[("M6", pr)])

            def f_tail(n, hp):
                pr = n % 2
                k0, W, boff, nb = geom(n)
                M6, NM6, ES6, RS6, RD6 = cols(n)
                SC6, E6 = SC6s[pr], E6s[pr]
                hs = slice(2 * hp, 2 * hp + 2)
                P.op("dve", lambda e: e.tensor_tensor(out=M6[:, hs], in0=M6[:, hs], in1=sinks_b[:, l, hs], op=ALU.max),
                     reads=[("M6", pr), "sinksb"], writes=[("M6", pr)])
                P.op("dve", lambda e: e.tensor_scalar(out=NM6[:, hs], in0=M6[:, hs], scalar1=-1.0, scalar2=None, op0=ALU.mult),
                     reads=[("M6", pr)], writes=[("NM6", pr, hp)])
                P.op("dve", lambda e: e.tensor_tensor(out=ES6[:, hs], in0=NM6[:, hs], in1=sinks_b[:, l, hs], op=ALU.add),
                     reads=[("NM6", pr, hp), "sinksb"], writes=[("ES6", pr, hp)])
                P.op("act", lambda e: e.activation(ES6[:, hs], ES6[:, hs], AF.Exp), reads=[("ES6", pr, hp)], writes=[("ES6", pr, hp)])
                for h in (2 * hp, 2 * hp + 1):
                    P.op("act", lambda e: e.activation(E6[:, h, 0:W], SC6[:, h, 0:W], AF.Exp, bias=NM6[:, h:h + 1],
                                                       accum_out=RS6[:, h:h + 1]), reads=[("SC6", pr, h), ("NM6", pr, hp)],
                         writes=[("E6", pr, h), ("RS6", pr, h)])

            def b_tr(n):
                pr = n % 2
                k0, W, boff, nb = geom(n)
                E6 = E6s[pr]
                for hg in range(2):
                    b = 6 + hg
                    pt = psv(b, [128, 3, 2, 128], BF16)
                    for hh in range(3):
                        h = hg * 3 + hh
                        for j in range(nb):
                            P.op("pe", lambda e: e.transpose(pt[:, hh, j, :], E6[:, h, j * 128:(j + 1) * 128], ident[:]),
                                 reads=[("E6", pr, h), "cst"], writes=[("ps", b)], sig=(hh == 2 and j == nb - 1))

            def b_ev(n):
                pr = n % 2
                k0, W, boff, nb = geom(n)
                M6, NM6, ES6, RS6, RD6 = cols(n)
                pt0 = psv(6, [128, 3, 2, 128], BF16)
                pt1 = psv(7, [128, 3, 2, 128], BF16)
                P.op("dve", lambda e: e.tensor_copy(ETs[:, 0:3, 0:nb, :], pt0[:, :, 0:nb, :]), reads=[("ps", 6)], writes=[("ETs", 0)])
                P.op("act", lambda e: e.copy(ETs[:, 3:6, 0:nb, :], pt1[:, :, 0:nb, :]), reads=[("ps", 7)], writes=[("ETs", 1)])
                P.op("dve", lambda e: e.tensor_tensor(out=RD6, in0=RS6, in1=ES6, op=ALU.add),
                     reads=[("RS6", pr, h) for h in range(6)] + [("ES6", pr, hp) for hp in range(3)], writes=[("RD6", pr)])
                P.op("dve", lambda e: e.reciprocal(RD6, RD6), reads=[("RD6", pr)], writes=[("RD6", pr)])

            def b_pv(n):
                k0, W, boff, nb = geom(n)
                po = psv(n % 2, [128, 6, 64])
                pb = n % 2
                for h in range(6):
                    kv = h // 3
                    for j in range(nb):
                        kb = k0 // 128 + j
                        P.op("pe", lambda e: e.matmul(po[:, h, :], ETs[:, h, j, :], VSa[:, kb, kv * 64:(kv + 1) * 64], start=(j == 0),
                                                      stop=(j == nb - 1)), reads=[("ETs", h // 3), ("VSa", kb)], writes=[("ps", pb)],
                             sig=(h == 5 and j == nb - 1))

            def b_ya(n):
                pr = n % 2
                M6, NM6, ES6, RS6, RD6 = cols(n)
                po = psv(n % 2, [128, 6, 64])
                pb = n % 2
                P.op("dve", lambda e: e.tensor_tensor(out=YA0.rearrange("p (a b) -> p a b", b=64), in0=po,
                                                      in1=RD6.unsqueeze(2).to_broadcast([128, 6, 64]), op=ALU.mult),
                     reads=[("ps", pb), ("RD6", pr)], writes=["YA0"])
                P.op("pool", lambda e: e.tensor_tensor(out=YA[:], in0=YA0[:], in1=ZSa[:, n, :], op=ALU.mult), reads=["YA0", ("ZSa", n)],
                     writes=["YA"])

            def b_out(n):
                b = 2 + n % 2
                pt = psv(b, [128, 3, 128], BF16)
                for j in range(3):
                    P.op("pe", lambda e: e.transpose(pt[:, j, :], YA[:, j * 128:(j + 1) * 128], ident[:]), reads=["YA", "cst"],
                         writes=[("ps", b)], sig=(j == 2))
                P.op("act", lambda e: e.copy(yT[:, 4:7, n * 128:(n + 1) * 128], pt), reads=[("ps", b)], writes=[("yTa", n)])

            for hp in range(3):
                f_pair(0, hp)
                f_tail(0, hp)
            for n in range(NT):
                nxt = n + 1 < NT
                if nxt:
                    f_pair(n + 1, 0)
                    f_tail(n + 1, 0)
                b_tr(n)
                b_ev(n)
                b_pv(n)
                if nxt:
                    f_pair(n + 1, 1)
                    f_tail(n + 1, 1)
                b_ya(n)
                if nxt:
                    f_pair(n + 1, 2)
                    f_tail(n + 1, 2)
                b_out(n)

        def phase_s(l):
            AR.reset()
            QS = AR.alloc([128, 2, S], BF16)
            KS = AR.alloc([128, 2, S], BF16)
            VS = AR.alloc([128, NT, 256], BF16)
            ZS = AR.alloc([128, NT, 256], BF16)
            YS = AR.alloc([128, 256], BF16)
            for t in range(NT):
                b = t % 2
                for k in range(8):
                    P.op("pe", lambda e: e.matmul(PS[b][:, :], hT[:, k, t * 128:(t + 1) * 128], WA[:, k, 512:1024],
                                                  start=(k == 0), stop=(k == 7)),
                         reads=[("WA", k), ("hT", k, t // 4)], writes=[("ps", b)], sig=(k == 7))
                P.op("dve", lambda e: e.tensor_copy(VS[:, t, :], PS[b][:, 0:256]), reads=[("ps", b)], writes=[("VS", t)])
                P.op("act", lambda e: e.activation(ZS[:, t, :], PS[b][:, 256:512], AF.Silu), reads=[("ps", b)], writes=[("ZS", t)])
            bi = 0
            for gg in range(4):
                for tc in range(4):
                    b = 2 + bi % 2
                    bi += 1
                    for k in range(8):
                        P.op("pe", lambda e: e.matmul(PS[b][:, :], WA[:, k, gg * 128:(gg + 1) * 128], hT[:, k, tc * 512:(tc + 1) * 512],
                                                      start=(k == 0), stop=(k == 7)),
                             reads=[("WA", k), ("hT", k, tc)], writes=[("ps", b)], sig=(k == 7))
                    if gg < 2:
                        P.op("act", lambda e: e.activation(QS[:, gg, tc * 512:(tc + 1) * 512], PS[b][:, :], AF.Copy, scale=0.125),
                             reads=[("ps", b)], writes=[("QS", tc)])
                    else:
                        P.op("dve", lambda e: e.tensor_copy(KS[:, gg - 2, tc * 512:(tc + 1) * 512], PS[b][:, :]),
                             reads=[("ps", b)], writes=[("KS", tc)])
            P.barrier()
            AH = Arena(hT[:].rearrange("p a b -> p (a b)"), 8 * S)
            Kh = [[AH.alloc([128, 512], F32) for _ in range(4)] for _ in range(2)]
            Qr = [q_[:] for q_ in Qr_t]
            Qh = Qr
            PEH = (3,)
            Ah = [AH.alloc([128, 512], BF16) for _ in range(4)]
            ATh = [AH.alloc([128, 4, 128], BF16) for _ in range(4)]
            cnt = {"z": 0, "t": 0}
            its = []
            for qb in range(NT):
                Wq = (qb + 1) * 128
                hi = Wq
                si = 0
                while hi > 0:
                    lo = max(0, hi - 512)
                    its.append((qb, si, lo, hi, lo == 0))
                    hi = lo
                    si += 1

            def stage1(i):
                qb, si, lo, hi, last_seg = its[i]
                pr = i % 2
                w = hi - lo
                for h in range(4):
                    zb = cnt["z"] % 3
                    cnt["z"] += 1
                    hp0 = 64 * (h % 2)
                    P.op("pe", lambda e: e.matmul(PS[zb][:, 0:w], QS[hp0:hp0 + 64, h // 2, qb * 128:(qb + 1) * 128],
                                                  KS[hp0:hp0 + 64, h // 2, lo:hi], start=True, stop=(si != 0)),
                         reads=[], writes=[("ps", zb)], sig=(si != 0))
                    if si == 0:
                        P.op("pe", lambda e: e.matmul(PS[zb][:, w - 128:w], ident[:], negm[:], start=False, stop=True),
                             reads=["cst"], writes=[("ps", zb)])
                    P.op("act", lambda e: e.activation(Kh[pr][h][:, 0:w], PS[zb][:, 0:w], AF.Sigmoid, scale=-1.0), reads=[("ps", zb)],
                         writes=[("K", pr, h)])

            def stage234(i):
                qb, si, lo, hi, last_seg = its[i]
                pr = i % 2
                w = hi - lo
                nbk = w // 128
                pob = 5 + qb % 2
                po = psv(pob, [128, 4, 64])
                for h in range(4):
                    Q = Qr[h]
                    if si == 0:
                        P.op("pool", lambda e: e.tensor_copy(Q[:, w:w + 1], ones[:, 0:1]), reads=[("Q", h), "cst"], writes=[("Qc", h)])
                    else:
                        P.op("pool", lambda e: e.tensor_copy(Q[:, w:w + 1], Qh[h][:, 0:1]), reads=[("Q", h)], writes=[("Qc", h)])
                for h in range(4):
                    Q = Qr[h]
                    P.op("dve", lambda e: e.tensor_tensor_scan(out=Q[:, 0:w][:, ::-1], data0=Kh[pr][h][:, 0:w][:, ::-1],
                                                               data1=ones[:, 0:w], initial=Qh[h][:, w:w + 1], op0=ALU.mult,
                                                               op1=ALU.mult), reads=[("K", pr, h), "cst", ("Qc", h)],
                         writes=[("Q", h)])
                for h in range(4):
                    if h not in PEH:
                        Q = Qr[h]
                        P.op("dve", lambda e: e.tensor_tensor(out=Ah[h][:, 0:w], in0=Q[:, 1:w + 1], in1=Q[:, 0:w], op=ALU.subtract),
                             reads=[("Q", h), ("Qc", h)], writes=[("A", h)])

                def tr(h):
                    tb = 3 + h % 2
                    if h in PEH:
                        pt = psv(tb, [128, 4, 128])
                        Q = Qr[h]
                        for j in range(nbk):
                            P.op("pe", lambda e: e.matmul(pt[:, j, :], Q[:, 1 + j * 128:1 + (j + 1) * 128], idp_r[:], start=True,
                                                          stop=False), reads=[("Q", h), ("Qc", h), "cst"], writes=[("ps", tb)], sig=False)
                            P.op("pe", lambda e: e.matmul(pt[:, j, :], Q[:, j * 128:(j + 1) * 128], idn_r[:], start=False, stop=True),
                                 reads=[("Q", h), "cst"], writes=[("ps", tb)], sig=(j == nbk - 1))
                    else:
                        pt = psv(tb, [128, 4, 128], BF16)
                        for j in range(nbk):
                            P.op("pe", lambda e: e.transpose(pt[:, j, :], Ah[h][:, j * 128:(j + 1) * 128], ident[:]),
                                 reads=[("A", h), "cst"], writes=[("ps", tb)], sig=(j == nbk - 1))

                def ev(h):
                    tb = 3 + h % 2
                    pt = psv(tb, [128, 4, 128]) if h in PEH else psv(tb, [128, 4, 128], BF16)
                    P.op("act", lambda e: e.copy(ATh[h][:, 0:nbk, :], pt[:, 0:nbk, :]), reads=[("ps", tb)], writes=[("AT", h)])

                def pv(h):
                    for j in range(nbk):
                        kb = lo // 128 + j
                        first_pv = (si == 0 and h == 3 and j == 0)
                        P.op("pe", lambda e: e.matmul(po[:, h, :], ATh[h][:, j, :], VS[:, kb, h * 64:(h + 1) * 64], start=first_pv,
                                                      stop=(last_seg and j == nbk - 1), skip_group_check=True),
                             reads=[("AT", h), ("VS", kb)], writes=[("ps", pob)], sig=(j == nbk - 1))

                tr(3); tr(0); ev(3); ev(0); tr(1); tr(2); pv(3); pv(0); ev(1); ev(2); pv(1); pv(2)
                return qb if last_seg else None

            def finalize(qb):
                pob = 5 + qb % 2
                P.op("dve", lambda e: e.tensor_tensor(out=YS[:], in0=psv(pob, [128, 256]), in1=ZS[:, qb, :], op=ALU.mult),
                     reads=[("ps", pob), ("ZS", qb)], writes=["YS"])
                pt = psv(7, [128, 2, 128], BF16)
                for j in range(2):
                    P.op("pe", lambda e: e.transpose(pt[:, j, :], YS[:, j * 128:(j + 1) * 128], ident[:]), reads=["YS", "cst"],
                         writes=[("ps", 7)], sig=(j == 1))
                P.op("act", lambda e: e.copy(yT[:, 7:9, qb * 128:(qb + 1) * 128], pt), reads=[("ps", 7)], writes=[("yTs", qb)])

            stage1(0)
            pend = None
            for i in range(len(its)):
                if i + 1 < len(its):
                    stage1(i + 1)
                done = stage234(i)
                if pend is not None:
                    finalize(pend)
                pend = done
            if pend is not None:
                finalize(pend)

        def phase_o(l, xsrc, xdst, fuse_next_h):
            AR.reset()
            XR = [AR.alloc([128, D], F32) for _ in range(2)]
            XO = [AR.alloc([128, D], F32) for _ in range(2)]
            jk = AR.alloc([128, D], BF16)
            xs = [AR.alloc([128, D], BF16) for _ in range(8)]
            sm = small
            evc = [0]

            def ldx(t):
                i2 = t % 2
                P.dma("sp", ("XR", i2), XR[i2][:], xsrc[t * 128:(t + 1) * 128, :], writes=[("XR", i2)])

            def mm(t):
                i2 = t % 2
                ts_ = slice(t * 128, (t + 1) * 128)
                o = 160 + 8 * i2
                for half in range(2):
                    b = 2 * i2 + half
                    for c in range(9):
                        kc = 96 if c < 4 else 128
                        P.op("pe", lambda e: e.matmul(PS[b][:, :], yT[0:kc, c, ts_], WB[0:kc, c, half * 512:(half + 1) * 512],
                                                      start=(c == 0), stop=(c == 8)), reads=[("WB", c)], writes=[("ps", b)],
                             sig=(c == 8))
                    P.op("act", lambda e: e.activation(jk[:, 0:512], PS[b][:, :], AF.Square, accum_out=sm[:, o + half:o + half + 1]),
                         reads=[("ps", b)], writes=["jko", ("smo", i2, half)])

            def chain(t):
                i2 = t % 2
                ts_ = slice(t * 128, (t + 1) * 128)
                o = 160 + 8 * i2
                P.op("dve", lambda e: e.tensor_tensor(out=sm[:, o + 2:o + 3], in0=sm[:, o:o + 1], in1=sm[:, o + 1:o + 2], op=ALU.add),
                     reads=[("smo", i2, 0), ("smo", i2, 1)], writes=[("smo", i2, 2)])
                P.op("act", lambda e: e.activation(sm[:, o + 3:o + 4], sm[:, o + 2:o + 3], AF.Sqrt, bias=EPS, scale=1.0 / D),
                     reads=[("smo", i2, 2)], writes=[("smo", i2, 3)])
                P.op("dve", lambda e: e.reciprocal(sm[:, o + 4:o + 5], sm[:, o + 3:o + 4]), reads=[("smo", i2, 3)],
                     writes=[("smo", i2, 4)])
                for half in range(2):
                    b = 2 * i2 + half
                    hs = slice(half * 512, (half + 1) * 512)
                    P.op("dve", lambda e: e.scalar_tensor_tensor(out=XO[i2][:, hs], in0=PS[b][:, :], scalar=sm[:, o + 4:o + 5],
                                                                 op0=ALU.mult, in1=GG[:, l, hs], op1=ALU.mult),
                         reads=[("ps", b), ("smo", i2, 4), ("GG", l)], writes=[("XO", i2, half)])
                    P.op("dve", lambda e: e.tensor_tensor(out=XO[i2][:, hs], in0=XO[i2][:, hs], in1=XR[i2][:, hs], op=ALU.add),
                         reads=[("XO", i2, half), ("XR", i2)], writes=[("XO", i2, half)])
                P.dma("pool", ("XO", i2), xdst[ts_, :], XO[i2][:], reads=[("XO", i2, 0), ("XO", i2, 1)], writes=[("xdst", t)])

            def hnorm(t):
                i2 = t % 2
                j = t % 8
                o2 = 208 + 4 * j
                P.op("act", lambda e: e.activation(jk[:], XO[i2][:], AF.Square, accum_out=sm[:, o2:o2 + 1]),
                     reads=[("XO", i2, 0), ("XO", i2, 1)], writes=["jko", ("hss", j)])
                P.op("act", lambda e: e.activation(sm[:, o2 + 1:o2 + 2], sm[:, o2:o2 + 1], AF.Sqrt, bias=EPS, scale=1.0 / D),
                     reads=[("hss", j)], writes=[("hsq", j)])
                P.op("dve", lambda e: e.reciprocal(sm[:, o2 + 2:o2 + 3], sm[:, o2 + 1:o2 + 2]), reads=[("hsq", j)],
                     writes=[("hrs", j)])
                P.op("act", lambda e: e.activation(xs[j][:], XO[i2][:], AF.Copy, scale=sm[:, o2 + 2:o2 + 3]),
                     reads=[("XO", i2, 0), ("XO", i2, 1), ("hrs", j)], writes=[("xs", j)])

            def htrans(g):
                ln = l + 1
                for k in range(8):
                    b = 4 + k % 2
                    pt = psv(b, [128, 4, 128], BF16)
                    for jj in range(4):
                        xj = (4 * g + jj) % 8
                        P.op("pe", lambda e: e.transpose(pt[:, jj, :], xs[xj][:, k * 128:(k + 1) * 128], ident[:]),
                             reads=[("xs", xj), "cst"], writes=[("ps", b)], sig=(jj == 3))
                    o_ap = hT[:, k, g * 512:(g + 1) * 512]
                    i_ap = psv(b, [128, 512], BF16)
                    if evc[0] % 2 == 0:
                        P.op("dve", lambda e: e.tensor_scalar(out=o_ap, in0=i_ap, scalar1=gcols[:, ln, 8 + k:9 + k],
                                                              scalar2=gcols[:, ln, k:k + 1], op0=ALU.mult, op1=ALU.add),
                             reads=[("ps", b), ("gcols", ln)], writes=[("hT", k, g)])
                    else:
                        P.op("act", lambda e: e.activation(o_ap, i_ap, AF.Identity, bias=gcols[:, ln, k:k + 1],
                                                           scale=gcols[:, ln, 8 + k:9 + k]),
                             reads=[("ps", b), ("gcols", ln)], writes=[("hT", k, g)])
                    evc[0] += 1

            ldx(0)
            for t in range(NT):
                if t + 1 < NT:
                    ldx(t + 1)
                mm(t)
                if fuse_next_h:
                    if t > 0:
                        hnorm(t - 1)
                    if t >= 5 and (t - 5) % 4 == 0:
                        htrans((t - 5) // 4)
                chain(t)
            if fuse_next_h:
                hnorm(NT - 1)
                htrans(3)

        def schedule():
            load_w_in(WA, "WA", 0, 0, 1160)
            phase_mod()
            P.barrier()
            if stop_after == "mod":
                return
            for l in range(n_layers):
                xsrc = x_in if l == 0 else x1
                xdst = out if l == n_layers - 1 else x1
                if l == 0:
                    phase_h(l, xsrc, 1 if n_layers > 1 else None)
                load_w_in(WB, "WBi", l, 1160, 1024)
                P.barrier()
                if debug and l == 0:
                    P.dma("sp", "dbg", dbg["hT"], hT[:], reads=HT_ALL)
                if stop_after == "h":
                    return
                phase_m(l)
                P.barrier()
                if stop_after == "m":
                    break
                load_w_in(WA, "WA", l, 2184, 1024)
                phase_a(l)
                P.barrier()
                if stop_after == "a":
                    break
                load_w_out(l)
                phase_s(l)
                P.barrier()
                if stop_after == "s":
                    break
                if l + 1 < n_layers:
                    load_w_in(WA, "WA", l + 1, 0, 1160)
                if debug and l == 0:
                    P.dma("sp", "dbg", dbg["yT"], yT[:], reads=[])
                phase_o(l, xsrc, xdst, l + 1 < n_layers)
                P.barrier()
            if stop_after in ("m", "a", "s") and debug:
                P.dma("sp", "dbg", dbg["yT"], yT[:], reads=[])

        schedule()
        P.wait_all_dma("sp")
        build.stats = (P.n_ops, P.n_waits, P.nsem, AR.hi, nc.sbuf_bytes_remaining)
    return nc


def make_in_maps(x, c, w_mod, b_mod, g_pre, g_post, w_in, m_conv_w, m_conv_b, m_wq, m_wk, m_b_i, m_b_f, m_norm_g, m_skip,
                 a_sinks, w_out):
    f = lambda a: np.ascontiguousarray(np.asarray(a, dtype=np.float32))
    x, c = f(x), f(c)
    shared = {
        "w_mod": f(w_mod), "b_mod": f(b_mod), "g_post": f(g_post), "w_in": f(w_in), "w_out": f(w_out),
        "gpre_cols": f(np.asarray(g_pre).reshape(NL, 8, 128).transpose(2, 0, 1)),
        "convw": f(np.asarray(m_conv_w).reshape(NL, 4, 4, 96).transpose(3, 0, 2, 1)),
        "mvec": f(np.stack([np.asarray(m_conv_b), np.asarray(m_norm_g), np.asarray(m_skip)], 1).reshape(NL, 3, 4, 96)
                  .transpose(3, 0, 1, 2)),
        "wq": f(np.asarray(m_wq).transpose(2, 0, 1, 3)),
        "wk": f(np.asarray(m_wk).transpose(2, 0, 1, 3)),
        "m_b_i": f(m_b_i), "m_b_f": f(m_b_f), "a_sinks": f(a_sinks),
    }
    maps = []
    for b in range(8):
        m = dict(shared)
        m["x"] = x[b]
        m["c_cols"] = f(c[b].reshape(8, 128).T)
        maps.append(m)
    return maps


def kernel(**inputs):
    nc = build()
    maps = make_in_maps(**inputs)
    res = run_bass_kernel_spmd(nc, maps, core_ids=list(range(8)))
    return np.stack([np.asarray(r["out"], dtype=np.float32) for r in res.results], axis=0)
```

```python
import math
from contextlib import ExitStack

import numpy as np
import concourse.bass as bass
import concourse.mybir as mybir
from concourse.bass_utils import run_bass_kernel_spmd

F32 = mybir.dt.float32
BF16 = mybir.dt.bfloat16
AF = mybir.ActivationFunctionType
ALU = mybir.AluOpType
AX = mybir.AxisListType

S = 2048
D = 1024
NT = 16
NL = 2
EPS = 1e-6
NIN = 3208
ENG_NAMES = ("pe", "act", "dve", "pool", "sp")


class _Op:
    __slots__ = ("eng", "sem", "val", "is_dma", "key")

    def __init__(self, eng):
        self.eng = eng
        self.sem = None
        self.val = None
        self.is_dma = False
        self.key = None


class Prog:
    EPOCH = 30000

    def __init__(self, nc, stack):
        self.nc = nc
        self.stack = stack
        self.engs = {"pe": nc.tensor, "act": nc.scalar, "dve": nc.vector, "pool": nc.gpsimd, "sp": nc.sync}
        self.cnt = {e: 0 for e in ENG_NAMES}
        self.sem = {e: None for e in ENG_NAMES}
        self.nsem = 0
        self.pending = {e: [] for e in ENG_NAMES}
        self.last_w = {}
        self.readers = {}
        self.waited = {}
        self.dma_sem = {}
        self.dma_cnt = {}
        self.n_ops = 0
        self.n_waits = 0

    def _new_sem(self, name):
        self.nsem += 1
        return self.stack.enter_context(self.nc.semaphore(f"{name}_{self.nsem}"))

    def _eng_sem(self, e):
        if self.sem[e] is None or (self.cnt[e] >= self.EPOCH and not self.pending[e]):
            self.sem[e] = self._new_sem("c" + e)
            self.cnt[e] = 0
        return self.sem[e]

    def _deps(self, reads, writes):
        deps = []
        for r in reads:
            p = self.last_w.get(r)
            if p is not None:
                deps.append((p, True))
        for w in writes:
            p = self.last_w.get(w)
            if p is not None:
                deps.append((p, False))
            deps.extend((q, False) for q in self.readers.get(w, ()))
        return deps

    def _wait(self, eng, sem, val):
        k = id(sem)
        if self.waited.get((eng, k), -1) >= val:
            return
        self.engs[eng].wait_ge(sem, val)
        self.waited[(eng, k)] = val
        self.n_waits += 1

    def _emit_waits(self, eng, deps, consumer_is_dma=False):
        need = {}
        for p, is_raw in deps:
            if p.is_dma:
                sem = self.dma_sem[p.key]
                val = 16 * self.dma_cnt[p.key]
            else:
                if p.eng == eng and not consumer_is_dma and (eng == "pe" or not is_raw):
                    continue
                assert p.val is not None, "dependency on an un-signalled op (close the PE group with sig=True)"
                sem, val = p.sem, p.val
            k = id(sem)
            if k not in need or need[k][1] < val:
                need[k] = (sem, val)
        for sem, val in need.values():
            self._wait(eng, sem, val)

    def _record(self, op, reads, writes):
        for w in writes:
            self.last_w[w] = op
            self.readers[w] = []
        for r in reads:
            if r in writes:
                continue
            self.readers.setdefault(r, []).append(op)

    def op(self, eng, fn, reads=(), writes=(), sig=True):
        psr = tuple(r for r in reads if isinstance(r, tuple) and r[0] == "ps")
        reads = tuple(r for r in reads if not (isinstance(r, tuple) and r[0] == "ps"))
        writes = tuple(writes) + psr
        o = _Op(eng)
        self._emit_waits(eng, self._deps(reads, writes))
        ins = fn(self.engs[eng])
        if sig:
            sem = self._eng_sem(eng)
            ins.then_inc(sem, 1)
            self.cnt[eng] += 1
            o.sem, o.val = sem, self.cnt[eng]
            for q in self.pending[eng]:
                q.sem, q.val = sem, self.cnt[eng]
            self.pending[eng] = []
        else:
            self.pending[eng].append(o)
        self._record(o, reads, writes)
        self.n_ops += 1
        return o

    def dma(self, eng, key, out, in_, reads=(), writes=(), **kw):
        reads = tuple(reads)
        writes = tuple(writes)
        key = (key, eng == "pool")
        o = _Op(eng)
        o.is_dma = True
        o.key = key
        if key not in self.dma_sem:
            self.dma_sem[key] = self._new_sem("d")
            self.dma_cnt[key] = 0
        self._emit_waits(eng, self._deps(reads, writes), consumer_is_dma=True)
        self.engs[eng].dma_start(out=out, in_=in_, **kw).then_inc(self.dma_sem[key], 16)
        self.dma_cnt[key] += 1
        self._record(o, reads, writes)
        self.n_ops += 1
        return o

    def barrier(self):
        for e in ENG_NAMES:
            assert not self.pending[e], f"open group on {e}"
        for e in ENG_NAMES:
            for f in ENG_NAMES:
                if f == e or self.sem[f] is None or self.cnt[f] == 0:
                    continue
                self._wait(e, self.sem[f], self.cnt[f])
            for key, sem in self.dma_sem.items():
                if self.dma_cnt[key]:
                    self._wait(e, sem, 16 * self.dma_cnt[key])
        self.last_w = {}
        self.readers = {}

    def wait_all_dma(self, eng="sp"):
        for key, sem in self.dma_sem.items():
            if self.dma_cnt[key]:
                self._wait(eng, sem, 16 * self.dma_cnt[key])


class Arena:
    def __init__(self, t, nelem):
        self.t = t
        self.n = nelem
        self.off = 0

    def reset(self):
        self.off = 0

    def alloc(self, shape, dt):
        n = int(np.prod(shape[1:]))
        nel = n * 2 if dt == F32 else n
        if self.off % 2:
            self.off += 1
        assert self.off + nel <= self.n, f"arena overflow {self.off + nel} > {self.n}"
        v = self.t[0:shape[0], self.off:self.off + nel]
        if dt == F32:
            v = v.bitcast(F32)
        self.off += nel + (nel % 2)
        self.hi = max(getattr(self, "hi", 0), self.off)
        if len(shape) == 3:
            v = v.rearrange("p (a b) -> p a b", b=shape[2])
        elif len(shape) == 4:
            v = v.rearrange("p (a b c) -> p a b c", b=shape[2], c=shape[3])
        return v


def build(debug=False, n_layers=NL, stop_after=None):
    nc = bass.Bass("TRN2", target_bir_lowering=False)
    dt_in = lambda name, shape: nc.dram_tensor(name, shape, F32, kind="ExternalInput").ap()
    x_in = dt_in("x", [S, D])
    c_cols = dt_in("c_cols", [128, 8])
    w_mod = dt_in("w_mod", [NL, D, 3 * D])
    b_mod = dt_in("b_mod", [NL, 3 * D])
    gpre_cols = dt_in("gpre_cols", [128, NL, 8])
    g_post = dt_in("g_post", [NL, D])
    w_in = dt_in("w_in", [NL, D, NIN])
    convw = dt_in("convw", [96, NL, 4, 4])
    mvec = dt_in("mvec", [96, NL, 3, 4])
    wq_d = dt_in("wq", [96, NL, 4, 96])
    wk_d = dt_in("wk", [96, NL, 4, 96])
    bi_d = dt_in("m_b_i", [NL, 4])
    bf_d = dt_in("m_b_f", [NL, 4])
    sinks_d = dt_in("a_sinks", [NL, 6])
    w_out = dt_in("w_out", [NL, D, D])
    out = nc.dram_tensor("out", [S, D], F32, kind="ExternalOutput").ap()
    x1 = nc.dram_tensor("x1_scratch", [S, D], F32).ap()
    dbg = {}
    if debug:
        dbg["hT"] = nc.dram_tensor("dbg_hT", [128, 8, S], BF16, kind="ExternalOutput").ap()
        dbg["yT"] = nc.dram_tensor("dbg_yT", [128, 9, S], BF16, kind="ExternalOutput").ap()
        dbg["mod"] = nc.dram_tensor("dbg_mod", [128, 16 + 1024], F32, kind="ExternalOutput").ap()

    with ExitStack() as st:
        P = Prog(nc, st)
        sb = lambda name, shape, dt: st.enter_context(nc.sbuf_tensor(name, shape, dt))
        PS = [st.enter_context(nc.psum_tensor(f"ps{i}", [128, 512], F32)) for i in range(8)]

        def psv(i, shape, dt=F32):
            n = int(np.prod(shape[1:]))
            if dt == F32:
                v = PS[i][0:shape[0], 0:n]
            else:
                v = PS[i][0:shape[0], 0:(n + 1) // 2].bitcast(BF16)
            if len(shape) == 3:
                v = v.rearrange("p (a b) -> p a b", b=shape[2])
            elif len(shape) == 4:
                v = v.rearrange("p (a b c) -> p a b c", b=shape[2], c=shape[3])
            return v

        hT = sb("hT", [128, 8, S], BF16)
        yT = sb("yT", [128, 9, S], BF16)
        WA = sb("WA", [128, 8, 1160], BF16)
        WB = sb("WB", [128, 9, 1024], BF16)
        ident = sb("ident", [128, 128], BF16)
        triu = sb("triu", [128, 128], F32)
        sellast = sb("sellast", [128, 96], F32)
        masksb = sb("masksb", [128, 128], F32)
        notmask = triu
        negm = sb("negm", [128, 128], BF16)
        F32R = mybir.dt.float32r
        Qr_t = [sb(f"Qr{i}", [128, 514], F32R) for i in range(4)]
        idp_r = sb("idp_r", [128, 128], F32R)
        idn_r = sb("idn_r", [128, 128], F32R)
        ones = sb("ones", [128, 512], F32)
        rel = sb("rel", [128, 256], F32)
        bias6 = sb("bias6", [128, 6, 256], F32)
        GG = sb("GG", [128, NL, D], F32)
        gcols = sb("gcols", [128, NL, 16], F32)
        gpre_sb = sb("gpre_sb", [128, NL, 8], F32)
        cact = sb("cact", [128, 8], F32)
        convw_sb = sb("convw_sb", [96, NL, 4, 4], F32)
        mvec_sb = sb("mvec_sb", [96, NL, 3, 4], F32)
        WQ = sb("WQ", [96, NL, 4, 96], BF16)
        WK = sb("WK", [96, NL, 4, 96], BF16)
        bi_b = sb("bi_b", [128, NL, 4], F32)
        bf_b = sb("bf_b", [128, NL, 4], F32)
        sinks_b = sb("sinks_b", [128, NL, 6], F32)
        small = sb("small", [128, 256], F32)
        ARENA_N = 35600
        arena_t = sb("arena", [128, ARENA_N], BF16)
        AR = Arena(arena_t, ARENA_N)

        P.dma("sp", "c0", cact[:], c_cols, writes=["cact"])
        P.dma("sp", "c0", gpre_sb[:], gpre_cols, writes=["gpre"])
        P.dma("sp", "c0", convw_sb[:], convw, writes=["convw"])
        P.dma("sp", "c0", mvec_sb[:], mvec, writes=["mvec"])
        for l in range(NL):
            P.dma("sp", "c0", bi_b[:, l, :], bi_d[l].partition_broadcast(128), writes=["bib"])
            P.dma("sp", "c0", bf_b[:, l, :], bf_d[l].partition_broadcast(128), writes=["bfb"])
            P.dma("sp", "c0", sinks_b[:, l, :], sinks_d[l].partition_broadcast(128), writes=["sinksb"])
        P.dma("pool", "c1", WQ[:], wq_d, writes=["WQ"])
        P.dma("pool", "c1", WK[:], wk_d, writes=["WK"])

        def aff(out_ap, pattern, base, cm, cmp, fill=0.0, in_ap=None):
            P.op("pool", lambda e: e.affine_select(out=out_ap, in_=out_ap if in_ap is None else in_ap, pattern=pattern,
                                                   compare_op=cmp, fill=fill, base=base, channel_multiplier=cm),
                 reads=["cst"], writes=["cst"])

        for tile_, in ((ident,), (triu,), (sellast,), (masksb,), (ones,)):
            P.op("pool", lambda e: e.memset(tile_[:], 1.0), writes=["cst"])
        aff(ident[:], [[-1, 128]], 0, 1, ALU.is_equal)
        aff(triu[:], [[1, 128]], 0, -1, ALU.is_ge)
        aff(sellast[:], [[0, 96]], -127, 1, ALU.is_equal)
        aff(masksb[:], [[-1, 128]], 0, 1, ALU.is_gt)
        P.op("dve", lambda e: e.tensor_scalar(out=negm[:], in0=triu[:], scalar1=-30000.0, scalar2=None, op0=ALU.mult), reads=["cst"],
             writes=["cst"])
        P.op("dve", lambda e: e.tensor_copy(idp_r[:], ident[:]), reads=["cst"], writes=["cst"])
        P.op("dve", lambda e: e.tensor_scalar(out=idn_r[:], in0=ident[:], scalar1=-1.0, scalar2=None, op0=ALU.mult), reads=["cst"],
             writes=["cst"])
        P.op("pool", lambda e: e.iota(rel[:], pattern=[[-1, 256]], base=128, channel_multiplier=1,
                                      allow_small_or_imprecise_dtypes=True), reads=["cst"], writes=["cst"])
        for h in range(6):
            slope = 2.0 ** (-8.0 * (h + 1) / 6.0)
            P.op("pool", lambda e: e.tensor_scalar(out=bias6[:, h, :], in0=rel[:], scalar1=-slope, scalar2=None, op0=ALU.mult),
                 reads=["cst"], writes=["cst"])
        aff(bias6[:], [[0, 6], [-1, 256]], 128, 1, ALU.is_ge, fill=-1e30)
        aff(bias6[:], [[0, 6], [1, 256]], -1, -1, ALU.is_ge, fill=-1e30)
        P.op("act", lambda e: e.activation(cact[:], cact[:], AF.Silu), reads=["cact"], writes=["cact"])

        def load_w_in(buf, name, l, c0, ncols):
            for k in range(8):
                P.dma("pool", name, buf[:, k, 0:ncols], w_in[l, k * 128:(k + 1) * 128, c0:c0 + ncols], writes=[(name, k)])

        def load_w_out(l):
            for c in range(4):
                P.dma("pool", "WB", WB[0:96, c, :], w_out[l, c * 96:(c + 1) * 96, :], writes=[("WB", c)])
            for c in range(4, 9):
                r0 = 384 + (c - 4) * 128
                P.dma("pool", "WB", WB[:, c, :], w_out[l, r0:r0 + 128, :], writes=[("WB", c)])

        def mod_parts(l, stg, acc, mod_row, gg_row, queues):
            NSTG = len(stg)

            def chunk(k):
                si = k % NSTG
                s_ = stg[si]
                if k == 0:
                    P.dma("sp", "rows", mod_row[:], b_mod[l:l + 1, :], writes=["mod_row"])
                    P.dma("sp", "rows", gg_row[:], g_post[l:l + 1, :], writes=["gg_row"])
                P.dma(queues[k % len(queues)], ("stg", si), s_[:], w_mod[l, k * 128:(k + 1) * 128, :], writes=[("stg", si)])
                if k == 0:
                    P.op("dve", lambda e: e.tensor_scalar(out=acc[:], in0=s_[:], scalar1=cact[:, k:k + 1], scalar2=None,
                                                          op0=ALU.mult), reads=[("stg", si), "cact"], writes=["acc"])
                else:
                    P.op("dve", lambda e: e.scalar_tensor_tensor(out=acc[:], in0=s_[:], scalar=cact[:, k:k + 1], op0=ALU.mult,
                                                                 in1=acc[:], op1=ALU.add), reads=[("stg", si), "cact", "acc"],
                         writes=["acc"])

            def finish():
                for j in range(6):
                    P.op("pe", lambda e: e.matmul(PS[j][0:1, :], ones[:, 0:1], acc[:, j * 512:(j + 1) * 512], start=True, stop=True),
                         reads=["acc", "cst"], writes=[("ps", j)])
                for j in range(6):
                    P.op("dve", lambda e: e.tensor_tensor(out=mod_row[:, j * 512:(j + 1) * 512], in0=PS[j][0:1, :],
                                                          in1=mod_row[:, j * 512:(j + 1) * 512], op=ALU.add),
                         reads=[("ps", j), "mod_row"], writes=["mod_row"])
                for k in range(16):
                    P.op("pe", lambda e: e.matmul(PS[6][:, k:k + 1], mod_row[:, k * 128:(k + 1) * 128], ones[0:1, 0:1],
                                                  start=True, stop=True),
                         reads=["mod_row", "cst"], writes=[("ps", 6)], sig=(k == 15))
                P.op("dve", lambda e: e.tensor_copy(gcols[:, l, 0:8], PS[6][:, 0:8]), reads=[("ps", 6)], writes=[("gcols", l)])
                P.op("dve", lambda e: e.scalar_tensor_tensor(out=gcols[:, l, 8:16], in0=PS[6][:, 8:16], scalar=1.0, op0=ALU.add,
                                                             in1=gpre_sb[:, l, :], op1=ALU.mult),
                     reads=[("ps", 6), "gpre"], writes=[("gcols", l)])
                P.op("dve", lambda e: e.tensor_tensor(out=gg_row[:], in0=mod_row[:, 2048:3072], in1=gg_row[:], op=ALU.mult),
                     reads=["mod_row", "gg_row"], writes=["gg_row"])
                for j in range(2):
                    P.op("pe", lambda e: e.matmul(PS[7][:, :], ones[0:1, 0:128], gg_row[:, j * 512:(j + 1) * 512],
                                                  start=True, stop=True), reads=["gg_row", "cst"], writes=[("ps", 7)])
                    P.op("act", lambda e: e.copy(GG[:, l, j * 512:(j + 1) * 512], PS[7][:, :]), reads=[("ps", 7)],
                         writes=[("GG", l)])

            return [lambda k=k: chunk(k) for k in range(8)], finish

        def phase_mod():
            AR.reset()
            stg = [AR.alloc([128, 3072], F32) for _ in range(3)]
            acc = AR.alloc([128, 3072], F32)
            mod_row = AR.alloc([1, 3072], F32)
            gg_row = AR.alloc([1, 1024], F32)
            chunks, fin = mod_parts(0, stg, acc, mod_row, gg_row, ("sp", "act"))
            for c_ in chunks:
                c_()
            fin()
            if debug:
                P.dma("sp", "dbg", dbg["mod"][:, 0:16], gcols[:, 0, :], reads=[("gcols", 0)])
                P.dma("sp", "dbg", dbg["mod"][:, 16:16 + 1024], GG[:, 0, :], reads=[("GG", 0)])

        def phase_h(l, xsrc, mod_layer=None):
            AR.reset()
            xt = [AR.alloc([128, D], F32) for _ in range(4)]
            xs = [AR.alloc([128, D], BF16) for _ in range(4)]
            junk = AR.alloc([128, D], BF16)
            mchunks, mfin = [], None
            if mod_layer is not None:
                AY = Arena(yT[:].rearrange("p a b -> p (a b)"), 9 * S)
                stg = [AY.alloc([128, 3072], F32) for _ in range(3)]
                acc = AR.alloc([128, 3072], F32)
                mod_row = AR.alloc([1, 3072], F32)
                gg_row = AR.alloc([1, 1024], F32)
                mchunks, mfin = mod_parts(mod_layer, stg, acc, mod_row, gg_row, ("pool",))
            ss = small[:, 0:16]
            sq = small[:, 16:32]
            rstd = small[:, 32:48]
            ev = 0
            for g in range(4):
                for c_ in mchunks[2 * g:2 * g + 2]:
                    c_()
                for j in range(4):
                    t = 4 * g + j
                    P.dma("sp", ("xt", j), xt[j][:], xsrc[t * 128:(t + 1) * 128, :], writes=[("xt", j)])
                    P.op("act", lambda e: e.activation(junk[:], xt[j][:], AF.Square, accum_out=ss[:, t:t + 1]),
                         reads=[("xt", j)], writes=["junk", ("ss", t)])
                    P.op("act", lambda e: e.activation(sq[:, t:t + 1], ss[:, t:t + 1], AF.Sqrt, bias=EPS, scale=1.0 / D),
                         reads=[("ss", t)], writes=[("sq", t)])
                    P.op("dve", lambda e: e.reciprocal(rstd[:, t:t + 1], sq[:, t:t + 1]), reads=[("sq", t)], writes=[("rstd", t)])
                    P.op("dve", lambda e: e.tensor_scalar(out=xs[j][:], in0=xt[j][:], scalar1=rstd[:, t:t + 1], scalar2=None,
                                                          op0=ALU.mult), reads=[("xt", j), ("rstd", t)], writes=[("xs", j)])
                for k in range(8):
                    b = k % 2
                    pt = psv(b, [128, 4, 128], BF16)
                    for j in range(4):
                        P.op("pe", lambda e: e.transpose(pt[:, j, :], xs[j][:, k * 128:(k + 1) * 128], ident[:]),
                             reads=[("xs", j), "cst"], writes=[("ps", b)], sig=(j == 3))
                    o_ap = hT[:, k, g * 512:(g + 1) * 512]
                    i_ap = psv(b, [128, 512], BF16)
                    if ev % 2 == 0:
                        P.op("dve", lambda e: e.tensor_scalar(out=o_ap, in0=i_ap, scalar1=gcols[:, l, 8 + k:9 + k],
                                                              scalar2=gcols[:, l, k:k + 1], op0=ALU.mult, op1=ALU.add),
                             reads=[("ps", b), ("gcols", l)], writes=[("hT", k, g)])
                    else:
                        P.op("act", lambda e: e.activation(o_ap, i_ap, AF.Identity, bias=gcols[:, l, k:k + 1],
                                                           scale=gcols[:, l, 8 + k:9 + k]),
                             reads=[("ps", b), ("gcols", l)], writes=[("hT", k, g)])
                    ev += 1
            if mfin is not None:
                mfin()

        HT_ALL = [("hT", k, g) for k in range(8) for g in range(4)]
        H_MOD_FIN = []

        def phase_m(l):
            AR.reset()
            VA = AR.alloc([128, NT, 4, 97], BF16)
            Us = [AR.alloc([96, S + 4], BF16) for _ in range(2)]
            DG = AR.alloc([96, 4, 4, 96], BF16)
            LI = AR.alloc([128, NT, 4], F32)
            XF = AR.alloc([128, NT, 4], F32)
            SP_ = AR.alloc([128, 64], F32)
            BP = AR.alloc([128, 64], F32)
            A_ = AR.alloc([128, 64], F32)
            C_ = AR.alloc([128, 64], F32)
            EG = AR.alloc([96, 64], F32)
            jk = AR.alloc([128, 96], BF16)
            HB = []
            for _ in range(2):
                HB.append(dict(
                    CU=AR.alloc([96, S], BF16),
                    SZ=AR.alloc([96, S], BF16), QT=AR.alloc([96, S], BF16), KT=AR.alloc([96, S], BF16),
                    KP=AR.alloc([128, NT, 96], BF16),
                    ST=[AR.alloc([128, 128], BF16) for _ in range(2)],
                    N2=[AR.alloc([128, 97], F32) for _ in range(2)],
                    HN=[AR.alloc([128, 96], BF16) for _ in range(2)],
                    TMP=[AR.alloc([96, 128], F32) for _ in range(2)],
                    Df=AR.alloc([96, 97], F32), Cb=AR.alloc([96, 97], BF16)))
            sm = small
            P.op("pool", lambda e: e.memset(VA[:, :, :, 96:97], 1.0), writes=["VA1"])
            for U in Us:
                P.op("pool", lambda e: e.memset(U[:, 0:4], 0.0), writes=["Upad"])
            for h in range(4):
                for j in range(4):
                    P.op("pool", lambda e: e.tensor_scalar(out=DG[:, h, j, :], in0=ident[0:96, 0:96], scalar1=convw_sb[:, l, h, j:j + 1],
                                                           scalar2=None, op0=ALU.mult), reads=["cst", "convw"], writes=["DG"])
            pg = psv(0, [128, NT, 8])
            for t in range(NT):
                for k in range(8):
                    P.op("pe", lambda e: e.matmul(pg[:, t, :], hT[:, k, t * 128:(t + 1) * 128], WA[:, k, 768:776],
                                                  start=(k == 0), stop=(k == 7)),
                         reads=[("WA", k), ("hT", k, t // 4)], writes=[("ps", 0)], sig=(t == NT - 1 and k == 7))
            P.op("dve", lambda e: e.tensor_tensor(out=LI[:], in0=pg[:, :, 0:4],
                                                  in1=bi_b[:, l, :].unsqueeze(1).to_broadcast([128, NT, 4]), op=ALU.add),
                 reads=[("ps", 0), "bib"], writes=["LI"])
            P.op("dve", lambda e: e.tensor_tensor(out=XF[:], in0=pg[:, :, 4:8],
                                                  in1=bf_b[:, l, :].unsqueeze(1).to_broadcast([128, NT, 4]), op=ALU.add),
                 reads=[("ps", 0), "bfb"], writes=["XF"])
            XF2 = XF.rearrange("p a b -> p (a b)")
            LI2 = LI.rearrange("p a b -> p (a b)")
            P.op("act", lambda e: e.activation(XF2, XF2, AF.Exp, scale=-1.0), reads=["XF"], writes=["XF"])
            P.op("act", lambda e: e.activation(SP_[:], XF2, AF.Ln, bias=1.0), reads=["XF"], writes=["SP"])
            P.op("pe", lambda e: e.matmul(PS[1][:, 0:64], triu[:], SP_[:], start=True, stop=True), reads=["SP", "cst"],
                 writes=[("ps", 1)])
            P.op("dve", lambda e: e.tensor_copy(BP[:], PS[1][:, 0:64]), reads=[("ps", 1)], writes=["BP"])
            P.op("act", lambda e: e.activation(A_[:], BP[:], AF.Exp, scale=-1.0), reads=["BP"], writes=["A"])
            P.op("dve", lambda e: e.tensor_tensor(out=C_[:], in0=LI2, in1=BP[:], op=ALU.add), reads=["LI", "BP"], writes=["C"])
            P.op("act", lambda e: e.activation(C_[:], C_[:], AF.Exp), reads=["C"], writes=["C"])
            P.op("pe", lambda e: e.matmul(PS[2][0:96, 0:64], sellast[:], BP[:], start=True, stop=True), reads=["BP", "cst"],
                 writes=[("ps", 2)])
            P.op("act", lambda e: e.activation(EG[:], PS[2][0:96, 0:64], AF.Exp, scale=-1.0), reads=[("ps", 2)], writes=["EG"])
            for t in range(NT):
                b = 3 + t % 2
                pv = psv(b, [128, 4, 96])
                for k in range(8):
                    P.op("pe", lambda e: e.matmul(PS[b][:, 0:384], hT[:, k, t * 128:(t + 1) * 128], WA[:, k, 384:768],
                                                  start=(k == 0), stop=(k == 7)),
                         reads=[("WA", k), ("hT", k, t // 4)], writes=[("ps", b)], sig=(k == 7))
                if t % 2 == 0:
                    P.op("dve", lambda e: e.tensor_copy(VA[:, t, :, 0:96], pv), reads=[("ps", b)], writes=[("VA", t)])
                else:
                    P.op("act", lambda e: e.copy(VA[:, t, :, 0:96], pv), reads=[("ps", b)], writes=[("VA", t)])
            bank = [5, 6]
            bi = [0]

            def nb_():
                b = bank[bi[0] % 2]
                bi[0] += 1
                return b

            def hp_u(h, s_):
                U = Us[s_]
                for tc in range(4):
                    b = nb_()
                    for k in range(8):
                        P.op("pe", lambda e: e.matmul(PS[b][0:96, :], WA[:, k, h * 96:(h + 1) * 96], hT[:, k, tc * 512:(tc + 1) * 512],
                                                      start=(k == 0), stop=(k == 7)),
                             reads=[("WA", k), ("hT", k, tc)], writes=[("ps", b)], sig=(k == 7))
                    P.op("act", lambda e: e.copy(U[:, 4 + tc * 512:4 + (tc + 1) * 512], PS[b][0:96, :]), reads=[("ps", b)],
                         writes=[("U", s_, tc)])

            def hp_z(h, s_):
                SZ = HB[s_]["SZ"]
                for tc in range(4):
                    b = nb_()
                    for k in range(8):
                        P.op("pe", lambda e: e.matmul(PS[b][0:96, :], WA[:, k, 776 + h * 96:776 + (h + 1) * 96],
                                                      hT[:, k, tc * 512:(tc + 1) * 512], start=(k == 0), stop=(k == 7)),
                             reads=[("WA", k), ("hT", k, tc)], writes=[("ps", b)], sig=(k == 7))
                    P.op("act", lambda e: e.activation(SZ[:, tc * 512:(tc + 1) * 512], PS[b][0:96, :], AF.Silu), reads=[("ps", b)],
                         writes=[("SZ", s_, tc)])

            def hp_conv(h, s_):
                U = Us[s_]
                CU = HB[s_]["CU"]
                cb_col = mvec_sb[:, l, 0, h:h + 1]
                UALL = [("U", s_, tc) for tc in range(4)] + ["Upad"]
                for tc in range(4):
                    b = nb_()
                    for j in range(4):
                        P.op("pe", lambda e: e.matmul(PS[b][0:96, :], DG[:, h, j, :], U[:, 1 + j + tc * 512:1 + j + (tc + 1) * 512],
                                                      start=(j == 0), stop=(j == 3)), reads=UALL + ["DG"], writes=[("ps", b)],
                             sig=(j == 3))
                    P.op("act", lambda e: e.activation(CU[:, tc * 512:(tc + 1) * 512], PS[b][0:96, :], AF.Silu, bias=cb_col),
                         reads=[("ps", b), "mvec"], writes=[("CU", s_)])

            def hp_qk(h, s_):
                B = HB[s_]
                CU, QT, KT, KP = B["CU"], B["QT"], B["KT"], B["KP"]
                skip_col = mvec_sb[:, l, 2, h:h + 1]
                for which, dst, wt in ((0, QT, WQ), (1, KT, WK)):
                    for tc in range(4):
                        b = nb_()
                        P.op("pe", lambda e: e.matmul(PS[b][0:96, :], wt[:, l, h, :], CU[:, tc * 512:(tc + 1) * 512], start=True,
                                                      stop=True), reads=[("CU", s_), "WQ", "WK"], writes=[("ps", b)])
                        if which == 0:
                            P.op("act", lambda e: e.activation(dst[:, tc * 512:(tc + 1) * 512], PS[b][0:96, :], AF.Copy,
                                                               scale=96.0 ** -0.5), reads=[("ps", b)], writes=[("QT", s_, tc)])
                        else:
                            P.op("act", lambda e: e.copy(dst[:, tc * 512:(tc + 1) * 512], PS[b][0:96, :]),
                                 reads=[("ps", b)], writes=[("KT", s_, tc)])
                for t4 in range(4):
                    b = nb_()
                    pk = psv(b, [128, 4, 96])
                    for j in range(4):
                        t = 4 * t4 + j
                        P.op("pe", lambda e: e.matmul(pk[:, j, :], CU[:, t * 128:(t + 1) * 128], WK[:, l, h, :], start=True,
                                                      stop=True), reads=[("CU", s_), "WK"], writes=[("ps", b)], sig=(j == 3))
                    cview = C_.rearrange("p (a b) -> p a b", b=4)[:, 4 * t4:4 * t4 + 4, h:h + 1].to_broadcast([128, 4, 96])
                    P.op("dve", lambda e: e.tensor_tensor(out=KP[:, 4 * t4:4 * t4 + 4, :], in0=pk, in1=cview, op=ALU.mult),
                         reads=[("ps", b), "C"], writes=[("KP", s_, t4)])
                P.op("act", lambda e: e.activation(CU[:], CU[:], AF.Copy, scale=skip_col), reads=[("CU", s_), "mvec"],
                     writes=[("CU", s_)])

            for pair in range(2):
                heads = (2 * pair, 2 * pair + 1)
                for stage in (hp_u, hp_z, hp_conv, hp_qk):
                    for s_, h in enumerate(heads):
                        stage(h, s_)
                PB1 = (0, 1)
                PBN = (2, 3)
                PBC = (5, 6)
                PBT = (7, 4)
                def o_(s_, c):
                    return 64 + 16 * s_ + 8 * (c % 2)

                def t_abs(c):
                    i2 = c % 2
                    for s_, h in enumerate(heads):
                        B = HB[s_]
                        o = o_(s_, c)
                        P.op("act", lambda e: e.activation(sm[:, o:o + 1], B["N2"][i2][:, 96:97], AF.Abs), reads=[("N2", s_, i2)],
                             writes=[("sm", s_, i2, 0)])

                def t_1(c):
                    i2 = c % 2
                    for s_, h in enumerate(heads):
                        o = o_(s_, c)
                        P.op("dve", lambda e: e.tensor_scalar(out=sm[:, o + 1:o + 2], in0=sm[:, o:o + 1], scalar1=1.0, scalar2=None,
                                                              op0=ALU.max), reads=[("sm", s_, i2, 0)], writes=[("sm", s_, i2, 1)])
                    for s_, h in enumerate(heads):
                        o = o_(s_, c)
                        P.op("dve", lambda e: e.reciprocal(sm[:, o + 2:o + 3], sm[:, o + 1:o + 2]), reads=[("sm", s_, i2, 1)],
                             writes=[("sm", s_, i2, 2)])

                def t_act2(c):
                    i2 = c % 2
                    for s_, h in enumerate(heads):
                        B = HB[s_]
                        o = o_(s_, c)
                        P.op("act", lambda e: e.activation(jk[:], B["N2"][i2][:, 0:96], AF.Square, scale=sm[:, o + 2:o + 3],
                                                           accum_out=sm[:, o + 3:o + 4]), reads=[("N2", s_, i2), ("sm", s_, i2, 2)],
                             writes=["jk", ("sm", s_, i2, 3)])
                    for s_, h in enumerate(heads):
                        o = o_(s_, c)
                        P.op("act", lambda e: e.activation(sm[:, o + 4:o + 5], sm[:, o + 3:o + 4], AF.Sqrt, bias=EPS, scale=1.0 / 96),
                             reads=[("sm", s_, i2, 3)], writes=[("sm", s_, i2, 4)])

                def t_2(c):
                    i2 = c % 2
                    for s_, h in enumerate(heads):
                        o = o_(s_, c)
                        P.op("dve", lambda e: e.reciprocal(sm[:, o + 5:o + 6], sm[:, o + 4:o + 5]), reads=[("sm", s_, i2, 4)],
                             writes=[("sm", s_, i2, 5)])
                    for s_, h in enumerate(heads):
                        o = o_(s_, c)
                        P.op("dve", lambda e: e.tensor_tensor(out=sm[:, o + 6:o + 7], in0=sm[:, o + 5:o + 6], in1=sm[:, o + 2:o + 3],
                                                              op=ALU.mult), reads=[("sm", s_, i2, 5), ("sm", s_, i2, 2)],
                             writes=[("sm", s_, i2, 6)])
                    for s_, h in enumerate(heads):
                        B = HB[s_]
                        o = o_(s_, c)
                        P.op("dve", lambda e: e.tensor_scalar(out=B["HN"][i2][:], in0=B["N2"][i2][:, 0:96], scalar1=sm[:, o + 6:o + 7],
                                                              scalar2=None, op0=ALU.mult), reads=[("N2", s_, i2), ("sm", s_, i2, 6)],
                             writes=[("HN", s_, i2)])

                def t_tr(c):
                    i2 = c % 2
                    for s_, h in enumerate(heads):
                        B = HB[s_]
                        ptr = psv(PBT[s_], [96, 128], BF16)
                        P.op("pe", lambda e: e.transpose(ptr, B["HN"][i2][:], ident[:]), reads=[("HN", s_, i2), "cst"],
                             writes=[("ps", PBT[s_])])

                def t_3(c):
                    i2 = c % 2
                    cs = slice(c * 128, (c + 1) * 128)
                    for s_, h in enumerate(heads):
                        B = HB[s_]
                        ptr = psv(PBT[s_], [96, 128], BF16)
                        g_col = mvec_sb[:, l, 1, h:h + 1]
                        P.op("dve", lambda e: e.scalar_tensor_tensor(out=B["TMP"][i2][:], in0=ptr, scalar=g_col, op0=ALU.mult,
                                                                     in1=B["CU"][:, cs], op1=ALU.add),
                             reads=[("ps", PBT[s_]), "mvec", ("CU", s_)], writes=[("TMP", s_, i2)])
                        P.op("pool", lambda e: e.tensor_tensor(out=yT[0:96, h, cs], in0=B["TMP"][i2][:], in1=B["SZ"][:, cs], op=ALU.mult),
                             reads=[("TMP", s_, i2), ("SZ", s_, c // 4)], writes=[("yT", h, c)])

                def mm1(c):
                    cs = slice(c * 128, (c + 1) * 128)
                    tq = c // 4
                    for s_, h in enumerate(heads):
                        B = HB[s_]
                        P.op("pe", lambda e: e.matmul(PS[PB1[s_]][:, 0:128], B["KT"][:, cs], B["QT"][:, cs], start=True, stop=True),
                             reads=[("KT", s_, tq), ("QT", s_, tq)], writes=[("ps", PB1[s_])])

                mm1(0)
                for c in range(NT):
                    cs = slice(c * 128, (c + 1) * 128)
                    tq = c // 4
                    i2 = c % 2
                    if c > 0:
                        t_abs(c - 1)
                    for s_, h in enumerate(heads):
                        B = HB[s_]
                        col = c * 4 + h
                        P.op("dve", lambda e: e.scalar_tensor_tensor(out=B["ST"][i2][:], in0=PS[PB1[s_]][:, 0:128],
                                                                     scalar=C_[:, col:col + 1], op0=ALU.mult, in1=triu[:],
                                                                     op1=ALU.mult),
                             reads=[("ps", PB1[s_]), "C", "cst"], writes=[("ST", s_, i2)])
                    if c + 1 < NT:
                        mm1(c + 1)
                    for s_, h in enumerate(heads):
                        B = HB[s_]
                        P.op("pe", lambda e: e.matmul(PS[PBN[s_]][:, 0:97], B["ST"][i2][:], VA[:, c, h, :], start=True, stop=(c == 0)),
                             reads=[("ST", s_, i2), ("VA", c), "VA1"], writes=[("ps", PBN[s_])], sig=(c == 0))
                        if c > 0:
                            P.op("pe", lambda e: e.matmul(PS[PBN[s_]][:, 0:97], B["QT"][:, cs], B["Cb"][:], start=False, stop=True),
                                 reads=[("QT", s_, tq), ("Cb", s_)], writes=[("ps", PBN[s_])])
                        if c < NT - 1:
                            P.op("pe", lambda e: e.matmul(PS[PBC[s_]][0:96, 0:97], B["KP"][:, c, :], VA[:, c, h, :], start=True,
                                                          stop=True), reads=[("KP", s_, c // 4), ("VA", c), "VA1"],
                                 writes=[("ps", PBC[s_])])
                    if c > 0:
                        t_1(c - 1)
                        t_act2(c - 1)
                    if c < NT - 1:
                        for s_, h in enumerate(heads):
                            B = HB[s_]
                            col = c * 4 + h
                            if c == 0:
                                P.op("dve", lambda e: e.tensor_copy(B["Df"][:], PS[PBC[s_]][0:96, 0:97]), reads=[("ps", PBC[s_])],
                                     writes=[("Df", s_)])
                            else:
                                pcol = (c - 1) * 4 + h
                                P.op("dve", lambda e: e.scalar_tensor_tensor(out=B["Df"][:], in0=B["Df"][:],
                                                                             scalar=EG[:, pcol:pcol + 1], op0=ALU.mult,
                                                                             in1=PS[PBC[s_]][0:96, 0:97], op1=ALU.add),
                                     reads=[("ps", PBC[s_]), ("Df", s_), "EG"], writes=[("Df", s_)])
                            P.op("act", lambda e: e.activation(B["Cb"][:], B["Df"][:], AF.Copy, scale=EG[:, col:col + 1]),
                                 reads=[("Df", s_), "EG"], writes=[("Cb", s_)])
                    if c > 0:
                        t_2(c - 1)
                        t_tr(c - 1)
                    for s_, h in enumerate(heads):
                        B = HB[s_]
                        col = c * 4 + h
                        P.op("dve", lambda e: e.tensor_scalar(out=B["N2"][i2][:], in0=PS[PBN[s_]][:, 0:97], scalar1=A_[:, col:col + 1],
                                                              scalar2=None, op0=ALU.mult), reads=[("ps", PBN[s_]), "A"],
                             writes=[("N2", s_, i2)])
                    if c > 0:
                        t_3(c - 1)
                c = NT - 1
                t_abs(c)
                t_1(c)
                t_act2(c)
                t_2(c)
                t_tr(c)
                t_3(c)

        def phase_a(l):
            AR.reset()
            QA = AR.alloc([128, 3, S], BF16)
            KA = AR.alloc([128, S], BF16)
            VSa = AR.alloc([128, NT, 128], BF16)
            ZSa = AR.alloc([128, NT, 384], BF16)
            SC6s = [AR.alloc([128, 6, 256], F32) for _ in range(2)]
            E6s = [AR.alloc([128, 6, 256], BF16) for _ in range(2)]
            ETs = AR.alloc([128, 6, 2, 128], BF16)
            YA0 = AR.alloc([128, 384], F32)
            YA = AR.alloc([128, 384], BF16)
            sm = small
            for t in range(NT):
                b = t % 2
                for k in range(8):
                    P.op("pe", lambda e: e.matmul(PS[b][:, :], hT[:, k, t * 128:(t + 1) * 128], WB[:, k, 512:1024],
                                                  start=(k == 0), stop=(k == 7)),
                         reads=[("WBi", k), ("hT", k, t // 4)], writes=[("ps", b)], sig=(k == 7))
                P.op("dve", lambda e: e.tensor_copy(VSa[:, t, :], PS[b][:, 0:128]), reads=[("ps", b)], writes=[("VSa", t)])
                P.op("act", lambda e: e.activation(ZSa[:, t, :], PS[b][:, 128:512], AF.Silu), reads=[("ps", b)], writes=[("ZSa", t)])
            bi = 0
            for gg in range(4):
                for tc in range(4):
                    b = 2 + bi % 2
                    bi += 1
                    if gg < 3:
                        for half, hq in enumerate((gg, gg + 3)):
                            for k in range(8):
                                P.op("pe", lambda e: e.matmul(PS[b][64 * half:64 * half + 64, :], WB[:, k, hq * 64:(hq + 1) * 64],
                                                              hT[:, k, tc * 512:(tc + 1) * 512], start=(k == 0), stop=(k == 7)),
                                     reads=[("WBi", k), ("hT", k, tc)], writes=[("ps", b)], sig=(half == 1 and k == 7))
                        P.op("act", lambda e: e.activation(QA[:, gg, tc * 512:(tc + 1) * 512], PS[b][:, :], AF.Copy, scale=0.125),
                             reads=[("ps", b)], writes=[("QA", tc)])
                    else:
                        for k in range(8):
                            P.op("pe", lambda e: e.matmul(PS[b][:, :], WB[:, k, 384:512], hT[:, k, tc * 512:(tc + 1) * 512],
                                                          start=(k == 0), stop=(k == 7)),
                                 reads=[("WBi", k), ("hT", k, tc)], writes=[("ps", b)], sig=(k == 7))
                        P.op("dve", lambda e: e.tensor_copy(KA[:, tc * 512:(tc + 1) * 512], PS[b][:, :]),
                             reads=[("ps", b)], writes=[("KA", tc)])
            def cols(n):
                o = 96 + 48 * (n % 2)
                return (sm[:, o:o + 6], sm[:, o + 8:o + 14], sm[:, o + 16:o + 22], sm[:, o + 24:o + 30], sm[:, o + 32:o + 38])

            def geom(n):
                k0 = max(0, n - 1) * 128
                W = (n + 1) * 128 - k0
                return k0, W, 256 - W, W // 128

            def f_pair(n, hp):
                pr = n % 2
                k0, W, boff, nb = geom(n)
                M6, NM6, ES6, RS6, RD6 = cols(n)
                SC6 = SC6s[pr]
                for h in (2 * hp, 2 * hp + 1):
                    kv = h // 3
                    b = 4 + h % 2
                    P.op("pe", lambda e: e.matmul(PS[b][:, 0:W], QA[64 * kv:64 * kv + 64, h % 3, n * 128:(n + 1) * 128],
                                                  KA[64 * kv:64 * kv + 64, k0:k0 + W], start=True, stop=True), reads=[], writes=[("ps", b)])
                    P.op("dve", lambda e: e.tensor_tensor(out=SC6[:, h, 0:W], in0=PS[b][:, 0:W], in1=bias6[:, h, boff:256], op=ALU.add),
                         reads=[("ps", b), "cst"], writes=[("SC6", pr, h)])
                    P.op("dve", lambda e: e.reduce_max(out=M6[:, h:h + 1], in_=SC6[:, h, 0:W], axis=AX.X), reads=[("SC6", pr, h)],
                         writes=[("M6", pr)])

            def f_tail(n, hp):
                pr = n % 2
                k0, W, boff, nb = geom(n)
                M6, NM6, ES6, RS6, RD6 = cols(n)
                SC6, E6 = SC6s[pr], E6s[pr]
                hs = slice(2 * hp, 2 * hp + 2)
                P.op("dve", lambda e: e.tensor_tensor(out=M6[:, hs], in0=M6[:, hs], in1=sinks_b[:, l, hs], op=ALU.max),
                     reads=[("M6", pr), "sinksb"], writes=[("M6", pr)])
                P.op("dve", lambda e: e.tensor_scalar(out=NM6[:, hs], in0=M6[:, hs], scalar1=-1.0, scalar2=None, op0=ALU.mult),
                     reads=[("M6", pr)], writes=[("NM6", pr, hp)])
                P.op("dve", lambda e: e.tensor_tensor(out=ES6[:, hs], in0=NM6[:, hs], in1=sinks_b[:, l, hs], op=ALU.add),
                     reads=[("NM6", pr, hp), "sinksb"], writes=[("ES6", pr, hp)])
                for h in (2 * hp, 2 * hp + 1):
                    P.op("act", lambda e: e.activation(E6[:, h, 0:W], SC6[:, h, 0:W], AF.Exp, bias=NM6[:, h:h + 1],
                                                       accum_out=RS6[:, h:h + 1]), reads=[("SC6", pr, h), ("NM6", pr, hp)],
                         writes=[("E6", pr, h), ("RS6", pr, h)])
                P.op("act", lambda e: e.activation(ES6[:, hs], ES6[:, hs], AF.Exp), reads=[("ES6", pr, hp)], writes=[("ES6", pr, hp)])

            def b_tr(n):
                pr = n % 2
                k0, W, boff, nb = geom(n)
                E6 = E6s[pr]
                for hg in range(2):
                    b = 6 + hg
                    pt = psv(b, [128, 3, 2, 128], BF16)
                    for hh in range(3):
                        h = hg * 3 + hh
                        for j in range(nb):
                            P.op("pe", lambda e: e.transpose(pt[:, hh, j, :], E6[:, h, j * 128:(j + 1) * 128], ident[:]),
                                 reads=[("E6", pr, h), "cst"], writes=[("ps", b)], sig=(hh == 2 and j == nb - 1))

            def b_ev(n):
                pr = n % 2
                k0, W, boff, nb = geom(n)
                M6, NM6, ES6, RS6, RD6 = cols(n)
                pt0 = psv(6, [128, 3, 2, 128], BF16)
                pt1 = psv(7, [128, 3, 2, 128], BF16)
                P.op("dve", lambda e: e.tensor_copy(ETs[:, 0:3, 0:nb, :], pt0[:, :, 0:nb, :]), reads=[("ps", 6)], writes=[("ETs", 0)])
                P.op("act", lambda e: e.copy(ETs[:, 3:6, 0:nb, :], pt1[:, :, 0:nb, :]), reads=[("ps", 7)], writes=[("ETs", 1)])
                P.op("dve", lambda e: e.tensor_tensor(out=RD6, in0=RS6, in1=ES6, op=ALU.add),
                     reads=[("RS6", pr, h) for h in range(6)] + [("ES6", pr, hp) for hp in range(3)], writes=[("RD6", pr)])
                P.op("dve", lambda e: e.reciprocal(RD6, RD6), reads=[("RD6", pr)], writes=[("RD6", pr)])

            def b_pv(n):
                k0, W, boff, nb = geom(n)
                po = psv(n % 2, [128, 6, 64])
                pb = n % 2
                for h in range(6):
                    kv = h // 3
                    for j in range(nb):
                        kb = k0 // 128 + j
                        P.op("pe", lambda e: e.matmul(po[:, h, :], ETs[:, h, j, :], VSa[:, kb, kv * 64:(kv + 1) * 64], start=(j == 0),
                                                      stop=(j == nb - 1)), reads=[("ETs", h // 3), ("VSa", kb)], writes=[("ps", pb)],
                             sig=(h == 5 and j == nb - 1))

            def b_ya(n):
                pr = n % 2
                M6, NM6, ES6, RS6, RD6 = cols(n)
                po = psv(n % 2, [128, 6, 64])
                pb = n % 2
                P.op("dve", lambda e: e.tensor_tensor(out=YA0.rearrange("p (a b) -> p a b", b=64), in0=po,
                                                      in1=RD6.unsqueeze(2).to_broadcast([128, 6, 64]), op=ALU.mult),
                     reads=[("ps", pb), ("RD6", pr)], writes=["YA0"])
                P.op("pool", lambda e: e.tensor_tensor(out=YA[:], in0=YA0[:], in1=ZSa[:, n, :], op=ALU.mult), reads=["YA0", ("ZSa", n)],
                     writes=["YA"])

            def b_out(n):
                b = 2 + n % 2
                pt = psv(b, [128, 3, 128], BF16)
                for j in range(3):
                    P.op("pe", lambda e: e.transpose(pt[:, j, :], YA[:, j * 128:(j + 1) * 128], ident[:]), reads=["YA", "cst"],
                         writes=[("ps", b)], sig=(j == 2))
                P.op("act", lambda e: e.copy(yT[:, 4:7, n * 128:(n + 1) * 128], pt), reads=[("ps", b)], writes=[("yTa", n)])

            for hp in range(3):
                f_pair(0, hp)
                f_tail(0, hp)
            for n in range(NT):
                nxt = n + 1 < NT
                if nxt:
                    f_pair(n + 1, 0)
                    f_tail(n + 1, 0)
                b_tr(n)
                b_ev(n)
                b_pv(n)
                if nxt:
                    f_pair(n + 1, 1)
                    f_tail(n + 1, 1)
                b_ya(n)
                if nxt:
                    f_pair(n + 1, 2)
                    f_tail(n + 1, 2)
                b_out(n)

        def phase_s(l):
            AR.reset()
            QS = AR.alloc([128, 2, S], BF16)
            KS = AR.alloc([128, 2, S], BF16)
            VS = AR.alloc([128, NT, 256], BF16)
            ZS = AR.alloc([128, NT, 256], BF16)
            YS = AR.alloc([128, 256], BF16)
            for t in range(NT):
                b = t % 2
                for k in range(8):
                    P.op("pe", lambda e: e.matmul(PS[b][:, :], hT[:, k, t * 128:(t + 1) * 128], WA[:, k, 512:1024],
                                                  start=(k == 0), stop=(k == 7)),
                         reads=[("WA", k), ("hT", k, t // 4)], writes=[("ps", b)], sig=(k == 7))
                P.op("dve", lambda e: e.tensor_copy(VS[:, t, :], PS[b][:, 0:256]), reads=[("ps", b)], writes=[("VS", t)])
                P.op("act", lambda e: e.activation(ZS[:, t, :], PS[b][:, 256:512], AF.Silu), reads=[("ps", b)], writes=[("ZS", t)])
            bi = 0
            for gg in range(4):
                for tc in range(4):
                    b = 2 + bi % 2
                    bi += 1
                    for k in range(8):
                        P.op("pe", lambda e: e.matmul(PS[b][:, :], WA[:, k, gg * 128:(gg + 1) * 128], hT[:, k, tc * 512:(tc + 1) * 512],
                                                      start=(k == 0), stop=(k == 7)),
                             reads=[("WA", k), ("hT", k, tc)], writes=[("ps", b)], sig=(k == 7))
                    if gg < 2:
                        P.op("act", lambda e: e.activation(QS[:, gg, tc * 512:(tc + 1) * 512], PS[b][:, :], AF.Copy, scale=0.125),
                             reads=[("ps", b)], writes=[("QS", tc)])
                    else:
                        P.op("dve", lambda e: e.tensor_copy(KS[:, gg - 2, tc * 512:(tc + 1) * 512], PS[b][:, :]),
                             reads=[("ps", b)], writes=[("KS", tc)])
            P.barrier()
            AH = Arena(hT[:].rearrange("p a b -> p (a b)"), 8 * S)
            Kh = [[AH.alloc([128, 512], F32) for _ in range(4)] for _ in range(2)]
            Qr = [q_[:] for q_ in Qr_t]
            Qh = Qr
            PEH = (3,)
            Ah = [AH.alloc([128, 512], BF16) for _ in range(4)]
            ATh = [AH.alloc([128, 4, 128], BF16) for _ in range(4)]
            cnt = {"z": 0, "t": 0}
            its = []
            for qb in range(NT):
                Wq = (qb + 1) * 128
                hi = Wq
                si = 0
                while hi > 0:
                    lo = max(0, hi - 512)
                    its.append((qb, si, lo, hi, lo == 0))
                    hi = lo
                    si += 1

            def stage1(i):
                qb, si, lo, hi, last_seg = its[i]
                pr = i % 2
                w = hi - lo
                for h in range(4):
                    zb = cnt["z"] % 3
                    cnt["z"] += 1
                    hp0 = 64 * (h % 2)
                    P.op("pe", lambda e: e.matmul(PS[zb][:, 0:w], QS[hp0:hp0 + 64, h // 2, qb * 128:(qb + 1) * 128],
                                                  KS[hp0:hp0 + 64, h // 2, lo:hi], start=True, stop=(si != 0)),
                         reads=[], writes=[("ps", zb)], sig=(si != 0))
                    if si == 0:
                        P.op("pe", lambda e: e.matmul(PS[zb][:, w - 128:w], ident[:], negm[:], start=False, stop=True),
                             reads=["cst"], writes=[("ps", zb)])
                    P.op("act", lambda e: e.activation(Kh[pr][h][:, 0:w], PS[zb][:, 0:w], AF.Sigmoid, scale=-1.0), reads=[("ps", zb)],
                         writes=[("K", pr, h)])

            def stage234(i):
                qb, si, lo, hi, last_seg = its[i]
                pr = i % 2
                w = hi - lo
                nbk = w // 128
                pob = 5 + qb % 2
                po = psv(pob, [128, 4, 64])
                for h in range(4):
                    Q = Qr[h]
                    if si == 0:
                        P.op("pool", lambda e: e.tensor_copy(Q[:, w:w + 1], ones[:, 0:1]), reads=[("Q", h), "cst"], writes=[("Qc", h)])
                    else:
                        P.op("pool", lambda e: e.tensor_copy(Q[:, w:w + 1], Qh[h][:, 0:1]), reads=[("Q", h)], writes=[("Qc", h)])
                for h in range(4):
                    Q = Qr[h]
                    P.op("dve", lambda e: e.tensor_tensor_scan(out=Q[:, 0:w][:, ::-1], data0=Kh[pr][h][:, 0:w][:, ::-1],
                                                               data1=ones[:, 0:w], initial=Qh[h][:, w:w + 1], op0=ALU.mult,
                                                               op1=ALU.mult), reads=[("K", pr, h), "cst", ("Qc", h)],
                         writes=[("Q", h)])
                for h in range(4):
                    if h not in PEH:
                        Q = Qr[h]
                        P.op("dve", lambda e: e.tensor_tensor(out=Ah[h][:, 0:w], in0=Q[:, 1:w + 1], in1=Q[:, 0:w], op=ALU.subtract),
                             reads=[("Q", h), ("Qc", h)], writes=[("A", h)])

                def tr(h):
                    tb = 3 + h % 2
                    if h in PEH:
                        pt = psv(tb, [128, 4, 128])
                        Q = Qr[h]
                        for j in range(nbk):
                            P.op("pe", lambda e: e.matmul(pt[:, j, :], Q[:, 1 + j * 128:1 + (j + 1) * 128], idp_r[:], start=True,
                                                          stop=False), reads=[("Q", h), ("Qc", h), "cst"], writes=[("ps", tb)], sig=False)
                            P.op("pe", lambda e: e.matmul(pt[:, j, :], Q[:, j * 128:(j + 1) * 128], idn_r[:], start=False, stop=True),
                                 reads=[("Q", h), "cst"], writes=[("ps", tb)], sig=(j == nbk - 1))
                    else:
                        pt = psv(tb, [128, 4, 128], BF16)
                        for j in range(nbk):
                            P.op("pe", lambda e: e.transpose(pt[:, j, :], Ah[h][:, j * 128:(j + 1) * 128], ident[:]),
                                 reads=[("A", h), "cst"], writes=[("ps", tb)], sig=(j == nbk - 1))

                def ev(h):
                    tb = 3 + h % 2
                    pt = psv(tb, [128, 4, 128]) if h in PEH else psv(tb, [128, 4, 128], BF16)
                    P.op("act", lambda e: e.copy(ATh[h][:, 0:nbk, :], pt[:, 0:nbk, :]), reads=[("ps", tb)], writes=[("AT", h)])

                def pv(h):
                    for j in range(nbk):
                        kb = lo // 128 + j
                        first_pv = (si == 0 and h == 3 and j == 0)
                        P.op("pe", lambda e: e.matmul(po[:, h, :], ATh[h][:, j, :], VS[:, kb, h * 64:(h + 1) * 64], start=first_pv,
                                                      stop=(last_seg and j == nbk - 1), skip_group_check=True),
                             reads=[("AT", h), ("VS", kb)], writes=[("ps", pob)], sig=(j == nbk - 1))

                tr(3); tr(0); ev(3); ev(0); tr(1); tr(2); pv(3); pv(0); ev(1); ev(2); pv(1); pv(2)
                return qb if last_seg else None

            def finalize(qb):
                pob = 5 + qb % 2
                P.op("dve", lambda e: e.tensor_tensor(out=YS[:], in0=psv(pob, [128, 256]), in1=ZS[:, qb, :], op=ALU.mult),
                     reads=[("ps", pob), ("ZS", qb)], writes=["YS"])
                pt = psv(7, [128, 2, 128], BF16)
                for j in range(2):
                    P.op("pe", lambda e: e.transpose(pt[:, j, :], YS[:, j * 128:(j + 1) * 128], ident[:]), reads=["YS", "cst"],
                         writes=[("ps", 7)], sig=(j == 1))
                P.op("act", lambda e: e.copy(yT[:, 7:9, qb * 128:(qb + 1) * 128], pt), reads=[("ps", 7)], writes=[("yTs", qb)])

            stage1(0)
            pend = None
            for i in range(len(its)):
                if i + 1 < len(its):
                    stage1(i + 1)
                done = stage234(i)
                if pend is not None:
                    finalize(pend)
                pend = done
            if pend is not None:
                finalize(pend)

        def phase_o(l, xsrc, xdst, fuse_next_h):
            AR.reset()
            XR = [AR.alloc([128, D], F32) for _ in range(2)]
            XO = [AR.alloc([128, D], F32) for _ in range(2)]
            jk = AR.alloc([128, D], BF16)
            xs = [AR.alloc([128, D], BF16) for _ in range(8)]
            sm = small
            evc = [0]

            def ldx(t):
                i2 = t % 2
                P.dma("sp", ("XR", i2), XR[i2][:], xsrc[t * 128:(t + 1) * 128, :], writes=[("XR", i2)])

            def mm(t):
                i2 = t % 2
                ts_ = slice(t * 128, (t + 1) * 128)
                o = 160 + 8 * i2
                for half in range(2):
                    b = 2 * i2 + half
                    for c in range(9):
                        kc = 96 if c < 4 else 128
                        P.op("pe", lambda e: e.matmul(PS[b][:, :], yT[0:kc, c, ts_], WB[0:kc, c, half * 512:(half + 1) * 512],
                                                      start=(c == 0), stop=(c == 8)), reads=[("WB", c)], writes=[("ps", b)],
                             sig=(c == 8))
                    P.op("act", lambda e: e.activation(jk[:, 0:512], PS[b][:, :], AF.Square, accum_out=sm[:, o + half:o + half + 1]),
                         reads=[("ps", b)], writes=["jko", ("smo", i2, half)])

            def chain(t):
                i2 = t % 2
                ts_ = slice(t * 128, (t + 1) * 128)
                o = 160 + 8 * i2
                P.op("dve", lambda e: e.tensor_tensor(out=sm[:, o + 2:o + 3], in0=sm[:, o:o + 1], in1=sm[:, o + 1:o + 2], op=ALU.add),
                     reads=[("smo", i2, 0), ("smo", i2, 1)], writes=[("smo", i2, 2)])
                P.op("act", lambda e: e.activation(sm[:, o + 3:o + 4], sm[:, o + 2:o + 3], AF.Sqrt, bias=EPS, scale=1.0 / D),
                     reads=[("smo", i2, 2)], writes=[("smo", i2, 3)])
                P.op("dve", lambda e: e.reciprocal(sm[:, o + 4:o + 5], sm[:, o + 3:o + 4]), reads=[("smo", i2, 3)],
                     writes=[("smo", i2, 4)])
                for half in range(2):
                    b = 2 * i2 + half
                    hs = slice(half * 512, (half + 1) * 512)
                    P.op("dve", lambda e: e.scalar_tensor_tensor(out=XO[i2][:, hs], in0=PS[b][:, :], scalar=sm[:, o + 4:o + 5],
                                                                 op0=ALU.mult, in1=GG[:, l, hs], op1=ALU.mult),
                         reads=[("ps", b), ("smo", i2, 4), ("GG", l)], writes=[("XO", i2, half)])
                    P.op("dve", lambda e: e.tensor_tensor(out=XO[i2][:, hs], in0=XO[i2][:, hs], in1=XR[i2][:, hs], op=ALU.add),
                         reads=[("XO", i2, half), ("XR", i2)], writes=[("XO", i2, half)])
                P.dma("pool", ("XO", i2), xdst[ts_, :], XO[i2][:], reads=[("XO", i2, 0), ("XO", i2, 1)], writes=[("xdst", t)])

            def hnorm(t):
                i2 = t % 2
                j = t % 8
                o2 = 208 + 4 * j
                P.op("act", lambda e: e.activation(jk[:], XO[i2][:], AF.Square, accum_out=sm[:, o2:o2 + 1]),
                     reads=[("XO", i2, 0), ("XO", i2, 1)], writes=["jko", ("hss", j)])
                P.op("act", lambda e: e.activation(sm[:, o2 + 1:o2 + 2], sm[:, o2:o2 + 1], AF.Sqrt, bias=EPS, scale=1.0 / D),
                     reads=[("hss", j)], writes=[("hsq", j)])
                P.op("dve", lambda e: e.reciprocal(sm[:, o2 + 2:o2 + 3], sm[:, o2 + 1:o2 + 2]), reads=[("hsq", j)],
                     writes=[("hrs", j)])
                P.op("act", lambda e: e.activation(xs[j][:], XO[i2][:], AF.Copy, scale=sm[:, o2 + 2:o2 + 3]),
                     reads=[("XO", i2, 0), ("XO", i2, 1), ("hrs", j)], writes=[("xs", j)])

            def htrans(g):
                ln = l + 1
                for k in range(8):
                    b = 4 + k % 2
                    pt = psv(b, [128, 4, 128], BF16)
                    for jj in range(4):
                        xj = (4 * g + jj) % 8
                        P.op("pe", lambda e: e.transpose(pt[:, jj, :], xs[xj][:, k * 128:(k + 1) * 128], ident[:]),
                             reads=[("xs", xj), "cst"], writes=[("ps", b)], sig=(jj == 3))
                    o_ap = hT[:, k, g * 512:(g + 1) * 512]
                    i_ap = psv(b, [128, 512], BF16)
                    if evc[0] % 2 == 0:
                        P.op("dve", lambda e: e.tensor_scalar(out=o_ap, in0=i_ap, scalar1=gcols[:, ln, 8 + k:9 + k],
                                                              scalar2=gcols[:, ln, k:k + 1], op0=ALU.mult, op1=ALU.add),
                             reads=[("ps", b), ("gcols", ln)], writes=[("hT", k, g)])
                    else:
                        P.op("act", lambda e: e.activation(o_ap, i_ap, AF.Identity, bias=gcols[:, ln, k:k + 1],
                                                           scale=gcols[:, ln, 8 + k:9 + k]),
                             reads=[("ps", b), ("gcols", ln)], writes=[("hT", k, g)])
                    evc[0] += 1

            ldx(0)
            for t in range(NT):
                if t + 1 < NT:
                    ldx(t + 1)
                mm(t)
                if fuse_next_h:
                    if t > 0:
                        hnorm(t - 1)
                    if t >= 5 and (t - 5) % 4 == 0:
                        htrans((t - 5) // 4)
                chain(t)
            if fuse_next_h:
                hnorm(NT - 1)
                htrans(3)

        def schedule():
            load_w_in(WA, "WA", 0, 0, 1160)
            phase_mod()
            P.barrier()
            if stop_after == "mod":
                return
            for l in range(n_layers):
                xsrc = x_in if l == 0 else x1
                xdst = out if l == n_layers - 1 else x1
                if l == 0:
                    phase_h(l, xsrc, 1 if n_layers > 1 else None)
                load_w_in(WB, "WBi", l, 1160, 1024)
                P.barrier()
                if debug and l == 0:
                    P.dma("sp", "dbg", dbg["hT"], hT[:], reads=HT_ALL)
                if stop_after == "h":
                    return
                phase_m(l)
                P.barrier()
                if stop_after == "m":
                    break
                load_w_in(WA, "WA", l, 2184, 1024)
                phase_a(l)
                P.barrier()
                if stop_after == "a":
                    break
                load_w_out(l)
                phase_s(l)
                P.barrier()
                if stop_after == "s":
                    break
                if l + 1 < n_layers:
                    load_w_in(WA, "WA", l + 1, 0, 1160)
                if debug and l == 0:
                    P.dma("sp", "dbg", dbg["yT"], yT[:], reads=[])
                phase_o(l, xsrc, xdst, l + 1 < n_layers)
                P.barrier()
            if stop_after in ("m", "a", "s") and debug:
                P.dma("sp", "dbg", dbg["yT"], yT[:], reads=[])

        schedule()
        P.wait_all_dma("sp")
        build.stats = (P.n_ops, P.n_waits, P.nsem, AR.hi, nc.sbuf_bytes_remaining)
    return nc


def make_in_maps(x, c, w_mod, b_mod, g_pre, g_post, w_in, m_conv_w, m_conv_b, m_wq, m_wk, m_b_i, m_b_f, m_norm_g, m_skip,
                 a_sinks, w_out):
    f = lambda a: np.ascontiguousarray(np.asarray(a, dtype=np.float32))
    x, c = f(x), f(c)
    shared = {
        "w_mod": f(w_mod), "b_mod": f(b_mod), "g_post": f(g_post), "w_in": f(w_in), "w_out": f(w_out),
        "gpre_cols": f(np.asarray(g_pre).reshape(NL, 8, 128).transpose(2, 0, 1)),
        "convw": f(np.asarray(m_conv_w).reshape(NL, 4, 4, 96).transpose(3, 0, 2, 1)),
        "mvec": f(np.stack([np.asarray(m_conv_b), np.asarray(m_norm_g), np.asarray(m_skip)], 1).reshape(NL, 3, 4, 96)
                  .transpose(3, 0, 1, 2)),
        "wq": f(np.asarray(m_wq).transpose(2, 0, 1, 3)),
        "wk": f(np.asarray(m_wk).transpose(2, 0, 1, 3)),
        "m_b_i": f(m_b_i), "m_b_f": f(m_b_f), "a_sinks": f(a_sinks),
    }
    maps = []
    for b in range(8):
        m = dict(shared)
        m["x"] = x[b]
        m["c_cols"] = f(c[b].reshape(8, 128).T)
        maps.append(m)
    return maps


def kernel(**inputs):
    nc = build()
    maps = make_in_maps(**inputs)
    res = run_bass_kernel_spmd(nc, maps, core_ids=list(range(8)))
    return np.stack([np.asarray(r["out"], dtype=np.float32) for r in res.results], axis=0)
```
